# Optimizing a Trainium2 kernel written in Bass

```python
import jax, jax.numpy as jnp
from jax import lax
import numpy as np

D_MODEL = 1024
BATCH = 4
SEQ = 8192
DEPTH = 2

N_MIXERS = 2
CONV_WIDTH = 31
HGRN_HEADS = 8
HGRN_DK = D_MODEL // HGRN_HEADS
HGRN_DV = D_MODEL // HGRN_HEADS
FORGET_DIM = HGRN_HEADS * HGRN_DK
CHUNK = 64
D_FF = 4 * D_MODEL
N_CONV_LAYERS = (DEPTH + 1) // 2
N_HGRN_LAYERS = DEPTH // 2
EPS = 1e-6

kernel_name = "hybrid_conformer_hgrn2_adaln_encoder"


def rms_norm(x, g):
    xf = x.astype(jnp.float32)
    y = xf * lax.rsqrt(jnp.mean(xf * xf, axis=-1, keepdims=True) + EPS)
    return (y * g.astype(jnp.float32)).astype(x.dtype)


def layer_norm(x, g, b):
    xf = x.astype(jnp.float32)
    mu = jnp.mean(xf, axis=-1, keepdims=True)
    xc = xf - mu
    y = xc * lax.rsqrt(jnp.mean(xc * xc, axis=-1, keepdims=True) + EPS)
    return (y * g.astype(jnp.float32) + b.astype(jnp.float32)).astype(x.dtype)


def modulate(h, shift, scale):
    return h * (1 + scale[:, None, :]) + shift[:, None, :]


def hgrn_lower_bounds(lb_logits):
    p = jax.nn.softmax(lb_logits.astype(jnp.float32), axis=0)
    return jnp.cumsum(p, axis=0) - p[0:1]


def conformer_conv(h, w_in, dw_w, dw_b, ln_g, ln_b, w_out):
    a, gate = jnp.split(h @ w_in, 2, axis=-1)
    u = a * jax.nn.sigmoid(gate)
    pad = CONV_WIDTH // 2
    y = lax.conv_general_dilated(
        u, dw_w[:, None, :], window_strides=(1,), padding=[(pad, pad)],
        dimension_numbers=("NWC", "WIO", "NWC"), feature_group_count=D_MODEL)
    y = y + dw_b
    y = jax.nn.silu(layer_norm(y, ln_g, ln_b))
    return y @ w_out


def _chunk_scan(q, logf, k, v):
    _, b_sz, g_sz, c_len, dk = q.shape
    dv = v.shape[-1]
    causal = jnp.tril(jnp.ones((c_len, c_len), dtype=bool))

    def step(state, xs):
        qc, lfc, kc, vc = xs
        bcum = jnp.cumsum(lfc, axis=2)
        o_inter = jnp.einsum("bgtk,bgkv->bgtv", qc * jnp.exp(bcum), state)
        diff = bcum[:, :, :, None, :] - bcum[:, :, None, :, :]
        decay = jnp.exp(jnp.where(causal[:, :, None], diff, -jnp.inf))
        scores = jnp.einsum("bgtk,bgtsk,bgsk->bgts", qc, decay, kc)
        o = o_inter + jnp.einsum("bgts,bgsv->bgtv", scores, vc)
        b_last = bcum[:, :, -1:, :]
        new_state = jnp.exp(b_last[:, :, 0, :])[..., None] * state + jnp.einsum(
            "bgsk,bgsv->bgkv", kc * jnp.exp(b_last - bcum), vc)
        return new_state, o

    state0 = jnp.zeros((b_sz, g_sz, dk, dv), jnp.float32)
    _, out = lax.scan(step, state0, (q, logf, k, v))
    return out


def hgrn2_mixer(h, w_in, lb, gn_g, w_out):
    b_sz, s_len, _ = h.shape
    n_c = s_len // CHUNK
    q, z_f, z_b, v, g = jnp.split(h @ w_in, 5, axis=-1)
    q = jax.nn.silu(q.astype(jnp.float32))

    def dirs(t_fwd, t_bwd):
        t = jnp.stack([t_fwd, t_bwd[:, ::-1]], axis=2).astype(jnp.float32)
        t = t.reshape(b_sz, n_c, CHUNK, 2 * HGRN_HEADS, -1)
        return jnp.transpose(t, (1, 0, 3, 2, 4))

    q2 = dirs(q, q)
    v2 = dirs(v, v)
    z2 = dirs(z_f, z_b)
    lb_h = jnp.tile(lb.astype(jnp.float32).reshape(HGRN_HEADS, HGRN_DK), (2, 1))[None, None, :, None, :]
    logf = jnp.logaddexp(jnp.log(lb_h), jnp.log1p(-lb_h) + jax.nn.log_sigmoid(z2))
    k2 = (1 - lb_h) * jax.nn.sigmoid(-z2)

    out = _chunk_scan(q2, logf, k2, v2)
    out = jnp.transpose(out, (1, 0, 3, 2, 4)).reshape(b_sz, s_len, 2, HGRN_HEADS, HGRN_DV)
    o = out[:, :, 0] + out[:, ::-1, 1]
    o = o * lax.rsqrt(jnp.mean(o * o, axis=-1, keepdims=True) + EPS)
    o = o * gn_g.astype(jnp.float32).reshape(HGRN_HEADS, HGRN_DV)
    o = o.reshape(b_sz, s_len, D_MODEL).astype(h.dtype) * jax.nn.silu(g)
    return o @ w_out


def sq_relu_mlp(h, w1, w2):
    a = jax.nn.relu(h @ w1)
    return (a * a) @ w2


def setup_inputs(seed: int = 0) -> dict:
    key = jax.random.key(seed)
    ks = jax.random.split(key, 20)
    f32 = jnp.float32
    D = D_MODEL

    def nrm(k, shape, scale):
        return jax.random.normal(k, shape, f32) * scale

    return {
        "x": nrm(ks[0], (BATCH, SEQ, D), 1.0),
        "c": nrm(ks[1], (BATCH, D), 1.0),
        "norm1_g": 1.0 + nrm(ks[2], (DEPTH, D), 0.05),
        "norm2_g": 1.0 + nrm(ks[3], (DEPTH, D), 0.05),
        "ada_w": nrm(ks[4], (DEPTH, D, 6 * D), 0.5 * D ** -0.5),
        "ada_b": nrm(ks[5], (DEPTH, 6 * D), 0.01),
        "mlp_w1": nrm(ks[6], (DEPTH, D, D_FF), D ** -0.5),
        "mlp_w2": nrm(ks[7], (DEPTH, D_FF, D), D_FF ** -0.5),
        "conv_w_in": nrm(ks[8], (N_CONV_LAYERS, D, 2 * D), D ** -0.5),
        "conv_dw_w": nrm(ks[9], (N_CONV_LAYERS, CONV_WIDTH, D), CONV_WIDTH ** -0.5),
        "conv_dw_b": nrm(ks[10], (N_CONV_LAYERS, D), 0.01),
        "conv_ln_g": 1.0 + nrm(ks[11], (N_CONV_LAYERS, D), 0.05),
        "conv_ln_b": nrm(ks[12], (N_CONV_LAYERS, D), 0.01),
        "conv_w_out": nrm(ks[13], (N_CONV_LAYERS, D, D), D ** -0.5),
        "hgrn_w_in": nrm(ks[14], (N_HGRN_LAYERS, D, 5 * D), D ** -0.5),
        "hgrn_lb_logits": nrm(ks[15], (DEPTH, FORGET_DIM), 0.5),
        "hgrn_gn_g": 1.0 + nrm(ks[16], (N_HGRN_LAYERS, D), 0.05),
        "hgrn_w_out": nrm(ks[17], (N_HGRN_LAYERS, D, D), D ** -0.5),
        "final_g": 1.0 + nrm(ks[18], (D,), 0.05),
    }


def reference(x, c, norm1_g, norm2_g, ada_w, ada_b, mlp_w1, mlp_w2, conv_w_in, conv_dw_w, conv_dw_b,
              conv_ln_g, conv_ln_b, conv_w_out, hgrn_w_in, hgrn_lb_logits, hgrn_gn_g, hgrn_w_out, final_g):
    cond = jax.nn.silu(c)
    lb_all = hgrn_lower_bounds(hgrn_lb_logits)
    for i in range(DEPTH):
        mod = (cond @ ada_w[i] + ada_b[i]).astype(x.dtype)
        sh1, sc1, g1, sh2, sc2, g2 = jnp.split(mod, 6, axis=-1)
        h = modulate(rms_norm(x, norm1_g[i]), sh1, sc1)
        j = i // N_MIXERS
        if i % N_MIXERS == 0:
            y = conformer_conv(h, conv_w_in[j], conv_dw_w[j], conv_dw_b[j], conv_ln_g[j], conv_ln_b[j],
                               conv_w_out[j])
        else:
            y = hgrn2_mixer(h, hgrn_w_in[j], lb_all[i], hgrn_gn_g[j], hgrn_w_out[j])
        x = x + g1[:, None, :] * y
        h = modulate(rms_norm(x, norm2_g[i]), sh2, sc2)
        x = x + g2[:, None, :] * sq_relu_mlp(h, mlp_w1[i], mlp_w2[i])
    return rms_norm(x, final_g)
```

```python
import contextlib
import numpy as np
import concourse.bass as bass
import concourse.mybir as mybir
from concourse.bass_utils import run_bass_kernel_spmd

F32 = mybir.dt.float32
BF16 = mybir.dt.bfloat16
F32R = mybir.dt.float32r
AF = mybir.ActivationFunctionType
ALU = mybir.AluOpType

ENGS = ["pe", "act", "dve", "pool", "sp"]
D = 1024
NT = 8
T = 512
TE = T + 30
LTOK = NT * T
XW = LTOK + 32
EPS = 1e-6
USE_PE_CONV = True

V_N1G, V_N2G, V_FG, V_DWB, V_LNG, V_LNB, V_LBL, V_GNG, V_FLAG, V_C, V_ADAB, V_DW = (
    0, 16, 32, 40, 48, 56, 64, 80, 88, 90, 98, 194)
NV = 194 + 248
CM_ID, CM_MASK, CM_SCAN, NCM = 0, 128, 384, 896

G_CIN, G_COUT, G_W1_0, G_W2_0, G_HIN, G_HOUT, G_W1_1, G_W2_1, NG = 0, 4, 6, 14, 22, 32, 34, 42, 50


class Res:
    __slots__ = ("name", "writer", "readers", "dma_sem", "dma_cnt")

    def __init__(self, name):
        self.name = name
        self.writer = None
        self.readers = []
        self.dma_sem = None
        self.dma_cnt = 0


class Sched:
    def __init__(self, nc, stack):
        self.nc = nc
        self.stack = stack
        self.ops = {e: [] for e in ENGS}
        self.eng_sem = {e: stack.enter_context(nc.semaphore("s_" + e)) for e in ENGS}
        self.n_res = 0
        self.dma_res = []
        self.bar = None

    def barrier(self, fn):
        deps = []
        for e in ENGS:
            for i in range(len(self.ops[e]) - 1, -1, -1):
                if self.ops[e][i]["dma"] is None:
                    deps.append(("eng", e, i))
                    break
        for r in self.dma_res:
            deps.append(("dma", r, r.dma_cnt))
        idx = len(self.ops["pool"])
        self.ops["pool"].append(dict(fn=fn, deps=deps, signal=False, dma=None))
        self.bar = ("eng", "pool", idx)

    def res(self, name=None):
        self.n_res += 1
        return Res(name or ("r%d" % self.n_res))

    def op(self, eng, fn, reads=(), writes=(), dma=False, nodep=False):
        deps = []
        for r in reads:
            if r.writer is not None:
                deps.append(r.writer)
        for r in writes:
            if nodep:
                continue
            if r.writer is not None:
                deps.append(r.writer)
            deps.extend(r.readers)
        if self.bar is not None:
            deps.append(self.bar)
        idx = len(self.ops[eng])
        rec = dict(fn=fn, deps=deps, signal=False, dma=None)
        if dma:
            r0 = writes[0]
            if r0.dma_sem is None:
                r0.dma_sem = self.stack.enter_context(self.nc.semaphore("d%d" % self.n_res + r0.name))
                self.n_res += 1
                self.dma_res.append(r0)
            r0.dma_cnt += 1
            rec["dma"] = (r0, r0.dma_cnt)
            me = ("dma", r0, r0.dma_cnt)
        else:
            me = ("eng", eng, idx)
        self.ops[eng].append(rec)
        for r in reads:
            r.readers.append(me)
        for r in writes:
            r.writer = me
            r.readers = []
        return me

    def emit(self, final_waits=()):
        nc = self.nc
        for e in ENGS:
            for i, rec in enumerate(self.ops[e]):
                for d in rec["deps"]:
                    if d[0] == "eng":
                        _, de, di = d
                        if de == "pe" and e == "pe":
                            continue
                        self.ops[de][di]["signal"] = True
        cnt = {}
        for e in ENGS:
            c = 0
            for i, rec in enumerate(self.ops[e]):
                if rec["signal"]:
                    c += 1
                    cnt[(e, i)] = c
        handles = {"pe": "tensor", "act": "scalar", "dve": "vector", "pool": "gpsimd", "sp": "sync"}
        with nc.Block() as block:
            for e in ENGS:
                def body(eng_h, e=e):
                    waited = {}
                    for i, rec in enumerate(self.ops[e]):
                        need = {}
                        for d in rec["deps"]:
                            if d[0] == "eng":
                                _, de, di = d
                                if de == "pe" and e == "pe":
                                    continue
                                key = ("e", de)
                                val = cnt[(de, di)]
                                sem = self.eng_sem[de]
                            else:
                                _, r, c = d
                                key = ("d", id(r))
                                val = 16 * c
                                sem = r.dma_sem
                            if need.get(key, (0, None))[0] < val:
                                need[key] = (val, sem)
                        for key, (val, sem) in need.items():
                            if waited.get(key, 0) >= val:
                                continue
                            waited[key] = val
                            eng_h.wait_ge(sem, val)
                        inst = rec["fn"](eng_h)
                        if rec["dma"] is not None:
                            inst.then_inc(rec["dma"][0].dma_sem, 16)
                        elif rec["signal"]:
                            inst.then_inc(self.eng_sem[e], 1)
                    if e == "sp":
                        for r in final_waits:
                            eng_h.wait_ge(r.dma_sem, 16 * r.dma_cnt)
                getattr(block, handles[e])(body)


def build_nc(debug=False):
    nc = bass.Bass("TRN2", target_bir_lowering=False)

    def din(name, shape):
        return nc.dram_tensor(name, shape, F32, kind="ExternalInput").ap()

    xT = din("xT", [128, 8, XW])
    vecs_d = din("vecs", [128, NV])
    cmat_d = din("cmat", [128, NCM])
    ada_w = din("ada_w", [2, D, 6 * D])
    w1_d = din("mlp_w1", [2, D, 4 * D])
    w2_d = din("mlp_w2", [2, 4 * D, D])
    cin_d = din("conv_w_in", [D, 2 * D])
    cout_d = din("conv_w_out", [D, D])
    hin_d = din("hgrn_w_in", [D, 5 * D])
    hout_d = din("hgrn_w_out", [D, D])
    outT = nc.dram_tensor("outT", [128, 8, LTOK], F32, kind="ExternalOutput").ap()

    wsc = nc.dram_tensor("wsc", [NG, 128, 4096], BF16).ap()
    x1s = nc.dram_tensor("x1s", [128, 8, LTOK], F32).ap()
    ofw = nc.dram_tensor("ofw", [128, 8, LTOK], F32).ap()
    qsc = nc.dram_tensor("qsc", [128, 8, LTOK], BF16).ap()
    vsc = nc.dram_tensor("vsc", [NT, 128, 4096], BF16).ap()
    gsc = nc.dram_tensor("gsc", [128, 8, LTOK], BF16).ap()
    zsc = nc.dram_tensor("zsc", [128, 8, LTOK], F32).ap()
    cc_in = nc.dram_tensor("cc_in", [128, 1024], F32).ap()
    cc_out = nc.dram_tensor("cc_out", [256, 1024], F32).ap()

    with contextlib.ExitStack() as st:
        S = Sched(nc, st)

        def sb(name, shape, dt):
            return st.enter_context(nc.sbuf_tensor("sb_" + name, shape, dt))

        NSLOT = 4
        slots = [sb("slot%d" % i, [128, 4096], BF16) for i in range(NSLOT)]
        r_slots = [S.res("slot%d" % i) for i in range(NSLOT)]
        r_slots_sw = [S.res("slotsw%d" % i) for i in range(NSLOT)]
        xt = [sb("xt%d" % i, [128, 8, TE], F32) for i in range(2)]
        r_xt = [S.res("xt%d" % i) for i in range(2)]
        h = sb("h", [128, 8, TE], BF16)
        r_h = [S.res("h%d" % c) for c in range(8)]
        sqb = [sb("sqb%d" % i, [128, TE], BF16) for i in range(2)]
        r_sqb = [S.res("sqb%d" % i) for i in range(2)]
        vA = sb("vA", [128, TE], F32); r_vA = S.res("vA")
        vB = sb("vB", [128, TE], F32); r_vB = S.res("vB")
        vC = sb("vC", [128, TE], F32); r_vC = S.res("vC")
        tmpn = [sb("tmpn%d" % i, [128, TE], F32) for i in range(2)]
        r_tmpn = [S.res("tmpn%d" % i) for i in range(2)]
        arena = sb("arena", [128, 16384], BF16)
        r_ar = [S.res("ar%d" % i) for i in range(32)]
        rt = [sb("rt%d" % i, [128, T], BF16) for i in range(2)]
        r_rt = [S.res("rt%d" % i) for i in range(2)]
        vecs = sb("vecs", [128, NV], F32); r_vecs = S.res("vecs")
        cmat = sb("cmat", [128, NCM], F32); r_cmat = S.res("cmat")
        ident = sb("ident", [128, 128], BF16)
        ones = sb("ones", [128, 128], BF16)
        r_const = S.res("const")
        mod = sb("mod", [128, 96], F32); r_mod = S.res("mod")
        cond = sb("cond", [128, 8], F32); r_cond = S.res("cond")
        cond_bf = sb("cond_bf", [128, 8], BF16)
        gv = sb("gv", [128, 2, 16], F32)
        lbv = sb("lbv", [128, 3, 8], F32)
        r_gv = S.res("gv")
        un = sb("un", [128, 17408], BF16)
        u = un[:, 0:4336].rearrange("p (c t) -> p c t", c=8); r_u = [S.res("u%d" % c) for c in range(8)]
        a_sb = un[:, 4336:6504].rearrange("p (c t) -> p c t", c=4); r_asb = [S.res("asb%d" % c) for c in range(4)]
        sgt = [un[:, 6504 + 1084 * i: 6504 + 1084 * (i + 1)].bitcast(F32) for i in range(2)]
        r_sgt = [S.res("sgt%d" % i) for i in range(2)]
        dg = [un[:, 8704 + 3968 * i: 8704 + 3968 * (i + 1)].rearrange("p (a b) -> p a b", a=31) for i in range(2)]
        r_dg = [S.res("dg%d" % i) for i in range(2)]
        qs = un[:, 0:4096].rearrange("p (c t) -> p c t", c=8); r_qs = [S.res("qs%d" % c) for c in range(8)]
        kT = un[:, 4096:8192].rearrange("p (c t) -> p c t", c=8); r_kT = [S.res("kT%d" % c) for c in range(8)]
        ktm = un[:, 8192:12288].rearrange("p (a b c) -> p a b c", a=4, b=8); r_ktm = [S.res("ktm%d" % c) for c in range(4)]
        vtm = un[:, 12288:16384].rearrange("p (a b) -> p a b", a=4); r_vtm = [S.res("vtm%d" % c) for c in range(4)]
        NGT, NSET = 4, 3
        gtb = [[sb("gt%d_%d" % (q, i), [128, T], F32) for i in range(NGT)] for q in range(NSET)]
        r_gtb = [[S.res("gt%d_%d" % (q, i)) for i in range(NGT)] for q in range(NSET)]
        Bcol = sb("Bcol", [128, 8, 8, 2], F32)
        Sst = sb("Sst", [128, 8, 128], F32); r_Sg = [S.res("S%d" % c) for c in range(8)]
        Sp = [sb("Sp%d" % i, [128, 8, 128], BF16) for i in range(2)]
        r_Sp = [[S.res("Sp%d_%d" % (i, g)) for g in range(2)] for i in range(2)]
        qh = sb("qh", [128, 8, T], BF16); r_qh = [S.res("qh%d" % c) for c in range(8)]
        kh = [sb("kh%d" % i, [128, T], BF16) for i in range(2)]; r_kh = [S.res("kh%d" % i) for i in range(2)]
        scT = [sb("scT%d" % i, [128, 8, 64], BF16) for i in range(2)]
        r_scT = [S.res("scT%d" % i) for i in range(2)]
        EA = sb("EA", [128, 8, 8], F32); EB = sb("EB", [128, 8, 8], F32); EC = sb("EC", [128, 8, 8], F32)
        r_E = [S.res("E%d" % c) for c in range(8)]
        ED = sb("ED", [128, 8, 8], F32)

        banks = [st.enter_context(nc.psum_tensor("pb%d" % i, [128, 512], F32)) for i in range(8)]
        r_banks = [S.res("pb%d" % i) for i in range(8)]
        state = dict(bank=0, slot=0)

        def nb():
            i = state["bank"]
            state["bank"] = (i + 1) % 8
            return banks[i], r_banks[i]

        _mats = [(G_CIN, 4), (G_COUT, 2), (G_W1_0, 8), (G_W2_0, 8), (G_HIN, 10), (G_HOUT, 2), (G_W1_1, 8), (G_W2_1, 8)]
        r_wsc = [None] * NG
        for (g0, n) in _mats:
            rm = S.res("wsc%d" % g0)
            for g in range(g0, g0 + n):
                r_wsc[g] = rm
        r_x1s = [S.res("x1s")] * NT
        r_ofw = [S.res("ofw")] * NT
        r_out = [S.res("out")] * NT
        r_ccin = S.res("ccin"); r_ccout = S.res("ccout")
        r_gsc = [S.res("gsc")] * NT
        r_zsc = [S.res("zsc")] * NT
        r_stgdma = [S.res("stgdma%d" % q) for q in range(3)]
        r_qsc = [S.res("qsc")] * NT
        r_vsc = [S.res("vsc")] * NT

        def wg(g):
            i = state["slot"]
            state["slot"] = (i + 1) % NSLOT
            S.op("sp", lambda e, i=i, g=g: e.dma_start(out=slots[i][:], in_=wsc[g]),
                 reads=[r_wsc[g]], writes=[r_slots[i]], dma=True)
            return slots[i], r_slots[i]

        def vcol(base, c, n=1):
            return vecs[:, base + c: base + c + n]

        S.op("sp", lambda e: e.dma_start(out=vecs[:], in_=vecs_d), writes=[r_vecs], dma=True)
        S.op("sp", lambda e: e.dma_start(out=cmat[:], in_=cmat_d), writes=[r_cmat], dma=True)
        S.op("dve", lambda e: e.tensor_copy(out=ident[:], in_=cmat[:, CM_ID:CM_ID + 128]), reads=[r_cmat], writes=[r_const])
        S.op("dve", lambda e: e.memset(ones[:], 1.0), writes=[r_const])
        masks = cmat[:, CM_MASK:CM_MASK + 256].rearrange("p (a b) -> p a b", a=4)
        scanmask = cmat[:, CM_SCAN:CM_SCAN + 512]

        def conv_k1024(g, src, col0):
            S.op("pool", lambda e: e.dma_start(
                out=wsc[g].rearrange("p (k n) -> p k n", k=8),
                in_=src[:, col0:col0 + 512].rearrange("(k p) n -> p k n", p=128)),
                writes=[r_wsc[g]], dma=True, nodep=True)

        def conv_w2(g, src, row0):
            S.op("pool", lambda e: e.dma_start(
                out=wsc[g].rearrange("p (k n) -> p k n", k=4),
                in_=src[row0:row0 + 512, :].rearrange("(k p) n -> p k n", p=128)),
                writes=[r_wsc[g]], dma=True, nodep=True)

        def convert_layer0():
            for j in range(4):
                conv_k1024(G_CIN + j, cin_d, 512 * j)
            for j in range(2):
                conv_k1024(G_COUT + j, cout_d, 512 * j)
            for j in range(8):
                conv_k1024(G_W1_0 + j, w1_d[0], 512 * j)
            for j in range(8):
                conv_w2(G_W2_0 + j, w2_d[0], 512 * j)

        l1_convs = ([lambda j=j: conv_k1024(G_HIN + j, hin_d, 512 * j) for j in range(10)]
                    + [lambda j=j: conv_k1024(G_HOUT + j, hout_d, 512 * j) for j in range(2)]
                    + [lambda j=j: conv_k1024(G_W1_1 + j, w1_d[1], 512 * j) for j in range(8)]
                    + [lambda j=j: conv_w2(G_W2_1 + j, w2_d[1], 512 * j) for j in range(8)])

        S.op("sp", lambda e: e.dma_start(out=xt[0][:], in_=xT[:, :, 1:1 + TE]), writes=[r_xt[0]], dma=True)

        S.op("act", lambda e: e.activation(out=cond_bf[:], in_=vcol(V_C, 0, 8), func=AF.Silu), reads=[r_vecs], writes=[r_cond])
        def ada_batch(l, k):
            mbank, r_mbank = nb()
            for g2 in range(2):
                grp = 2 * k + g2
                i = state["slot"]
                state["slot"] = (i + 1) % NSLOT
                sl3 = slots[i][:].rearrange("p (k n) -> p k n", k=8)
                r_sl = r_slots[i]
                S.op("pool", lambda e, grp=grp, sl3=sl3: e.dma_start(
                    out=sl3, in_=ada_w[l][:, grp * 512:(grp + 1) * 512].rearrange("(k p) n -> p k n", p=128)),
                    writes=[r_slots_sw[i], r_sl], dma=True)
                for j in range(4):
                    ch = g2 * 4 + j
                    for kc in range(8):
                        S.op("pe", lambda e, sl3=sl3, j=j, kc=kc, ch=ch: e.matmul(
                            mbank[:, ch:ch + 1], lhsT=sl3[:, kc, j * 128:(j + 1) * 128], rhs=cond_bf[:, kc:kc + 1],
                            start=(kc == 0), stop=(kc == 7)),
                            reads=[r_sl, r_cond], writes=[r_mbank])
            c0 = l * 48 + k * 8
            S.op("dve", lambda e: e.tensor_tensor(out=mod[:, c0:c0 + 8], in0=mbank[:, 0:8], in1=vcol(V_ADAB, c0, 8), op=ALU.add),
                 reads=[r_mbank, r_vecs], writes=[r_mod])
            if k == 1:
                S.op("dve", lambda e: e.scalar_tensor_tensor(
                    out=gv[:, l, 0:8], in0=mod[:, c0:c0 + 8], scalar=1.0, in1=vcol(V_N1G, l * 8, 8),
                    op0=ALU.add, op1=ALU.mult), reads=[r_mod, r_vecs], writes=[r_gv])
            if k == 4:
                S.op("dve", lambda e: e.scalar_tensor_tensor(
                    out=gv[:, l, 8:16], in0=mod[:, c0:c0 + 8], scalar=1.0, in1=vcol(V_N2G, l * 8, 8),
                    op0=ALU.add, op1=ALU.mult), reads=[r_mod, r_vecs], writes=[r_gv])

        for k in range(6):
            ada_batch(0, k)
        convert_layer0()
        S.op("dve", lambda e: e.tensor_tensor(out=lbv[:, 2, :], in0=vcol(V_LBL, 8, 8), in1=vcol(V_LBL, 0, 8), op=ALU.subtract),
             reads=[r_vecs], writes=[r_gv])
        S.op("act", lambda e: e.activation(out=lbv[:, 0, :], in_=lbv[:, 2, :], func=AF.Sigmoid), reads=[r_gv], writes=[r_gv])
        S.op("act", lambda e: e.activation(out=lbv[:, 1, :], in_=lbv[:, 0, :], func=AF.Ln, scale=-1.0, bias=1.0),
             reads=[r_gv], writes=[r_gv])

        def MOD(l, k, c):
            return mod[:, l * 48 + k * 8 + c: l * 48 + k * 8 + c + 1]

        def rms_norm(xsrc, r_x, W, gain, shift, out_fn, r_out_fn, out_final=False):
            segs = [(0, W)] if W <= 512 else [(0, W // 2), (W // 2, W)]
            pbs = [nb() for _ in segs]
            for c in range(8):
                k = c % 2
                xa = xsrc(c)
                S.op("act", lambda e, xa=xa, k=k: e.activation(out=sqb[k][:, 0:W], in_=xa, func=AF.Square),
                     reads=[r_x], writes=[r_sqb[k]])
                for (s0, s1), (pb, r_pb) in zip(segs, pbs):
                    S.op("pe", lambda e, c=c, k=k, s0=s0, s1=s1, pb=pb: e.matmul(
                        pb[:, 0:s1 - s0], lhsT=ones[:], rhs=sqb[k][:, s0:s1], start=(c == 0), stop=(c == 7)),
                        reads=[r_sqb[k], r_const], writes=[r_pb])
            for (s0, s1), (pb, r_pb) in zip(segs, pbs):
                S.op("act", lambda e, s0=s0, s1=s1, pb=pb: e.activation(
                    out=vA[:, s0:s1], in_=pb[:, 0:s1 - s0], func=AF.Ln, scale=1.0 / D, bias=EPS),
                    reads=[r_pb], writes=[r_vA])
            S.op("act", lambda e: e.activation(out=vB[:, 0:W], in_=vA[:, 0:W], func=AF.Exp, scale=-0.5),
                 reads=[r_vA], writes=[r_vB])
            for c in range(8):
                k = c % 2
                xa = xsrc(c); ga = gain(c); oa = out_fn(c); r_o = r_out_fn(c)
                if out_final:
                    S.op("dve", lambda e, xa=xa, ga=ga, oa=oa: e.scalar_tensor_tensor(
                        out=oa, in0=xa, scalar=ga, in1=vB[:, 0:W], op0=ALU.mult, op1=ALU.mult),
                        reads=[r_x, r_vB, r_gv, r_vecs], writes=[r_o])
                else:
                    sa = shift(c)
                    S.op("dve", lambda e, xa=xa, ga=ga, k=k: e.scalar_tensor_tensor(
                        out=tmpn[k][:, 0:W], in0=xa, scalar=ga, in1=vB[:, 0:W], op0=ALU.mult, op1=ALU.mult),
                        reads=[r_x, r_vB, r_gv], writes=[r_tmpn[k]])
                    S.op("act", lambda e, oa=oa, sa=sa, k=k: e.activation(
                        out=oa, in_=tmpn[k][:, 0:W], func=AF.Identity, scale=1.0, bias=sa),
                        reads=[r_tmpn[k], r_mod], writes=[r_o])

        def proj_fm(slot, r_slot, j, rhs_fn, r_rhs, W):
            segs = [(0, W)] if W <= 512 else [(0, W // 2), (W // 2, W)]
            outs = []
            for (s0, s1) in segs:
                pb, r_pb = nb()
                for kc in range(8):
                    S.op("pe", lambda e, kc=kc, s0=s0, s1=s1, pb=pb: e.matmul(
                        pb[:, 0:s1 - s0], lhsT=slot.rearrange("p (k n) -> p k n", k=8)[:, kc, j * 128:(j + 1) * 128],
                        rhs=rhs_fn(kc)[:, s0:s1], start=(kc == 0), stop=(kc == 7)),
                        reads=[r_slot] + r_rhs, writes=[r_pb])
                outs.append((pb, r_pb, s0, s1))
            return outs

        def mlp(l, xb, r_x, g_w1, g_w2, mid_hook=None, group_hook=None):
            xc = lambda c: xt[xb][:, c, 15:15 + T]
            rms_norm(xc, r_x, T, lambda c: gv[:, l, 8 + c:9 + c], lambda c: MOD(l, 3, c),
                     lambda c: h[:, c, 0:T], lambda c: r_h[c])
            aT = lambda fc: arena[:, fc * 512:(fc + 1) * 512]
            for g in range(8):
                slot, r_slot = wg(g_w1 + g)
                for j in range(4):
                    fc = 4 * g + j
                    (pb, r_pb, _, _), = proj_fm(slot[:], r_slot, j, lambda kc: h[:, kc, 0:T], r_h, T)
                    k = fc % 2
                    S.op("act", lambda e, pb=pb, k=k: e.activation(out=rt[k][:], in_=pb[:], func=AF.Relu),
                         reads=[r_pb], writes=[r_rt[k]])
                    S.op("dve" if group_hook is not None else "pool",
                         lambda e, fc=fc, k=k: e.tensor_tensor(out=aT(fc), in0=rt[k][:], in1=rt[k][:], op=ALU.mult),
                         reads=[r_rt[k]], writes=[r_ar[fc]])
                if group_hook is not None:
                    group_hook(g)
            if mid_hook is not None:
                mid_hook()
            for g in range(8):
                slot, r_slot = wg(g_w2 + g)
                s3 = slot[:].rearrange("p (k n) -> p k n", k=4)
                for oc in range(8):
                    for fcl in range(4):
                        fc = 4 * g + fcl
                        S.op("pe", lambda e, s3=s3, oc=oc, fcl=fcl, fc=fc, g=g: e.matmul(
                            banks[oc][:], lhsT=s3[:, fcl, oc * 128:(oc + 1) * 128], rhs=aT(fc),
                            start=(g == 0 and fcl == 0), stop=(g == 7 and fcl == 3)),
                            reads=[r_slot, r_ar[fc]], writes=[r_banks[oc]])
            for oc in range(8):
                S.op("dve", lambda e, oc=oc: e.scalar_tensor_tensor(
                    out=xc(oc), in0=banks[oc][:], scalar=MOD(l, 5, oc), in1=xc(oc), op0=ALU.mult, op1=ALU.add),
                    reads=[r_banks[oc], r_mod, r_x], writes=[r_x])

        def wout_residual(l, xb, r_x, g_wout):
            xc = lambda c: xt[xb][:, c, 15:15 + T]
            for half in range(2):
                slot, r_slot = wg(g_wout + half)
                for j in range(4):
                    oc = 4 * half + j
                    (pb, r_pb, _, _), = proj_fm(slot[:], r_slot, j, lambda kc: h[:, kc, 0:T], r_h, T)
                    S.op("dve", lambda e, pb=pb, oc=oc: e.scalar_tensor_tensor(
                        out=xc(oc), in0=pb[:], scalar=MOD(l, 2, oc), in1=xc(oc), op0=ALU.mult, op1=ALU.add),
                        reads=[r_pb, r_mod, r_x], writes=[r_x])

        def load_x0(i):
            b = i % 2
            S.op("sp", lambda e: e.dma_start(out=xt[b][:], in_=xT[:, :, 512 * i + 1: 512 * i + 1 + TE]),
                 writes=[r_xt[b]], dma=True)

        y32 = arena[:, 0:8192].bitcast(F32).rearrange("p (c t) -> p c t", c=8)
        ybf = arena[:, 8192:12288].rearrange("p (c t) -> p c t", c=8)
        ysq = arena[:, 12288:16384].rearrange("p (c t) -> p c t", c=8)
        r_y32 = lambda c: [r_ar[2 * c], r_ar[2 * c + 1]]

        def l0_mixer(i):
            b = i % 2
            r_x = r_xt[b]
            rms_norm(lambda c: xt[b][:, c, :], r_x, TE, lambda c: gv[:, 0, c:c + 1], lambda c: MOD(0, 0, c),
                     lambda c: h[:, c, :], lambda c: r_h[c])
            for half in range(2):
                slA, r_slA = wg(G_CIN + half)
                slG, r_slG = wg(G_CIN + 2 + half)
                for j in range(4):
                    for (pb, r_pb, s0, s1) in proj_fm(slA[:], r_slA, j, lambda kc: h[:, kc, :], r_h, TE):
                        S.op("act", lambda e, pb=pb, s0=s0, s1=s1, j=j: e.activation(
                            out=a_sb[:, j, s0:s1], in_=pb[:, 0:s1 - s0], func=AF.Copy), reads=[r_pb], writes=[r_asb[j]])
                for j in range(4):
                    c = 4 * half + j
                    k = c % 2
                    for (pb, r_pb, s0, s1) in proj_fm(slG[:], r_slG, j, lambda kc: h[:, kc, :], r_h, TE):
                        S.op("act", lambda e, pb=pb, s0=s0, s1=s1, k=k: e.activation(
                            out=sgt[k][:, s0:s1], in_=pb[:, 0:s1 - s0], func=AF.Sigmoid), reads=[r_pb], writes=[r_sgt[k]])
                    S.op("pool", lambda e, c=c, j=j, k=k: e.tensor_tensor(
                        out=u[:, c, :], in0=a_sb[:, j, :], in1=sgt[k], op=ALU.mult),
                        reads=[r_asb[j], r_sgt[k]], writes=[r_u[c]])
                    if i == 0:
                        S.op("pool", lambda e, c=c: e.memset(u[:, c, 0:15], 0.0), writes=[r_u[c]])
            if USE_PE_CONV:
                def build_diag(c):
                    par = c % 2
                    S.op("dve", lambda e: e.tensor_tensor(
                        out=dg[par], in0=ident[:].unsqueeze(1).broadcast_to([128, 31, 128]),
                        in1=vecs[:, V_DW + c * 31: V_DW + c * 31 + 31].unsqueeze(2).broadcast_to([128, 31, 128]), op=ALU.mult),
                        reads=[r_const, r_vecs], writes=[r_dg[par]])

                build_diag(0)
                build_diag(1)
                for c in range(8):
                    par = c % 2
                    pb, r_pb = nb()
                    for tap in range(31):
                        S.op("pe", lambda e, pb=pb, par=par, tap=tap, c=c: e.matmul(
                            pb[:], lhsT=dg[par][:, tap, :], rhs=u[:, c, tap:tap + T], start=(tap == 0), stop=(tap == 30)),
                            reads=[r_dg[par], r_u[c]], writes=[r_pb])
                    if c + 2 < 8:
                        build_diag(c + 2)
                    S.op("act", lambda e, pb=pb, c=c: e.activation(out=y32[:, c, :], in_=pb[:], func=AF.Identity, scale=1.0, bias=vcol(V_DWB, c)),
                         reads=[r_pb, r_vecs], writes=r_y32(c))
                    S.op("act", lambda e, pb=pb, c=c: e.activation(out=ybf[:, c, :], in_=pb[:], func=AF.Identity, scale=1.0, bias=vcol(V_DWB, c)),
                         reads=[r_pb, r_vecs], writes=[r_ar[16 + c]])
                    S.op("pool", lambda e, c=c: e.tensor_tensor(out=ysq[:, c, :], in0=y32[:, c, :], in1=y32[:, c, :], op=ALU.mult),
                         reads=r_y32(c), writes=[r_ar[24 + c]])
            else:
                for tap in range(31):
                    for c in range(8):
                        if tap == 0:
                            S.op("dve", lambda e, c=c: e.tensor_scalar(
                                out=y32[:, c, :], in0=u[:, c, 0:T], scalar1=vcol(V_DW, c * 31), scalar2=vcol(V_DWB, c),
                                op0=ALU.mult, op1=ALU.add), reads=[r_u[c], r_vecs], writes=r_y32(c))
                        else:
                            S.op("dve", lambda e, c=c, tap=tap: e.scalar_tensor_tensor(
                                out=y32[:, c, :], in0=u[:, c, tap:tap + T], scalar=vcol(V_DW, c * 31 + tap), in1=y32[:, c, :],
                                op0=ALU.mult, op1=ALU.add), reads=[r_u[c]] + r_y32(c), writes=r_y32(c))
                for c in range(8):
                    S.op("act", lambda e, c=c: e.activation(out=ybf[:, c, :], in_=y32[:, c, :], func=AF.Copy),
                         reads=r_y32(c), writes=[r_ar[16 + c]])
                    S.op("pool", lambda e, c=c: e.tensor_tensor(out=ysq[:, c, :], in0=y32[:, c, :], in1=y32[:, c, :], op=ALU.mult),
                         reads=r_y32(c), writes=[r_ar[24 + c]])
            pbm, r_pbm = nb()
            pbv, r_pbv = nb()
            for c in range(8):
                S.op("pe", lambda e, c=c, pbm=pbm: e.matmul(pbm[:], lhsT=ones[:], rhs=ybf[:, c, :], start=(c == 0), stop=(c == 7)),
                     reads=[r_ar[16 + c], r_const], writes=[r_pbm])
                S.op("pe", lambda e, c=c, pbv=pbv: e.matmul(pbv[:], lhsT=ones[:], rhs=ysq[:, c, :], start=(c == 0), stop=(c == 7)),
                     reads=[r_ar[24 + c], r_const], writes=[r_pbv])
            if 1 <= i <= 6:
                ada_batch(1, i - 1)
            S.op("act", lambda e, pbm=pbm: e.activation(out=vA[:, 0:T], in_=pbm[:], func=AF.Copy, scale=1.0 / D), reads=[r_pbm], writes=[r_vA])
            S.op("dve", lambda e: e.tensor_tensor(out=vC[:, 0:T], in0=vA[:, 0:T], in1=vA[:, 0:T], op=ALU.mult), reads=[r_vA], writes=[r_vC])
            S.op("dve", lambda e, pbv=pbv: e.scalar_tensor_tensor(out=vC[:, 0:T], in0=pbv[:], scalar=1.0 / D, in1=vC[:, 0:T],
                                                          op0=ALU.mult, op1=ALU.subtract), reads=[r_pbv, r_vC], writes=[r_vC])
            S.op("act", lambda e: e.activation(out=vC[:, 0:T], in_=vC[:, 0:T], func=AF.Ln, scale=1.0, bias=EPS), reads=[r_vC], writes=[r_vC])
            S.op("act", lambda e: e.activation(out=vB[:, 0:T], in_=vC[:, 0:T], func=AF.Exp, scale=-0.5), reads=[r_vC], writes=[r_vB])
            for c in range(8):
                k = c % 2
                S.op("dve", lambda e, c=c, k=k: e.tensor_tensor(out=tmpn[k][:, 0:T], in0=y32[:, c, :], in1=vA[:, 0:T], op=ALU.subtract),
                     reads=r_y32(c) + [r_vA], writes=[r_tmpn[k]])
                S.op("dve", lambda e, k=k: e.tensor_tensor(out=tmpn[k][:, 0:T], in0=tmpn[k][:, 0:T], in1=vB[:, 0:T], op=ALU.mult),
                     reads=[r_tmpn[k], r_vB], writes=[r_tmpn[k]])
                S.op("act", lambda e, c=c, k=k: e.activation(out=h[:, c, 0:T], in_=tmpn[k][:, 0:T], func=AF.Silu,
                                                              scale=vcol(V_LNG, c), bias=vcol(V_LNB, c)),
                     reads=[r_tmpn[k], r_vecs], writes=[r_h[c]])
            wout_residual(0, b, r_x, G_COUT)

        def l0_mlp(i):
            b = i % 2
            r_x = r_xt[b]
            hook = None
            if i >= 1:
                hook = lambda i=i: [f() for f in l1_convs[4 * (i - 1):4 * i]]
            mlp(0, b, r_x, G_W1_0, G_W2_0, mid_hook=hook)
            S.op("pool", lambda e, i=i, b=b: e.dma_start(out=x1s[:, :, T * i:T * (i + 1)], in_=xt[b][:, :, 15:15 + T]),
                 reads=[r_x], writes=[r_x1s[i]], dma=True, nodep=True)

        load_x0(1)
        l0_mixer(0)
        l0_mixer(1)
        l0_mlp(0)
        load_x0(2)
        l0_mlp(1)
        load_x0(3)
        for i in range(2, NT):
            l0_mixer(i)
            l0_mlp(i)
            if i + 2 < NT:
                load_x0(i + 2)

        S.barrier(lambda e: e.memset(ED[:], 0.0))
        S.op("dve", lambda e: e.memset(Sst[:], 0.0), writes=r_Sg)
        S.op("dve", lambda e: e.memset(Sp[0][:], 0.0), writes=r_Sp[0])
        state["cn"] = 0

        def load_x1(i, b):
            S.op("sp", lambda e: e.dma_start(out=xt[b][:, :, 15:15 + T], in_=x1s[:, :, T * i:T * (i + 1)]),
                 reads=[r_x1s[i]], writes=[r_xt[b]], dma=True)

        o_sb = arena[:, 0:8192].bitcast(F32).rearrange("p (c t) -> p c t", c=8)
        of_sb = arena[:, 8192:16384].bitcast(F32).rearrange("p (c t) -> p c t", c=8)
        r_osb = r_ar[0:16]
        r_ofsb = r_ar[16:32]

        def gate_pipeline(fwd, zslots, hk, ztile=None):
            tstate = {}

            zstate = {}

            def Z(hd):
                if ztile is not None:
                    g_e = gtb[hd % NSET][0]
                    r_e = r_gtb[hd % NSET][0]
                    S.op("sp", lambda e: e.dma_start(out=g_e[:], in_=zsc[:, hd, T * ztile:T * (ztile + 1)]),
                         reads=[r_zsc[ztile]], writes=[r_e], dma=True)
                    zstate[hd] = (g_e, r_e)
                    return
                slot, r_slot = zslots[hd // 4]
                (pz, r_pz, _, _), = proj_fm(slot[:], r_slot, hd % 4, hk, r_h, T)
                zstate[hd] = (pz, r_pz)

            def A1(hd):
                g_e, g_w, g_lf, g_B = gtb[hd % NSET]
                r_e, r_w, r_lf, r_B = r_gtb[hd % NSET]
                pz, r_pz = zstate[hd]
                lb_c = lbv[:, 0, hd:hd + 1]
                S.op("act", lambda e: e.activation(out=g_e[:], in_=pz[:], func=AF.Exp), reads=[r_pz, r_e], writes=[r_e])
                S.op("act", lambda e: e.activation(out=g_w[:], in_=g_e[:], func=AF.Ln, scale=1.0, bias=1.0), reads=[r_e], writes=[r_w])
                S.op("act", lambda e: e.activation(out=g_lf[:], in_=g_e[:], func=AF.Ln, scale=1.0, bias=lb_c), reads=[r_e, r_gv], writes=[r_lf])

            def D1(hd):
                g_e, g_w, g_lf, g_B = gtb[hd % NSET]
                r_e, r_w, r_lf, r_B = r_gtb[hd % NSET]
                B3 = g_B[:].rearrange("p (c t) -> p c t", t=64)
                S.op("pool", lambda e: e.tensor_tensor(out=g_lf[:], in0=g_lf[:], in1=g_w[:], op=ALU.subtract), reads=[r_lf, r_w], writes=[r_lf])
                S.op("dve", lambda e: e.tensor_tensor_scan(out=g_B[:], data0=scanmask, data1=g_lf[:], initial=0.0, op0=ALU.mult, op1=ALU.add),
                     reads=[r_cmat, r_lf], writes=[r_B])
                if fwd:
                    S.op("dve", lambda e: e.tensor_copy(out=Bcol[:, hd, :, :], in_=B3[:, :, 31::32]), reads=[r_B], writes=[r_E[hd]])
                    S.op("dve", lambda e: e.tensor_tensor(out=B3, in0=B3, in1=Bcol[:, hd, :, 0:1].broadcast_to([128, 8, 64]), op=ALU.subtract),
                         reads=[r_B, r_E[hd]], writes=[r_B])
                    S.op("dve", lambda e: e.tensor_tensor(out=g_w[:], in0=g_w[:], in1=g_B[:], op=ALU.add), reads=[r_w, r_B], writes=[r_w])
                else:
                    S.op("dve", lambda e: e.tensor_copy(out=Bcol[:, hd, :, 1:2], in_=B3[:, :, 63:64]), reads=[r_B], writes=[r_E[hd]])
                    S.op("dve", lambda e: e.tensor_tensor(out=g_B[:], in0=g_B[:], in1=g_lf[:], op=ALU.subtract), reads=[r_B, r_lf], writes=[r_B])
                    S.op("dve", lambda e: e.tensor_copy(out=Bcol[:, hd, :, 0:1], in_=B3[:, :, 32:33]), reads=[r_B], writes=[r_E[hd]])
                    S.op("dve", lambda e: e.tensor_tensor(out=B3, in0=B3, in1=Bcol[:, hd, :, 0:1].broadcast_to([128, 8, 64]), op=ALU.subtract),
                         reads=[r_B, r_E[hd]], writes=[r_B])
                    S.op("dve", lambda e: e.tensor_tensor(out=g_w[:], in0=g_w[:], in1=g_B[:], op=ALU.subtract), reads=[r_w, r_B], writes=[r_w])
                    S.op("dve", lambda e: e.tensor_tensor(out=ED[:, hd, :], in0=Bcol[:, hd, :, 1], in1=Bcol[:, hd, :, 0], op=ALU.subtract),
                         reads=[r_E[hd]], writes=[r_E[hd]])

            def A2(hd):
                g_e, g_w, g_lf, g_B = gtb[hd % NSET]
                r_e, r_w, r_lf, r_B = r_gtb[hd % NSET]
                l1m_c = lbv[:, 1, hd:hd + 1]
                B3 = g_B[:].rearrange("p (c t) -> p c t", t=64)
                S.op("act", lambda e: e.activation(out=g_e[:], in_=g_B[:], func=AF.Exp, scale=(1.0 if fwd else -1.0)), reads=[r_B], writes=[r_e])
                S.op("act", lambda e: e.activation(out=kT[:, hd, :], in_=g_w[:], func=AF.Exp, scale=-1.0, bias=l1m_c),
                     reads=[r_w, r_gv], writes=[r_kT[hd]])
                S.op("act", lambda e: e.activation(out=EB[:, hd, :], in_=Bcol[:, hd, :, 1], func=AF.Exp), reads=[r_E[hd]], writes=[r_E[hd]])
                if fwd:
                    S.op("act", lambda e: e.activation(out=EA[:, hd, :], in_=Bcol[:, hd, :, 0], func=AF.Exp), reads=[r_E[hd]], writes=[r_E[hd]])
                    S.op("act", lambda e: e.activation(out=EC[:, hd, :], in_=B3[:, :, 63], func=AF.Exp), reads=[r_B], writes=[r_E[hd]])
                else:
                    S.op("act", lambda e: e.activation(out=EC[:, hd, :], in_=Bcol[:, hd, :, 0], func=AF.Exp), reads=[r_E[hd]], writes=[r_E[hd]])
                    S.op("act", lambda e: e.activation(out=EA[:, hd, :], in_=ED[:, hd, :], func=AF.Exp), reads=[r_E[hd]], writes=[r_E[hd]])
                S.op("pool", lambda e: e.tensor_tensor(out=qs[:, hd, :], in0=qs[:, hd, :], in1=g_e[:], op=ALU.mult),
                     reads=[r_qs[hd], r_e], writes=[r_qs[hd]])
                S.op("pool", lambda e: e.tensor_tensor(
                    out=qh[:, hd, :].rearrange("p (c t) -> p c t", t=64), in0=qs[:, hd, :].rearrange("p (c t) -> p c t", t=64),
                    in1=EA[:, hd, :].unsqueeze(2).broadcast_to([128, 8, 64]), op=ALU.mult),
                    reads=[r_qs[hd], r_E[hd]], writes=[r_qh[hd]])
                kk = hd % 2
                S.op("dve", lambda e: e.tensor_tensor(
                    out=kh[kk][:].rearrange("p (c t) -> p c t", t=64), in0=kT[:, hd, :].rearrange("p (c t) -> p c t", t=64),
                    in1=EC[:, hd, :].unsqueeze(2).broadcast_to([128, 8, 64]), op=ALU.mult),
                    reads=[r_kT[hd], r_E[hd]], writes=[r_kh[kk]])
                pb, r_pb = nb()
                pbb = pb[:].bitcast(BF16)
                for blk in range(4):
                    S.op("pe", lambda e, blk=blk: e.transpose(
                        pbb[:, blk * 128:(blk + 1) * 128], kh[kk][:, blk * 128:(blk + 1) * 128], ident[:]),
                        reads=[r_kh[kk], r_const], writes=[r_pb])
                tstate[hd] = (pbb, r_pb)

            def A3(hd):
                pbb, r_pb = tstate[hd]
                if fwd:
                    S.op("dve", lambda e: e.tensor_copy(out=ktm[:, :, hd, :], in_=pbb[:, 0:512].rearrange("p (a b) -> p a b", a=4)),
                         reads=[r_pb], writes=r_ktm)
                else:
                    S.op("act", lambda e: e.activation(out=ktm[:, :, hd, :], in_=pbb[:, 0:512].rearrange("p (a b) -> p a b", a=4), func=AF.Copy),
                         reads=[r_pb], writes=r_ktm)

            def step(k):
                if k == -2:
                    Z(0); Z(1); Z(2)
                if ztile is None and 0 <= k + 3 < 8 and k + 3 >= 3:
                    Z(k + 3)
                if 0 <= k + 2 < 8:
                    A1(k + 2)
                if 0 <= k + 1 < 8:
                    D1(k + 1)
                if 0 <= k - 1 < 8:
                    A3(k - 1)
                if 0 <= k < 8:
                    A2(k)
                if ztile is not None and 0 <= k + 3 < 8 and k + 3 >= 3:
                    Z(k + 3)

            return step

        def load_bwd_operands(i):
            S.op("sp", lambda e: e.dma_start(out=qs, in_=qsc[:, :, T * i:T * (i + 1)]), reads=[r_qsc[i]], writes=r_qs, dma=True)
            S.op("sp", lambda e: e.dma_start(out=un[:, 12288:16384], in_=vsc[i]), reads=[r_vsc[i]], writes=r_vtm, dma=True)

        def scan_pass(direction):
            fwd = direction == 0
            order = list(range(NT)) if fwd else list(range(NT - 1, -1, -1))
            zg = G_HIN + 2 if fwd else G_HIN + 4
            load_x1(order[0], 0)
            for n, i in enumerate(order):
                b = n % 2
                if n + 1 < NT:
                    load_x1(order[n + 1], (n + 1) % 2)
                r_x = r_xt[b]
                xc = lambda c: xt[b][:, c, 15:15 + T]
                if not fwd:
                    S.op("pool", lambda e, i=i: e.dma_start(out=of_sb, in_=ofw[:, :, T * i:T * (i + 1)]),
                         reads=[r_ofw[i]], writes=r_ofsb, dma=True)
                    if n == 0:
                        load_bwd_operands(i)
                        st0 = gate_pipeline(False, None, None, ztile=i)
                        for k in range(-2, 9):
                            st0(k)
                hk = lambda kc: h[:, kc, 0:T]
                if fwd:
                    rms_norm(xc, r_x, T, lambda c: gv[:, 1, c:c + 1], lambda c: MOD(1, 0, c),
                             lambda c: h[:, c, 0:T], lambda c: r_h[c])
                    for half in range(2):
                        slot, r_slot = wg(G_HIN + half)
                        for j in range(4):
                            hd = 4 * half + j
                            (pb, r_pb, _, _), = proj_fm(slot[:], r_slot, j, hk, r_h, T)
                            S.op("act", lambda e, pb=pb, hd=hd: e.activation(out=qs[:, hd, :], in_=pb[:], func=AF.Silu),
                                 reads=[r_pb], writes=[r_qs[hd]])
                    for half in range(2):
                        slot, r_slot = wg(G_HIN + 6 + half)
                        s3 = slot[:].rearrange("p (k n) -> p k n", k=8)
                        for blk in range(4):
                            pb, r_pb = nb()
                            for kc in range(8):
                                S.op("pe", lambda e, s3=s3, blk=blk, kc=kc, pb=pb: e.matmul(
                                    pb[:], lhsT=h[:, kc, blk * 128:(blk + 1) * 128], rhs=s3[:, kc, :],
                                    start=(kc == 0), stop=(kc == 7)), reads=[r_slot] + r_h, writes=[r_pb])
                            S.op("act", lambda e, pb=pb, blk=blk, half=half: e.activation(
                                out=vtm[:, blk, half * 512:(half + 1) * 512], in_=pb[:], func=AF.Copy),
                                reads=[r_pb], writes=[r_vtm[blk]])
                    S.op("sp", lambda e, i=i: e.dma_start(out=qsc[:, :, T * i:T * (i + 1)], in_=qs), reads=r_qs, writes=[r_qsc[i]], dma=True, nodep=True)
                    S.op("sp", lambda e, i=i: e.dma_start(out=vsc[i], in_=un[:, 12288:16384]), reads=r_vtm, writes=[r_vsc[i]], dma=True, nodep=True)
                    for half in range(2):
                        slot, r_slot = wg(G_HIN + 8 + half)
                        for j in range(4):
                            hd = 4 * half + j
                            (pb, r_pb, _, _), = proj_fm(slot[:], r_slot, j, hk, r_h, T)
                            S.op("act", lambda e, pb=pb, hd=hd: e.activation(out=kT[:, hd, :], in_=pb[:], func=AF.Silu),
                                 reads=[r_pb], writes=[r_kT[hd]])
                    S.op("sp", lambda e, i=i: e.dma_start(out=gsc[:, :, T * i:T * (i + 1)], in_=kT), reads=r_kT, writes=[r_gsc[i]], dma=True, nodep=True)
                    for half in range(2):
                        slot, r_slot = wg(G_HIN + 4 + half)
                        for j in range(4):
                            hd = 4 * half + j
                            (pb, r_pb, _, _), = proj_fm(slot[:], r_slot, j, hk, r_h, T)
                            stg, r_stg = gtb[hd % NSET][0], r_gtb[hd % NSET][0]
                            S.op("dve", lambda e, pb=pb, stg=stg: e.tensor_copy(out=stg[:], in_=pb[:]), reads=[r_pb], writes=[r_stg])
                            S.op("sp", lambda e, i=i, hd=hd, stg=stg: e.dma_start(out=zsc[:, hd, T * i:T * (i + 1)], in_=stg[:]),
                                 reads=[r_stg], writes=[r_stgdma[hd % NSET], r_zsc[i]], dma=True, nodep=True)
                    zslots = [wg(zg + half) for half in range(2)]
                    stf = gate_pipeline(True, zslots, hk)
                    for k in range(-2, 9):
                        stf(k)
                corder = list(range(8)) if fwd else list(range(7, -1, -1))
                for cn, c in enumerate(corder):
                    blk, par = c // 2, c % 2
                    kk = cn % 2
                    gcn = state["cn"]; state["cn"] += 1
                    sp_cur, sp_nxt = gcn % 2, (gcn + 1) % 2
                    mk = masks[:, (0 if fwd else 2) + par, :]
                    psc, r_psc = nb()
                    for hd in range(8):
                        S.op("pe", lambda e, psc=psc, hd=hd, blk=blk, c=c: e.matmul(
                            psc[:, hd * 64:(hd + 1) * 64], lhsT=kT[:, hd, blk * 128:(blk + 1) * 128],
                            rhs=qs[:, hd, c * 64:(c + 1) * 64], start=True, stop=True),
                            reads=[r_kT[hd], r_qs[hd]], writes=[r_psc])
                    S.op("dve", lambda e, psc=psc, kk=kk, mk=mk: e.tensor_tensor(
                        out=scT[kk][:], in0=psc[:].rearrange("p (a b) -> p a b", a=8),
                        in1=mk.unsqueeze(1).broadcast_to([128, 8, 64]), op=ALU.mult),
                        reads=[r_psc, r_cmat], writes=[r_scT[kk]])
                    pds = [nb(), nb()]
                    for hd in range(8):
                        pd, r_pd = pds[hd // 4]
                        p0 = 64 * par
                        S.op("pe", lambda e, pd=pd, hd=hd, blk=blk, p0=p0: e.matmul(
                            pd[:, (hd % 4) * 128:(hd % 4 + 1) * 128], lhsT=ktm[p0:p0 + 64, blk, hd, :],
                            rhs=vtm[p0:p0 + 64, blk, hd * 128:(hd + 1) * 128], start=True, stop=True),
                            reads=r_ktm + [r_vtm[blk]], writes=[r_pd])
                    po, r_po = nb()
                    for hd in range(8):
                        S.op("pe", lambda e, po=po, hd=hd, c=c, sp_cur=sp_cur: e.matmul(
                            po[:, hd * 64:(hd + 1) * 64], lhsT=Sp[sp_cur][:, hd, :], rhs=qh[:, hd, c * 64:(c + 1) * 64],
                            start=True, stop=False), reads=[r_Sp[sp_cur][hd // 4], r_qh[hd]], writes=[r_po])
                        S.op("pe", lambda e, po=po, hd=hd, blk=blk, kk=kk: e.matmul(
                            po[:, hd * 64:(hd + 1) * 64], lhsT=vtm[:, blk, hd * 128:(hd + 1) * 128], rhs=scT[kk][:, hd, :],
                            start=False, stop=True), reads=[r_vtm[blk], r_scT[kk]], writes=[r_po])
                    for hh in range(2):
                        pd, r_pd = pds[hh]
                        for j in range(4):
                            hd = 4 * hh + j
                            S.op("dve", lambda e, pd=pd, hd=hd, j=j, c=c: e.scalar_tensor_tensor(
                                out=Sst[:, hd, :], in0=Sst[:, hd, :], scalar=EB[:, hd, c:c + 1], in1=pd[:, j * 128:(j + 1) * 128],
                                op0=ALU.mult, op1=ALU.add), reads=[r_pd, r_Sg[hd], r_E[hd]], writes=[r_Sg[hd]])
                        S.op("act", lambda e, hh=hh, sp_nxt=sp_nxt: e.activation(
                            out=Sp[sp_nxt][:, 4 * hh:4 * hh + 4, :], in_=Sst[:, 4 * hh:4 * hh + 4, :], func=AF.Copy),
                            reads=r_Sg[4 * hh:4 * hh + 4], writes=[r_Sp[sp_nxt][hh]])
                    po3 = po[:].rearrange("p (a b) -> p a b", a=8)
                    if fwd:
                        S.op("act", lambda e, po3=po3, c=c: e.activation(out=o_sb[:, :, c * 64:(c + 1) * 64], in_=po3, func=AF.Copy),
                             reads=[r_po], writes=r_osb)
                    else:
                        S.op("dve", lambda e, po3=po3, c=c: e.tensor_tensor(
                            out=o_sb[:, :, c * 64:(c + 1) * 64], in0=po3, in1=of_sb[:, :, c * 64:(c + 1) * 64], op=ALU.add),
                            reads=[r_po] + r_ofsb, writes=r_osb)
                if fwd:
                    S.op("pool", lambda e, i=i: e.dma_start(out=ofw[:, :, T * i:T * (i + 1)], in_=o_sb),
                         reads=r_osb, writes=[r_ofw[i]], dma=True, nodep=True)
                    continue
                sg = kT
                S.op("sp", lambda e, i=i: e.dma_start(out=kT, in_=gsc[:, :, T * i:T * (i + 1)]), reads=[r_gsc[i]], writes=r_kT, dma=True)
                for hd in range(8):
                    k = hd % 2
                    S.op("act", lambda e, hd=hd, k=k: e.activation(out=sqb[k][:, 0:T], in_=o_sb[:, hd, :], func=AF.Square),
                         reads=r_osb, writes=[r_sqb[k]])
                    pb, r_pb = nb()
                    S.op("pe", lambda e, pb=pb, k=k: e.matmul(pb[:], lhsT=ones[:], rhs=sqb[k][:, 0:T], start=True, stop=True),
                         reads=[r_sqb[k], r_const], writes=[r_pb])
                    S.op("act", lambda e, pb=pb, k=k: e.activation(out=tmpn[k][:, 0:T], in_=pb[:], func=AF.Ln, scale=1.0 / 128, bias=EPS),
                         reads=[r_pb], writes=[r_tmpn[k]])
                    S.op("act", lambda e, k=k: e.activation(out=tmpn[k][:, 0:T], in_=tmpn[k][:, 0:T], func=AF.Exp, scale=-0.5),
                         reads=[r_tmpn[k]], writes=[r_tmpn[k]])
                    S.op("dve", lambda e, hd=hd, k=k: e.tensor_tensor(out=tmpn[k][:, 0:T], in0=o_sb[:, hd, :], in1=tmpn[k][:, 0:T], op=ALU.mult),
                         reads=r_osb + [r_tmpn[k]], writes=[r_tmpn[k]])
                    S.op("dve", lambda e, hd=hd, k=k: e.scalar_tensor_tensor(
                        out=h[:, hd, 0:T], in0=tmpn[k][:, 0:T], scalar=vcol(V_GNG, hd), in1=sg[:, hd, :], op0=ALU.mult, op1=ALU.mult),
                        reads=[r_tmpn[k], r_vecs, r_kT[hd]] + r_h, writes=[r_h[hd]])
                wout_residual(1, b, r_x, G_HOUT)
                ghook = mhook = None
                if n + 1 < NT:
                    inext = order[n + 1]
                    load_bwd_operands(inext)
                    stn = gate_pipeline(False, None, None, ztile=inext)

                    def ghook(g, stn=stn):
                        if g == 0:
                            stn(-2); stn(-1); stn(0)
                        else:
                            stn(g)

                    def mhook(stn=stn):
                        stn(8)
                mlp(1, b, r_x, G_W1_1, G_W2_1, mid_hook=mhook, group_hook=ghook)
                outb = arena[:, 0:8192].bitcast(F32).rearrange("p (c t) -> p c t", c=8)
                rms_norm(xc, r_x, T, lambda c: vcol(V_FG, c), None,
                         lambda c: outb[:, c, :], lambda c: r_ar[2 * c], out_final=True)
                S.op("pool", lambda e, i=i: e.dma_start(out=outT[:, :, T * i:T * (i + 1)], in_=outb),
                     reads=r_ar[0:16], writes=[r_out[i]], dma=True, nodep=True)

        scan_pass(0)
        S.barrier(lambda e: e.memset(ED[:], 0.0))
        S.op("pool", lambda e: e.dma_start(out=cc_in, in_=Sst[:].rearrange("p a b -> p (a b)")), reads=r_Sg, writes=[r_ccin], dma=True)
        S.op("pool", lambda e: e.collective_compute("AllGather", ALU.bypass, replica_groups=[[0, 1], [2, 3], [4, 5], [6, 7]],
                                                     ins=[cc_in], outs=[cc_out]), reads=[r_ccin], writes=[r_ccout])
        Sx = arena[:, 0:4096].bitcast(F32).rearrange("p (r f) -> p r f", r=2)
        r_Sx = r_ar[0:8]
        S.op("pool", lambda e: e.dma_start(out=Sx, in_=cc_out.rearrange("(r p) f -> p r f", p=128)),
             reads=[r_ccout], writes=r_Sx, dma=True)
        Sflat = Sst[:].rearrange("p a b -> p (a b)")
        S.op("dve", lambda e: e.tensor_scalar(out=Sflat, in0=Sx[:, 0, :], scalar1=vcol(V_FLAG, 0), scalar2=None, op0=ALU.mult),
             reads=r_Sx + [r_vecs], writes=r_Sg)
        S.op("dve", lambda e: e.scalar_tensor_tensor(out=Sflat, in0=Sx[:, 1, :], scalar=vcol(V_FLAG, 1), in1=Sflat,
                                                      op0=ALU.mult, op1=ALU.add), reads=r_Sx + [r_vecs] + r_Sg, writes=r_Sg)
        nxt = state["cn"] % 2
        S.op("act", lambda e: e.activation(out=Sp[nxt][:], in_=Sst[:], func=AF.Copy), reads=r_Sg, writes=r_Sp[nxt])
        scan_pass(1)
        S.emit(final_waits=[r_out[0]])
    return nc


_NC = None


def _fm(v):
    return np.ascontiguousarray(np.asarray(v, np.float32).reshape(8, 128).T)


def kernel(x, c, norm1_g, norm2_g, ada_w, ada_b, mlp_w1, mlp_w2, conv_w_in, conv_dw_w, conv_dw_b,
           conv_ln_g, conv_ln_b, conv_w_out, hgrn_w_in, hgrn_lb_logits, hgrn_gn_g, hgrn_w_out, final_g):
    global _NC
    f = lambda a: np.ascontiguousarray(np.asarray(a, np.float32))
    x = f(x); c = f(c)
    ada_w = f(ada_w); mlp_w1 = f(mlp_w1); mlp_w2 = f(mlp_w2)
    cin = f(conv_w_in)[0]; cout = f(conv_w_out)[0]; hout = f(hgrn_w_out)[0]
    hin = f(hgrn_w_in)[0]
    hin_sw = np.ascontiguousarray(np.concatenate(
        [hin[:, 0:1024], hin[:, 2048:3072], hin[:, 1024:2048], hin[:, 3072:]], axis=1))
    dw = f(conv_dw_w)[0]
    cm = np.zeros((128, NCM), np.float32)
    cm[:, CM_ID:CM_ID + 128] = np.eye(128, dtype=np.float32)
    s_ = np.arange(64)[:, None]; t_ = np.arange(64)[None, :]
    fe = np.zeros((128, 64), np.float32); fe[:64] = (s_ <= t_)
    fo = np.zeros((128, 64), np.float32); fo[64:] = (s_ <= t_)
    be = np.zeros((128, 64), np.float32); be[:64] = (s_ >= t_)
    bo = np.zeros((128, 64), np.float32); bo[64:] = (s_ >= t_)
    cm[:, CM_MASK:CM_MASK + 256] = np.concatenate([fe, fo, be, bo], axis=1)
    sm = np.ones(512, np.float32); sm[::64] = 0.0
    cm[:, CM_SCAN:CM_SCAN + 512] = sm[None, :]
    in_maps = []
    for r in range(8):
        b, half = r // 2, r % 2
        if half == 0:
            xs = x[b, 0:LTOK + 16]
        else:
            xs = x[b, ::-1][0:LTOK + 16]
        xTr = np.zeros((128, 8, XW), np.float32)
        xTr[:, :, 16:16 + LTOK + 16] = xs.T.reshape(8, 128, LTOK + 16).transpose(1, 0, 2)
        vv = np.zeros((128, NV), np.float32)
        for l in range(2):
            vv[:, V_N1G + 8 * l:V_N1G + 8 * l + 8] = _fm(norm1_g[l])
            vv[:, V_N2G + 8 * l:V_N2G + 8 * l + 8] = _fm(norm2_g[l])
            vv[:, V_LBL + 8 * l:V_LBL + 8 * l + 8] = _fm(hgrn_lb_logits[l])
            vv[:, V_ADAB + 48 * l:V_ADAB + 48 * l + 48] = np.asarray(ada_b[l], np.float32).reshape(48, 128).T
        vv[:, V_FG:V_FG + 8] = _fm(final_g)
        vv[:, V_DWB:V_DWB + 8] = _fm(conv_dw_b[0])
        vv[:, V_LNG:V_LNG + 8] = _fm(conv_ln_g[0])
        vv[:, V_LNB:V_LNB + 8] = _fm(conv_ln_b[0])
        vv[:, V_GNG:V_GNG + 8] = _fm(hgrn_gn_g[0])
        vv[:, V_FLAG:V_FLAG + 2] = np.array([0.0, 1.0] if half == 0 else [1.0, 0.0], np.float32)[None, :]
        vv[:, V_C:V_C + 8] = _fm(c[b])
        dwr = dw if half == 0 else dw[::-1]
        vv[:, V_DW:V_DW + 248] = dwr.T.reshape(8, 128, 31).transpose(1, 0, 2).reshape(128, 248)
        in_maps.append({
            "xT": xTr, "vecs": vv, "cmat": cm, "ada_w": ada_w, "mlp_w1": mlp_w1, "mlp_w2": mlp_w2,
            "conv_w_in": cin, "conv_w_out": cout, "hgrn_w_in": hin if half == 0 else hin_sw, "hgrn_w_out": hout,
        })
    if _NC is None:
        _NC = build_nc()
    res = run_bass_kernel_spmd(_NC, in_maps, core_ids=list(range(8)))
    out = np.empty((4, 8192, 1024), np.float32)
    for r in range(8):
        b, half = r // 2, r % 2
        o = res.results[r]["outT"]
        tok = o.transpose(2, 1, 0).reshape(LTOK, 1024)
        if half == 0:
            out[b, 0:LTOK] = tok
        else:
            out[b, LTOK:] = tok[::-1]
    return out
```

```python
import contextlib
import numpy as np
import concourse.bass as bass
import concourse.mybir as mybir
from concourse.bass_utils import run_bass_kernel_spmd

F32 = mybir.dt.float32
BF16 = mybir.dt.bfloat16
F32R = mybir.dt.float32r
AF = mybir.ActivationFunctionType
ALU = mybir.AluOpType

ENGS = ["pe", "act", "dve", "pool", "sp"]
D = 1024
NT = 8
T = 512
TE = T + 30
LTOK = NT * T
XW = LTOK + 32
EPS = 1e-6
USE_PE_CONV = True

V_N1G, V_N2G, V_FG, V_DWB, V_LNG, V_LNB, V_LBL, V_GNG, V_FLAG, V_C, V_ADAB, V_DW = (
    0, 16, 32, 40, 48, 56, 64, 80, 88, 90, 98, 194)
NV = 194 + 248
CM_ID, CM_MASK, CM_SCAN, NCM = 0, 128, 384, 896

G_CIN, G_COUT, G_W1_0, G_W2_0, G_HIN, G_HOUT, G_W1_1, G_W2_1, NG = 0, 4, 6, 14, 22, 32, 34, 42, 50


class Res:
    __slots__ = ("name", "writer", "readers", "dma_sem", "dma_cnt")

    def __init__(self, name):
        self.name = name
        self.writer = None
        self.readers = []
        self.dma_sem = None
        self.dma_cnt = 0


class Sched:
    def __init__(self, nc, stack):
        self.nc = nc
        self.stack = stack
        self.ops = {e: [] for e in ENGS}
        self.eng_sem = {e: stack.enter_context(nc.semaphore("s_" + e)) for e in ENGS}
        self.n_res = 0
        self.dma_res = []
        self.bar = None

    def barrier(self, fn):
        deps = []
        for e in ENGS:
            for i in range(len(self.ops[e]) - 1, -1, -1):
                if self.ops[e][i]["dma"] is None:
                    deps.append(("eng", e, i))
                    break
        for r in self.dma_res:
            deps.append(("dma", r, r.dma_cnt))
        idx = len(self.ops["pool"])
        self.ops["pool"].append(dict(fn=fn, deps=deps, signal=False, dma=None))
        self.bar = ("eng", "pool", idx)

    def res(self, name=None):
        self.n_res += 1
        return Res(name or ("r%d" % self.n_res))

    def op(self, eng, fn, reads=(), writes=(), dma=False, nodep=False):
        deps = []
        for r in reads:
            if r.writer is not None:
                deps.append(r.writer)
        for r in writes:
            if nodep:
                continue
            if r.writer is not None:
                deps.append(r.writer)
            deps.extend(r.readers)
        if self.bar is not None:
            deps.append(self.bar)
        idx = len(self.ops[eng])
        rec = dict(fn=fn, deps=deps, signal=False, dma=None)
        if dma:
            r0 = writes[0]
            if r0.dma_sem is None:
                r0.dma_sem = self.stack.enter_context(self.nc.semaphore("d%d" % self.n_res + r0.name))
                self.n_res += 1
                self.dma_res.append(r0)
            r0.dma_cnt += 1
            rec["dma"] = (r0, r0.dma_cnt)
            me = ("dma", r0, r0.dma_cnt)
        else:
            me = ("eng", eng, idx)
        self.ops[eng].append(rec)
        for r in reads:
            r.readers.append(me)
        for r in writes:
            r.writer = me
            r.readers = []
        return me

    def emit(self, final_waits=()):
        nc = self.nc
        for e in ENGS:
            for i, rec in enumerate(self.ops[e]):
                for d in rec["deps"]:
                    if d[0] == "eng":
                        _, de, di = d
                        if de == "pe" and e == "pe":
                            continue
                        self.ops[de][di]["signal"] = True
        cnt = {}
        for e in ENGS:
            c = 0
            for i, rec in enumerate(self.ops[e]):
                if rec["signal"]:
                    c += 1
                    cnt[(e, i)] = c
        handles = {"pe": "tensor", "act": "scalar", "dve": "vector", "pool": "gpsimd", "sp": "sync"}
        with nc.Block() as block:
            for e in ENGS:
                def body(eng_h, e=e):
                    waited = {}
                    for i, rec in enumerate(self.ops[e]):
                        need = {}
                        for d in rec["deps"]:
                            if d[0] == "eng":
                                _, de, di = d
                                if de == "pe" and e == "pe":
                                    continue
                                key = ("e", de)
                                val = cnt[(de, di)]
                                sem = self.eng_sem[de]
                            else:
                                _, r, c = d
                                key = ("d", id(r))
                                val = 16 * c
                                sem = r.dma_sem
                            if need.get(key, (0, None))[0] < val:
                                need[key] = (val, sem)
                        for key, (val, sem) in need.items():
                            if waited.get(key, 0) >= val:
                                continue
                            waited[key] = val
                            eng_h.wait_ge(sem, val)
                        inst = rec["fn"](eng_h)
                        if rec["dma"] is not None:
                            inst.then_inc(rec["dma"][0].dma_sem, 16)
                        elif rec["signal"]:
                            inst.then_inc(self.eng_sem[e], 1)
                    if e == "sp":
                        for r in final_waits:
                            eng_h.wait_ge(r.dma_sem, 16 * r.dma_cnt)
                getattr(block, handles[e])(body)


def build_nc(debug=False):
    nc = bass.Bass("TRN2", target_bir_lowering=False)

    def din(name, shape):
        return nc.dram_tensor(name, shape, F32, kind="ExternalInput").ap()

    xT = din("xT", [128, 8, XW])
    vecs_d = din("vecs", [128, NV])
    cmat_d = din("cmat", [128, NCM])
    ada_w = din("ada_w", [2, D, 6 * D])
    w1_d = din("mlp_w1", [2, D, 4 * D])
    w2_d = din("mlp_w2", [2, 4 * D, D])
    cin_d = din("conv_w_in", [D, 2 * D])
    cout_d = din("conv_w_out", [D, D])
    hin_d = din("hgrn_w_in", [D, 5 * D])
    hout_d = din("hgrn_w_out", [D, D])
    outT = nc.dram_tensor("outT", [128, 8, LTOK], F32, kind="ExternalOutput").ap()

    wsc = nc.dram_tensor("wsc", [NG, 128, 4096], BF16).ap()
    x1s = nc.dram_tensor("x1s", [128, 8, LTOK], F32).ap()
    ofw = nc.dram_tensor("ofw", [128, 8, LTOK], F32).ap()
    qsc = nc.dram_tensor("qsc", [128, 8, LTOK], BF16).ap()
    vsc = nc.dram_tensor("vsc", [NT, 128, 4096], BF16).ap()
    gsc = nc.dram_tensor("gsc", [128, 8, LTOK], BF16).ap()
    zsc = nc.dram_tensor("zsc", [128, 8, LTOK], F32).ap()
    cc_in = nc.dram_tensor("cc_in", [128, 1024], F32).ap()
    cc_out = nc.dram_tensor("cc_out", [256, 1024], F32).ap()

    with contextlib.ExitStack() as st:
        S = Sched(nc, st)

        def sb(name, shape, dt):
            return st.enter_context(nc.sbuf_tensor("sb_" + name, shape, dt))

        NSLOT = 4
        slots = [sb("slot%d" % i, [128, 4096], BF16) for i in range(NSLOT)]
        r_slots = [S.res("slot%d" % i) for i in range(NSLOT)]
        r_slots_sw = [S.res("slotsw%d" % i) for i in range(NSLOT)]
        xt = [sb("xt%d" % i, [128, 8, TE], F32) for i in range(2)]
        r_xt = [S.res("xt%d" % i) for i in range(2)]
        h = sb("h", [128, 8, TE], BF16)
        r_h = [S.res("h%d" % c) for c in range(8)]
        sqb = [sb("sqb%d" % i, [128, TE], BF16) for i in range(2)]
        r_sqb = [S.res("sqb%d" % i) for i in range(2)]
        vA = sb("vA", [128, TE], F32); r_vA = S.res("vA")
        vB = sb("vB", [128, TE], F32); r_vB = S.res("vB")
        vC = sb("vC", [128, TE], F32); r_vC = S.res("vC")
        tmpn = [sb("tmpn%d" % i, [128, TE], F32) for i in range(2)]
        r_tmpn = [S.res("tmpn%d" % i) for i in range(2)]
        arena = sb("arena", [128, 16384], BF16)
        r_ar = [S.res("ar%d" % i) for i in range(32)]
        rt = [sb("rt%d" % i, [128, T], BF16) for i in range(2)]
        r_rt = [S.res("rt%d" % i) for i in range(2)]
        vecs = sb("vecs", [128, NV], F32); r_vecs = S.res("vecs")
        cmat = sb("cmat", [128, NCM], F32); r_cmat = S.res("cmat")
        ident = sb("ident", [128, 128], BF16)
        ones = sb("ones", [128, 128], BF16)
        r_const = S.res("const")
        mod = sb("mod", [128, 96], F32); r_mod = S.res("mod")
        cond = sb("cond", [128, 8], F32); r_cond = S.res("cond")
        cond_bf = sb("cond_bf", [128, 8], BF16)
        gv = sb("gv", [128, 2, 16], F32)
        lbv = sb("lbv", [128, 3, 8], F32)
        r_gv = S.res("gv")
        un = sb("un", [128, 17408], BF16)
        u = un[:, 0:4336].rearrange("p (c t) -> p c t", c=8); r_u = [S.res("u%d" % c) for c in range(8)]
        a_sb = un[:, 4336:6504].rearrange("p (c t) -> p c t", c=4); r_asb = [S.res("asb%d" % c) for c in range(4)]
        sgt = [un[:, 6504 + 1084 * i: 6504 + 1084 * (i + 1)].bitcast(F32) for i in range(2)]
        r_sgt = [S.res("sgt%d" % i) for i in range(2)]
        dg = [un[:, 8704 + 3968 * i: 8704 + 3968 * (i + 1)].rearrange("p (a b) -> p a b", a=31) for i in range(2)]
        r_dg = [S.res("dg%d" % i) for i in range(2)]
        qs = un[:, 0:4096].rearrange("p (c t) -> p c t", c=8); r_qs = [S.res("qs%d" % c) for c in range(8)]
        kT = un[:, 4096:8192].rearrange("p (c t) -> p c t", c=8); r_kT = [S.res("kT%d" % c) for c in range(8)]
        ktm = un[:, 8192:12288].rearrange("p (a b c) -> p a b c", a=4, b=8); r_ktm = [S.res("ktm%d" % c) for c in range(4)]
        vtm = un[:, 12288:16384].rearrange("p (a b) -> p a b", a=4); r_vtm = [S.res("vtm%d" % c) for c in range(4)]
        NGT, NSET = 4, 3
        gtb = [[sb("gt%d_%d" % (q, i), [128, T], F32) for i in range(NGT)] for q in range(NSET)]
        r_gtb = [[S.res("gt%d_%d" % (q, i)) for i in range(NGT)] for q in range(NSET)]
        Bcol = sb("Bcol", [128, 8, 8, 2], F32)
        Sst = sb("Sst", [128, 8, 128], F32); r_Sg = [S.res("S%d" % c) for c in range(8)]
        Sp = [sb("Sp%d" % i, [128, 8, 128], BF16) for i in range(2)]
        r_Sp = [[S.res("Sp%d_%d" % (i, g)) for g in range(2)] for i in range(2)]
        qh = sb("qh", [128, 8, T], BF16); r_qh = [S.res("qh%d" % c) for c in range(8)]
        kh = [sb("kh%d" % i, [128, T], BF16) for i in range(2)]; r_kh = [S.res("kh%d" % i) for i in range(2)]
        scT = [sb("scT%d" % i, [128, 8, 64], BF16) for i in range(2)]
        r_scT = [S.res("scT%d" % i) for i in range(2)]
        EA = sb("EA", [128, 8, 8], F32); EB = sb("EB", [128, 8, 8], F32); EC = sb("EC", [128, 8, 8], F32)
        r_E = [S.res("E%d" % c) for c in range(8)]
        ED = sb("ED", [128, 8, 8], F32)

        banks = [st.enter_context(nc.psum_tensor("pb%d" % i, [128, 512], F32)) for i in range(8)]
        r_banks = [S.res("pb%d" % i) for i in range(8)]
        state = dict(bank=0, slot=0)

        def nb():
            i = state["bank"]
            state["bank"] = (i + 1) % 8
            return banks[i], r_banks[i]

        _mats = [(G_CIN, 4), (G_COUT, 2), (G_W1_0, 8), (G_W2_0, 8), (G_HIN, 10), (G_HOUT, 2), (G_W1_1, 8), (G_W2_1, 8)]
        r_wsc = [None] * NG
        for (g0, n) in _mats:
            rm = S.res("wsc%d" % g0)
            for g in range(g0, g0 + n):
                r_wsc[g] = rm
        r_x1s = [S.res("x1s")] * NT
        r_ofw = [S.res("ofw")] * NT
        r_out = [S.res("out")] * NT
        r_ccin = S.res("ccin"); r_ccout = S.res("ccout")
        r_gsc = [S.res("gsc")] * NT
        r_zsc = [S.res("zsc")] * NT
        r_stgdma = [S.res("stgdma%d" % q) for q in range(3)]
        r_qsc = [S.res("qsc")] * NT
        r_vsc = [S.res("vsc")] * NT

        def wg(g):
            i = state["slot"]
            state["slot"] = (i + 1) % NSLOT
            S.op("sp", lambda e, i=i, g=g: e.dma_start(out=slots[i][:], in_=wsc[g]),
                 reads=[r_wsc[g]], writes=[r_slots[i]], dma=True)
            return slots[i], r_slots[i]

        def vcol(base, c, n=1):
            return vecs[:, base + c: base + c + n]

        S.op("sp", lambda e: e.dma_start(out=vecs[:], in_=vecs_d), writes=[r_vecs], dma=True)
        S.op("sp", lambda e: e.dma_start(out=cmat[:], in_=cmat_d), writes=[r_cmat], dma=True)
        S.op("dve", lambda e: e.tensor_copy(out=ident[:], in_=cmat[:, CM_ID:CM_ID + 128]), reads=[r_cmat], writes=[r_const])
        S.op("dve", lambda e: e.memset(ones[:], 1.0), writes=[r_const])
        masks = cmat[:, CM_MASK:CM_MASK + 256].rearrange("p (a b) -> p a b", a=4)
        scanmask = cmat[:, CM_SCAN:CM_SCAN + 512]

        def conv_k1024(g, src, col0):
            S.op("pool", lambda e: e.dma_start(
                out=wsc[g].rearrange("p (k n) -> p k n", k=8),
                in_=src[:, col0:col0 + 512].rearrange("(k p) n -> p k n", p=128)),
                writes=[r_wsc[g]], dma=True, nodep=True)

        def conv_w2(g, src, row0):
            S.op("pool", lambda e: e.dma_start(
                out=wsc[g].rearrange("p (k n) -> p k n", k=4),
                in_=src[row0:row0 + 512, :].rearrange("(k p) n -> p k n", p=128)),
                writes=[r_wsc[g]], dma=True, nodep=True)

        def convert_layer0():
            for j in range(4):
                conv_k1024(G_CIN + j, cin_d, 512 * j)
            for j in range(2):
                conv_k1024(G_COUT + j, cout_d, 512 * j)
            for j in range(8):
                conv_k1024(G_W1_0 + j, w1_d[0], 512 * j)
            for j in range(8):
                conv_w2(G_W2_0 + j, w2_d[0], 512 * j)

        l1_convs = ([lambda j=j: conv_k1024(G_HIN + j, hin_d, 512 * j) for j in range(10)]
                    + [lambda j=j: conv_k1024(G_HOUT + j, hout_d, 512 * j) for j in range(2)]
                    + [lambda j=j: conv_k1024(G_W1_1 + j, w1_d[1], 512 * j) for j in range(8)]
                    + [lambda j=j: conv_w2(G_W2_1 + j, w2_d[1], 512 * j) for j in range(8)])

        S.op("sp", lambda e: e.dma_start(out=xt[0][:], in_=xT[:, :, 1:1 + TE]), writes=[r_xt[0]], dma=True)

        S.op("act", lambda e: e.activation(out=cond_bf[:], in_=vcol(V_C, 0, 8), func=AF.Silu), reads=[r_vecs], writes=[r_cond])
        def ada_batch(l, k):
            mbank, r_mbank = nb()
            for g2 in range(2):
                grp = 2 * k + g2
                i = state["slot"]
                state["slot"] = (i + 1) % NSLOT
                sl3 = slots[i][:].rearrange("p (k n) -> p k n", k=8)
                r_sl = r_slots[i]
                S.op("pool", lambda e, grp=grp, sl3=sl3: e.dma_start(
                    out=sl3, in_=ada_w[l][:, grp * 512:(grp + 1) * 512].rearrange("(k p) n -> p k n", p=128)),
                    writes=[r_slots_sw[i], r_sl], dma=True)
                for j in range(4):
                    ch = g2 * 4 + j
                    for kc in range(8):
                        S.op("pe", lambda e, sl3=sl3, j=j, kc=kc, ch=ch: e.matmul(
                            mbank[:, ch:ch + 1], lhsT=sl3[:, kc, j * 128:(j + 1) * 128], rhs=cond_bf[:, kc:kc + 1],
                            start=(kc == 0), stop=(kc == 7)),
                            reads=[r_sl, r_cond], writes=[r_mbank])
            c0 = l * 48 + k * 8
            S.op("dve", lambda e: e.tensor_tensor(out=mod[:, c0:c0 + 8], in0=mbank[:, 0:8], in1=vcol(V_ADAB, c0, 8), op=ALU.add),
                 reads=[r_mbank, r_vecs], writes=[r_mod])
            if k == 1:
                S.op("dve", lambda e: e.scalar_tensor_tensor(
                    out=gv[:, l, 0:8], in0=mod[:, c0:c0 + 8], scalar=1.0, in1=vcol(V_N1G, l * 8, 8),
                    op0=ALU.add, op1=ALU.mult), reads=[r_mod, r_vecs], writes=[r_gv])
            if k == 4:
                S.op("dve", lambda e: e.scalar_tensor_tensor(
                    out=gv[:, l, 8:16], in0=mod[:, c0:c0 + 8], scalar=1.0, in1=vcol(V_N2G, l * 8, 8),
                    op0=ALU.add, op1=ALU.mult), reads=[r_mod, r_vecs], writes=[r_gv])

        for k in range(6):
            ada_batch(0, k)
        convert_layer0()
        S.op("dve", lambda e: e.tensor_tensor(out=lbv[:, 2, :], in0=vcol(V_LBL, 8, 8), in1=vcol(V_LBL, 0, 8), op=ALU.subtract),
             reads=[r_vecs], writes=[r_gv])
        S.op("act", lambda e: e.activation(out=lbv[:, 0, :], in_=lbv[:, 2, :], func=AF.Sigmoid), reads=[r_gv], writes=[r_gv])
        S.op("act", lambda e: e.activation(out=lbv[:, 1, :], in_=lbv[:, 0, :], func=AF.Ln, scale=-1.0, bias=1.0),
             reads=[r_gv], writes=[r_gv])

        def MOD(l, k, c):
            return mod[:, l * 48 + k * 8 + c: l * 48 + k * 8 + c + 1]

        def rms_norm(xsrc, r_x, W, gain, shift, out_fn, r_out_fn, out_final=False):
            segs = [(0, W)] if W <= 512 else [(0, W // 2), (W // 2, W)]
            pbs = [nb() for _ in segs]
            for c in range(8):
                k = c % 2
                xa = xsrc(c)
                S.op("act", lambda e, xa=xa, k=k: e.activation(out=sqb[k][:, 0:W], in_=xa, func=AF.Square),
                     reads=[r_x], writes=[r_sqb[k]])
                for (s0, s1), (pb, r_pb) in zip(segs, pbs):
                    S.op("pe", lambda e, c=c, k=k, s0=s0, s1=s1, pb=pb: e.matmul(
                        pb[:, 0:s1 - s0], lhsT=ones[:], rhs=sqb[k][:, s0:s1], start=(c == 0), stop=(c == 7)),
                        reads=[r_sqb[k], r_const], writes=[r_pb])
            for (s0, s1), (pb, r_pb) in zip(segs, pbs):
                S.op("act", lambda e, s0=s0, s1=s1, pb=pb: e.activation(
                    out=vA[:, s0:s1], in_=pb[:, 0:s1 - s0], func=AF.Ln, scale=1.0 / D, bias=EPS),
                    reads=[r_pb], writes=[r_vA])
            S.op("act", lambda e: e.activation(out=vB[:, 0:W], in_=vA[:, 0:W], func=AF.Exp, scale=-0.5),
                 reads=[r_vA], writes=[r_vB])
            for c in range(8):
                k = c % 2
                xa = xsrc(c); ga = gain(c); oa = out_fn(c); r_o = r_out_fn(c)
                if out_final:
                    S.op("dve", lambda e, xa=xa, ga=ga, oa=oa: e.scalar_tensor_tensor(
                        out=oa, in0=xa, scalar=ga, in1=vB[:, 0:W], op0=ALU.mult, op1=ALU.mult),
                        reads=[r_x, r_vB, r_gv, r_vecs], writes=[r_o])
                else:
                    sa = shift(c)
                    S.op("dve", lambda e, xa=xa, ga=ga, k=k: e.scalar_tensor_tensor(
                        out=tmpn[k][:, 0:W], in0=xa, scalar=ga, in1=vB[:, 0:W], op0=ALU.mult, op1=ALU.mult),
                        reads=[r_x, r_vB, r_gv], writes=[r_tmpn[k]])
                    S.op("act", lambda e, oa=oa, sa=sa, k=k: e.activation(
                        out=oa, in_=tmpn[k][:, 0:W], func=AF.Identity, scale=1.0, bias=sa),
                        reads=[r_tmpn[k], r_mod], writes=[r_o])

        def proj_fm(slot, r_slot, j, rhs_fn, r_rhs, W):
            segs = [(0, W)] if W <= 512 else [(0, W // 2), (W // 2, W)]
            outs = []
            for (s0, s1) in segs:
                pb, r_pb = nb()
                for kc in range(8):
                    S.op("pe", lambda e, kc=kc, s0=s0, s1=s1, pb=pb: e.matmul(
                        pb[:, 0:s1 - s0], lhsT=slot.rearrange("p (k n) -> p k n", k=8)[:, kc, j * 128:(j + 1) * 128],
                        rhs=rhs_fn(kc)[:, s0:s1], start=(kc == 0), stop=(kc == 7)),
                        reads=[r_slot] + r_rhs, writes=[r_pb])
                outs.append((pb, r_pb, s0, s1))
            return outs

        def mlp(l, xb, r_x, g_w1, g_w2, mid_hook=None, group_hook=None):
            xc = lambda c: xt[xb][:, c, 15:15 + T]
            rms_norm(xc, r_x, T, lambda c: gv[:, l, 8 + c:9 + c], lambda c: MOD(l, 3, c),
                     lambda c: h[:, c, 0:T], lambda c: r_h[c])
            aT = lambda fc: arena[:, fc * 512:(fc + 1) * 512]
            for g in range(8):
                slot, r_slot = wg(g_w1 + g)
                for j in range(4):
                    fc = 4 * g + j
                    (pb, r_pb, _, _), = proj_fm(slot[:], r_slot, j, lambda kc: h[:, kc, 0:T], r_h, T)
                    k = fc % 2
                    S.op("act", lambda e, pb=pb, k=k: e.activation(out=rt[k][:], in_=pb[:], func=AF.Relu),
                         reads=[r_pb], writes=[r_rt[k]])
                    S.op("dve" if group_hook is not None else "pool",
                         lambda e, fc=fc, k=k: e.tensor_tensor(out=aT(fc), in0=rt[k][:], in1=rt[k][:], op=ALU.mult),
                         reads=[r_rt[k]], writes=[r_ar[fc]])
                if group_hook is not None:
                    group_hook(g)
            if mid_hook is not None:
                mid_hook()
            for g in range(8):
                slot, r_slot = wg(g_w2 + g)
                s3 = slot[:].rearrange("p (k n) -> p k n", k=4)
                for oc in range(8):
                    for fcl in range(4):
                        fc = 4 * g + fcl
                        S.op("pe", lambda e, s3=s3, oc=oc, fcl=fcl, fc=fc, g=g: e.matmul(
                            banks[oc][:], lhsT=s3[:, fcl, oc * 128:(oc + 1) * 128], rhs=aT(fc),
                            start=(g == 0 and fcl == 0), stop=(g == 7 and fcl == 3)),
                            reads=[r_slot, r_ar[fc]], writes=[r_banks[oc]])
            for oc in range(8):
                S.op("dve", lambda e, oc=oc: e.scalar_tensor_tensor(
                    out=xc(oc), in0=banks[oc][:], scalar=MOD(l, 5, oc), in1=xc(oc), op0=ALU.mult, op1=ALU.add),
                    reads=[r_banks[oc], r_mod, r_x], writes=[r_x])

        def wout_residual(l, xb, r_x, g_wout):
            xc = lambda c: xt[xb][:, c, 15:15 + T]
            for half in range(2):
                slot, r_slot = wg(g_wout + half)
                for j in range(4):
                    oc = 4 * half + j
                    (pb, r_pb, _, _), = proj_fm(slot[:], r_slot, j, lambda kc: h[:, kc, 0:T], r_h, T)
                    S.op("dve", lambda e, pb=pb, oc=oc: e.scalar_tensor_tensor(
                        out=xc(oc), in0=pb[:], scalar=MOD(l, 2, oc), in1=xc(oc), op0=ALU.mult, op1=ALU.add),
                        reads=[r_pb, r_mod, r_x], writes=[r_x])

        def load_x0(i):
            b = i % 2
            S.op("sp", lambda e: e.dma_start(out=xt[b][:], in_=xT[:, :, 512 * i + 1: 512 * i + 1 + TE]),
                 writes=[r_xt[b]], dma=True)

        y32 = arena[:, 0:8192].bitcast(F32).rearrange("p (c t) -> p c t", c=8)
        ybf = arena[:, 8192:12288].rearrange("p (c t) -> p c t", c=8)
        ysq = arena[:, 12288:16384].rearrange("p (c t) -> p c t", c=8)
        r_y32 = lambda c: [r_ar[2 * c], r_ar[2 * c + 1]]

        def l0_mixer(i):
            b = i % 2
            r_x = r_xt[b]
            rms_norm(lambda c: xt[b][:, c, :], r_x, TE, lambda c: gv[:, 0, c:c + 1], lambda c: MOD(0, 0, c),
                     lambda c: h[:, c, :], lambda c: r_h[c])
            for half in range(2):
                slA, r_slA = wg(G_CIN + half)
                slG, r_slG = wg(G_CIN + 2 + half)
                for j in range(4):
                    for (pb, r_pb, s0, s1) in proj_fm(slA[:], r_slA, j, lambda kc: h[:, kc, :], r_h, TE):
                        S.op("act", lambda e, pb=pb, s0=s0, s1=s1, j=j: e.activation(
                            out=a_sb[:, j, s0:s1], in_=pb[:, 0:s1 - s0], func=AF.Copy), reads=[r_pb], writes=[r_asb[j]])
                for j in range(4):
                    c = 4 * half + j
                    k = c % 2
                    for (pb, r_pb, s0, s1) in proj_fm(slG[:], r_slG, j, lambda kc: h[:, kc, :], r_h, TE):
                        S.op("act", lambda e, pb=pb, s0=s0, s1=s1, k=k: e.activation(
                            out=sgt[k][:, s0:s1], in_=pb[:, 0:s1 - s0], func=AF.Sigmoid), reads=[r_pb], writes=[r_sgt[k]])
                    S.op("pool", lambda e, c=c, j=j, k=k: e.tensor_tensor(
                        out=u[:, c, :], in0=a_sb[:, j, :], in1=sgt[k], op=ALU.mult),
                        reads=[r_asb[j], r_sgt[k]], writes=[r_u[c]])
                    if i == 0:
                        S.op("pool", lambda e, c=c: e.memset(u[:, c, 0:15], 0.0), writes=[r_u[c]])
            if USE_PE_CONV:
                def build_diag(c):
                    par = c % 2
                    S.op("dve", lambda e: e.tensor_tensor(
                        out=dg[par], in0=ident[:].unsqueeze(1).broadcast_to([128, 31, 128]),
                        in1=vecs[:, V_DW + c * 31: V_DW + c * 31 + 31].unsqueeze(2).broadcast_to([128, 31, 128]), op=ALU.mult),
                        reads=[r_const, r_vecs], writes=[r_dg[par]])

                build_diag(0)
                build_diag(1)
                for c in range(8):
                    par = c % 2
                    pb, r_pb = nb()
                    for tap in range(31):
                        S.op("pe", lambda e, pb=pb, par=par, tap=tap, c=c: e.matmul(
                            pb[:], lhsT=dg[par][:, tap, :], rhs=u[:, c, tap:tap + T], start=(tap == 0), stop=(tap == 30)),
                            reads=[r_dg[par], r_u[c]], writes=[r_pb])
                    if c + 2 < 8:
                        build_diag(c + 2)
                    S.op("act", lambda e, pb=pb, c=c: e.activation(out=y32[:, c, :], in_=pb[:], func=AF.Identity, scale=1.0, bias=vcol(V_DWB, c)),
                         reads=[r_pb, r_vecs], writes=r_y32(c))
                    S.op("act", lambda e, pb=pb, c=c: e.activation(out=ybf[:, c, :], in_=pb[:], func=AF.Identity, scale=1.0, bias=vcol(V_DWB, c)),
                         reads=[r_pb, r_vecs], writes=[r_ar[16 + c]])
                    S.op("pool", lambda e, c=c: e.tensor_tensor(out=ysq[:, c, :], in0=y32[:, c, :], in1=y32[:, c, :], op=ALU.mult),
                         reads=r_y32(c), writes=[r_ar[24 + c]])
            else:
                for tap in range(31):
                    for c in range(8):
                        if tap == 0:
                            S.op("dve", lambda e, c=c: e.tensor_scalar(
                                out=y32[:, c, :], in0=u[:, c, 0:T], scalar1=vcol(V_DW, c * 31), scalar2=vcol(V_DWB, c),
                                op0=ALU.mult, op1=ALU.add), reads=[r_u[c], r_vecs], writes=r_y32(c))
                        else:
                            S.op("dve", lambda e, c=c, tap=tap: e.scalar_tensor_tensor(
                                out=y32[:, c, :], in0=u[:, c, tap:tap + T], scalar=vcol(V_DW, c * 31 + tap), in1=y32[:, c, :],
                                op0=ALU.mult, op1=ALU.add), reads=[r_u[c]] + r_y32(c), writes=r_y32(c))
                for c in range(8):
                    S.op("act", lambda e, c=c: e.activation(out=ybf[:, c, :], in_=y32[:, c, :], func=AF.Copy),
                         reads=r_y32(c), writes=[r_ar[16 + c]])
                    S.op("pool", lambda e, c=c: e.tensor_tensor(out=ysq[:, c, :], in0=y32[:, c, :], in1=y32[:, c, :], op=ALU.mult),
                         reads=r_y32(c), writes=[r_ar[24 + c]])
            pbm, r_pbm = nb()
            pbv, r_pbv = nb()
            for c in range(8):
                S.op("pe", lambda e, c=c, pbm=pbm: e.matmul(pbm[:], lhsT=ones[:], rhs=ybf[:, c, :], start=(c == 0), stop=(c == 7)),
                     reads=[r_ar[16 + c], r_const], writes=[r_pbm])
                S.op("pe", lambda e, c=c, pbv=pbv: e.matmul(pbv[:], lhsT=ones[:], rhs=ysq[:, c, :], start=(c == 0), stop=(c == 7)),
                     reads=[r_ar[24 + c], r_const], writes=[r_pbv])
            if 1 <= i <= 6:
                ada_batch(1, i - 1)
            S.op("act", lambda e, pbm=pbm: e.activation(out=vA[:, 0:T], in_=pbm[:], func=AF.Copy, scale=1.0 / D), reads=[r_pbm], writes=[r_vA])
            S.op("dve", lambda e: e.tensor_tensor(out=vC[:, 0:T], in0=vA[:, 0:T], in1=vA[:, 0:T], op=ALU.mult), reads=[r_vA], writes=[r_vC])
            S.op("dve", lambda e, pbv=pbv: e.scalar_tensor_tensor(out=vC[:, 0:T], in0=pbv[:], scalar=1.0 / D, in1=vC[:, 0:T],
                                                          op0=ALU.mult, op1=ALU.subtract), reads=[r_pbv, r_vC], writes=[r_vC])
            S.op("act", lambda e: e.activation(out=vC[:, 0:T], in_=vC[:, 0:T], func=AF.Ln, scale=1.0, bias=EPS), reads=[r_vC], writes=[r_vC])
            S.op("act", lambda e: e.activation(out=vB[:, 0:T], in_=vC[:, 0:T], func=AF.Exp, scale=-0.5), reads=[r_vC], writes=[r_vB])
            for c in range(8):
                k = c % 2
                S.op("dve", lambda e, c=c, k=k: e.tensor_tensor(out=tmpn[k][:, 0:T], in0=y32[:, c, :], in1=vA[:, 0:T], op=ALU.subtract),
                     reads=r_y32(c) + [r_vA], writes=[r_tmpn[k]])
                S.op("dve", lambda e, k=k: e.tensor_tensor(out=tmpn[k][:, 0:T], in0=tmpn[k][:, 0:T], in1=vB[:, 0:T], op=ALU.mult),
                     reads=[r_tmpn[k], r_vB], writes=[r_tmpn[k]])
                S.op("act", lambda e, c=c, k=k: e.activation(out=h[:, c, 0:T], in_=tmpn[k][:, 0:T], func=AF.Silu,
                                                              scale=vcol(V_LNG, c), bias=vcol(V_LNB, c)),
                     reads=[r_tmpn[k], r_vecs], writes=[r_h[c]])
            wout_residual(0, b, r_x, G_COUT)

        def l0_mlp(i):
            b = i % 2
            r_x = r_xt[b]
            hook = None
            if i >= 1:
                hook = lambda i=i: [f() for f in l1_convs[4 * (i - 1):4 * i]]
            mlp(0, b, r_x, G_W1_0, G_W2_0, mid_hook=hook)
            S.op("pool", lambda e, i=i, b=b: e.dma_start(out=x1s[:, :, T * i:T * (i + 1)], in_=xt[b][:, :, 15:15 + T]),
                 reads=[r_x], writes=[r_x1s[i]], dma=True, nodep=True)

        load_x0(1)
        l0_mixer(0)
        l0_mixer(1)
        l0_mlp(0)
        load_x0(2)
        l0_mlp(1)
        load_x0(3)
        for i in range(2, NT):
            l0_mixer(i)
            l0_mlp(i)
            if i + 2 < NT:
                load_x0(i + 2)

        S.barrier(lambda e: e.memset(ED[:], 0.0))
        S.op("dve", lambda e: e.memset(Sst[:], 0.0), writes=r_Sg)
        S.op("dve", lambda e: e.memset(Sp[0][:], 0.0), writes=r_Sp[0])
        state["cn"] = 0

        def load_x1(i, b):
            S.op("sp", lambda e: e.dma_start(out=xt[b][:, :, 15:15 + T], in_=x1s[:, :, T * i:T * (i + 1)]),
                 reads=[r_x1s[i]], writes=[r_xt[b]], dma=True)

        o_sb = arena[:, 0:8192].bitcast(F32).rearrange("p (c t) -> p c t", c=8)
        of_sb = arena[:, 8192:16384].bitcast(F32).rearrange("p (c t) -> p c t", c=8)
        r_osb = r_ar[0:16]
        r_ofsb = r_ar[16:32]

        def gate_pipeline(fwd, zslots, hk, ztile=None):
            tstate = {}

            zstate = {}

            def Z(hd):
                if ztile is not None:
                    g_e = gtb[hd % NSET][0]
                    r_e = r_gtb[hd % NSET][0]
                    S.op("sp", lambda e: e.dma_start(out=g_e[:], in_=zsc[:, hd, T * ztile:T * (ztile + 1)]),
                         reads=[r_zsc[ztile]], writes=[r_e], dma=True)
                    zstate[hd] = (g_e, r_e)
                    return
                slot, r_slot = zslots[hd // 4]
                (pz, r_pz, _, _), = proj_fm(slot[:], r_slot, hd % 4, hk, r_h, T)
                zstate[hd] = (pz, r_pz)

            def A1(hd):
                g_e, g_w, g_lf, g_B = gtb[hd % NSET]
                r_e, r_w, r_lf, r_B = r_gtb[hd % NSET]
                pz, r_pz = zstate[hd]
                lb_c = lbv[:, 0, hd:hd + 1]
                S.op("act", lambda e: e.activation(out=g_e[:], in_=pz[:], func=AF.Exp), reads=[r_pz, r_e], writes=[r_e])
                S.op("act", lambda e: e.activation(out=g_w[:], in_=g_e[:], func=AF.Ln, scale=1.0, bias=1.0), reads=[r_e], writes=[r_w])
                S.op("act", lambda e: e.activation(out=g_lf[:], in_=g_e[:], func=AF.Ln, scale=1.0, bias=lb_c), reads=[r_e, r_gv], writes=[r_lf])

            def D1(hd):
                g_e, g_w, g_lf, g_B = gtb[hd % NSET]
                r_e, r_w, r_lf, r_B = r_gtb[hd % NSET]
                B3 = g_B[:].rearrange("p (c t) -> p c t", t=64)
                S.op("pool", lambda e: e.tensor_tensor(out=g_lf[:], in0=g_lf[:], in1=g_w[:], op=ALU.subtract), reads=[r_lf, r_w], writes=[r_lf])
                S.op("dve", lambda e: e.tensor_tensor_scan(out=g_B[:], data0=scanmask, data1=g_lf[:], initial=0.0, op0=ALU.mult, op1=ALU.add),
                     reads=[r_cmat, r_lf], writes=[r_B])
                if fwd:
                    S.op("dve", lambda e: e.tensor_copy(out=Bcol[:, hd, :, :], in_=B3[:, :, 31::32]), reads=[r_B], writes=[r_E[hd]])
                    S.op("dve", lambda e: e.tensor_tensor(out=B3, in0=B3, in1=Bcol[:, hd, :, 0:1].broadcast_to([128, 8, 64]), op=ALU.subtract),
                         reads=[r_B, r_E[hd]], writes=[r_B])
                    S.op("dve", lambda e: e.tensor_tensor(out=g_w[:], in0=g_w[:], in1=g_B[:], op=ALU.add), reads=[r_w, r_B], writes=[r_w])
                else:
                    S.op("dve", lambda e: e.tensor_copy(out=Bcol[:, hd, :, 1:2], in_=B3[:, :, 63:64]), reads=[r_B], writes=[r_E[hd]])
                    S.op("dve", lambda e: e.tensor_tensor(out=g_B[:], in0=g_B[:], in1=g_lf[:], op=ALU.subtract), reads=[r_B, r_lf], writes=[r_B])
                    S.op("dve", lambda e: e.tensor_copy(out=Bcol[:, hd, :, 0:1], in_=B3[:, :, 32:33]), reads=[r_B], writes=[r_E[hd]])
                    S.op("dve", lambda e: e.tensor_tensor(out=B3, in0=B3, in1=Bcol[:, hd, :, 0:1].broadcast_to([128, 8, 64]), op=ALU.subtract),
                         reads=[r_B, r_E[hd]], writes=[r_B])
                    S.op("dve", lambda e: e.tensor_tensor(out=g_w[:], in0=g_w[:], in1=g_B[:], op=ALU.subtract), reads=[r_w, r_B], writes=[r_w])
                    S.op("dve", lambda e: e.tensor_tensor(out=ED[:, hd, :], in0=Bcol[:, hd, :, 1], in1=Bcol[:, hd, :, 0], op=ALU.subtract),
                         reads=[r_E[hd]], writes=[r_E[hd]])

            def A2(hd):
                g_e, g_w, g_lf, g_B = gtb[hd % NSET]
                r_e, r_w, r_lf, r_B = r_gtb[hd % NSET]
                l1m_c = lbv[:, 1, hd:hd + 1]
                B3 = g_B[:].rearrange("p (c t) -> p c t", t=64)
                S.op("act", lambda e: e.activation(out=g_e[:], in_=g_B[:], func=AF.Exp, scale=(1.0 if fwd else -1.0)), reads=[r_B], writes=[r_e])
                S.op("act", lambda e: e.activation(out=kT[:, hd, :], in_=g_w[:], func=AF.Exp, scale=-1.0, bias=l1m_c),
                     reads=[r_w, r_gv], writes=[r_kT[hd]])
                S.op("act", lambda e: e.activation(out=EB[:, hd, :], in_=Bcol[:, hd, :, 1], func=AF.Exp), reads=[r_E[hd]], writes=[r_E[hd]])
                if fwd:
                    S.op("act", lambda e: e.activation(out=EA[:, hd, :], in_=Bcol[:, hd, :, 0], func=AF.Exp), reads=[r_E[hd]], writes=[r_E[hd]])
                    S.op("act", lambda e: e.activation(out=EC[:, hd, :], in_=B3[:, :, 63], func=AF.Exp), reads=[r_B], writes=[r_E[hd]])
                else:
                    S.op("act", lambda e: e.activation(out=EC[:, hd, :], in_=Bcol[:, hd, :, 0], func=AF.Exp), reads=[r_E[hd]], writes=[r_E[hd]])
                    S.op("act", lambda e: e.activation(out=EA[:, hd, :], in_=ED[:, hd, :], func=AF.Exp), reads=[r_E[hd]], writes=[r_E[hd]])
                S.op("pool", lambda e: e.tensor_tensor(out=qs[:, hd, :], in0=qs[:, hd, :], in1=g_e[:], op=ALU.mult),
                     reads=[r_qs[hd], r_e], writes=[r_qs[hd]])
                S.op("pool", lambda e: e.tensor_tensor(
                    out=qh[:, hd, :].rearrange("p (c t) -> p c t", t=64), in0=qs[:, hd, :].rearrange("p (c t) -> p c t", t=64),
                    in1=EA[:, hd, :].unsqueeze(2).broadcast_to([128, 8, 64]), op=ALU.mult),
                    reads=[r_qs[hd], r_E[hd]], writes=[r_qh[hd]])
                kk = hd % 2
                S.op("dve", lambda e: e.tensor_tensor(
                    out=kh[kk][:].rearrange("p (c t) -> p c t", t=64), in0=kT[:, hd, :].rearrange("p (c t) -> p c t", t=64),
                    in1=EC[:, hd, :].unsqueeze(2).broadcast_to([128, 8, 64]), op=ALU.mult),
                    reads=[r_kT[hd], r_E[hd]], writes=[r_kh[kk]])

            def TT(hd):
                kk = hd % 2
                pb, r_pb = nb()
                pbb = pb[:].bitcast(BF16)
                for blk in range(4):
                    S.op("pe", lambda e, blk=blk: e.transpose(
                        pbb[:, blk * 128:(blk + 1) * 128], kh[kk][:, blk * 128:(blk + 1) * 128], ident[:]),
                        reads=[r_kh[kk], r_const], writes=[r_pb])
                tstate[hd] = (pbb, r_pb)

            def A3(hd):
                pbb, r_pb = tstate[hd]
                if fwd:
                    S.op("dve", lambda e: e.tensor_copy(out=ktm[:, :, hd, :], in_=pbb[:, 0:512].rearrange("p (a b) -> p a b", a=4)),
                         reads=[r_pb], writes=r_ktm)
                else:
                    S.op("act", lambda e: e.activation(out=ktm[:, :, hd, :], in_=pbb[:, 0:512].rearrange("p (a b) -> p a b", a=4), func=AF.Copy),
                         reads=[r_pb], writes=r_ktm)

            def step(k):
                if k == -2:
                    Z(0); Z(1); Z(2)
                if ztile is None and 0 <= k + 3 < 8 and k + 3 >= 3:
                    Z(k + 3)
                if 0 <= k + 2 < 8:
                    A1(k + 2)
                if 0 <= k - 1 < 8:
                    TT(k - 1)
                if 0 <= k + 1 < 8:
                    D1(k + 1)
                if 0 <= k - 2 < 8:
                    A3(k - 2)
                if 0 <= k < 8:
                    A2(k)
                if ztile is not None and 0 <= k + 3 < 8 and k + 3 >= 3:
                    Z(k + 3)

            return step

        def load_bwd_operands(i):
            S.op("sp", lambda e: e.dma_start(out=qs, in_=qsc[:, :, T * i:T * (i + 1)]), reads=[r_qsc[i]], writes=r_qs, dma=True)
            S.op("sp", lambda e: e.dma_start(out=un[:, 12288:16384], in_=vsc[i]), reads=[r_vsc[i]], writes=r_vtm, dma=True)

        def scan_pass(direction):
            fwd = direction == 0
            order = list(range(NT)) if fwd else list(range(NT - 1, -1, -1))
            zg = G_HIN + 2 if fwd else G_HIN + 4
            load_x1(order[0], 0)
            for n, i in enumerate(order):
                b = n % 2
                if n + 1 < NT:
                    load_x1(order[n + 1], (n + 1) % 2)
                r_x = r_xt[b]
                xc = lambda c: xt[b][:, c, 15:15 + T]
                if not fwd:
                    S.op("pool", lambda e, i=i: e.dma_start(out=of_sb, in_=ofw[:, :, T * i:T * (i + 1)]),
                         reads=[r_ofw[i]], writes=r_ofsb, dma=True)
                    if n == 0:
                        load_bwd_operands(i)
                        st0 = gate_pipeline(False, None, None, ztile=i)
                        for k in range(-2, 10):
                            st0(k)
                hk = lambda kc: h[:, kc, 0:T]
                if fwd:
                    rms_norm(xc, r_x, T, lambda c: gv[:, 1, c:c + 1], lambda c: MOD(1, 0, c),
                             lambda c: h[:, c, 0:T], lambda c: r_h[c])
                    for half in range(2):
                        slot, r_slot = wg(G_HIN + half)
                        for j in range(4):
                            hd = 4 * half + j
                            (pb, r_pb, _, _), = proj_fm(slot[:], r_slot, j, hk, r_h, T)
                            S.op("act", lambda e, pb=pb, hd=hd: e.activation(out=qs[:, hd, :], in_=pb[:], func=AF.Silu),
                                 reads=[r_pb], writes=[r_qs[hd]])
                    for half in range(2):
                        slot, r_slot = wg(G_HIN + 6 + half)
                        s3 = slot[:].rearrange("p (k n) -> p k n", k=8)
                        for blk in range(4):
                            pb, r_pb = nb()
                            for kc in range(8):
                                S.op("pe", lambda e, s3=s3, blk=blk, kc=kc, pb=pb: e.matmul(
                                    pb[:], lhsT=h[:, kc, blk * 128:(blk + 1) * 128], rhs=s3[:, kc, :],
                                    start=(kc == 0), stop=(kc == 7)), reads=[r_slot] + r_h, writes=[r_pb])
                            S.op("act", lambda e, pb=pb, blk=blk, half=half: e.activation(
                                out=vtm[:, blk, half * 512:(half + 1) * 512], in_=pb[:], func=AF.Copy),
                                reads=[r_pb], writes=[r_vtm[blk]])
                    S.op("sp", lambda e, i=i: e.dma_start(out=qsc[:, :, T * i:T * (i + 1)], in_=qs), reads=r_qs, writes=[r_qsc[i]], dma=True, nodep=True)
                    S.op("sp", lambda e, i=i: e.dma_start(out=vsc[i], in_=un[:, 12288:16384]), reads=r_vtm, writes=[r_vsc[i]], dma=True, nodep=True)
                    for half in range(2):
                        slot, r_slot = wg(G_HIN + 8 + half)
                        for j in range(4):
                            hd = 4 * half + j
                            (pb, r_pb, _, _), = proj_fm(slot[:], r_slot, j, hk, r_h, T)
                            S.op("act", lambda e, pb=pb, hd=hd: e.activation(out=kT[:, hd, :], in_=pb[:], func=AF.Silu),
                                 reads=[r_pb], writes=[r_kT[hd]])
                    S.op("sp", lambda e, i=i: e.dma_start(out=gsc[:, :, T * i:T * (i + 1)], in_=kT), reads=r_kT, writes=[r_gsc[i]], dma=True, nodep=True)
                    for half in range(2):
                        slot, r_slot = wg(G_HIN + 4 + half)
                        for j in range(4):
                            hd = 4 * half + j
                            (pb, r_pb, _, _), = proj_fm(slot[:], r_slot, j, hk, r_h, T)
                            stg, r_stg = gtb[hd % NSET][0], r_gtb[hd % NSET][0]
                            S.op("dve", lambda e, pb=pb, stg=stg: e.tensor_copy(out=stg[:], in_=pb[:]), reads=[r_pb], writes=[r_stg])
                            S.op("sp", lambda e, i=i, hd=hd, stg=stg: e.dma_start(out=zsc[:, hd, T * i:T * (i + 1)], in_=stg[:]),
                                 reads=[r_stg], writes=[r_stgdma[hd % NSET], r_zsc[i]], dma=True, nodep=True)
                    zslots = [wg(zg + half) for half in range(2)]
                    stf = gate_pipeline(True, zslots, hk)
                    for k in range(-2, 10):
                        stf(k)
                corder = list(range(8)) if fwd else list(range(7, -1, -1))
                for cn, c in enumerate(corder):
                    blk, par = c // 2, c % 2
                    kk = cn % 2
                    gcn = state["cn"]; state["cn"] += 1
                    sp_cur, sp_nxt = gcn % 2, (gcn + 1) % 2
                    mk = masks[:, (0 if fwd else 2) + par, :]
                    psc, r_psc = nb()
                    for hd in range(8):
                        S.op("pe", lambda e, psc=psc, hd=hd, blk=blk, c=c: e.matmul(
                            psc[:, hd * 64:(hd + 1) * 64], lhsT=kT[:, hd, blk * 128:(blk + 1) * 128],
                            rhs=qs[:, hd, c * 64:(c + 1) * 64], start=True, stop=True),
                            reads=[r_kT[hd], r_qs[hd]], writes=[r_psc])
                    S.op("dve", lambda e, psc=psc, kk=kk, mk=mk: e.tensor_tensor(
                        out=scT[kk][:], in0=psc[:].rearrange("p (a b) -> p a b", a=8),
                        in1=mk.unsqueeze(1).broadcast_to([128, 8, 64]), op=ALU.mult),
                        reads=[r_psc, r_cmat], writes=[r_scT[kk]])
                    pds = [nb(), nb()]
                    for hd in range(8):
                        pd, r_pd = pds[hd // 4]
                        p0 = 64 * par
                        S.op("pe", lambda e, pd=pd, hd=hd, blk=blk, p0=p0: e.matmul(
                            pd[:, (hd % 4) * 128:(hd % 4 + 1) * 128], lhsT=ktm[p0:p0 + 64, blk, hd, :],
                            rhs=vtm[p0:p0 + 64, blk, hd * 128:(hd + 1) * 128], start=True, stop=True),
                            reads=r_ktm + [r_vtm[blk]], writes=[r_pd])
                    po, r_po = nb()
                    for hd in range(8):
                        S.op("pe", lambda e, po=po, hd=hd, c=c, sp_cur=sp_cur: e.matmul(
                            po[:, hd * 64:(hd + 1) * 64], lhsT=Sp[sp_cur][:, hd, :], rhs=qh[:, hd, c * 64:(c + 1) * 64],
                            start=True, stop=False), reads=[r_Sp[sp_cur][hd // 4], r_qh[hd]], writes=[r_po])
                        S.op("pe", lambda e, po=po, hd=hd, blk=blk, kk=kk: e.matmul(
                            po[:, hd * 64:(hd + 1) * 64], lhsT=vtm[:, blk, hd * 128:(hd + 1) * 128], rhs=scT[kk][:, hd, :],
                            start=False, stop=True), reads=[r_vtm[blk], r_scT[kk]], writes=[r_po])
                    for hh in range(2):
                        pd, r_pd = pds[hh]
                        for j in range(4):
                            hd = 4 * hh + j
                            S.op("dve", lambda e, pd=pd, hd=hd, j=j, c=c: e.scalar_tensor_tensor(
                                out=Sst[:, hd, :], in0=Sst[:, hd, :], scalar=EB[:, hd, c:c + 1], in1=pd[:, j * 128:(j + 1) * 128],
                                op0=ALU.mult, op1=ALU.add), reads=[r_pd, r_Sg[hd], r_E[hd]], writes=[r_Sg[hd]])
                        S.op("act", lambda e, hh=hh, sp_nxt=sp_nxt: e.activation(
                            out=Sp[sp_nxt][:, 4 * hh:4 * hh + 4, :], in_=Sst[:, 4 * hh:4 * hh + 4, :], func=AF.Copy),
                            reads=r_Sg[4 * hh:4 * hh + 4], writes=[r_Sp[sp_nxt][hh]])
                    po3 = po[:].rearrange("p (a b) -> p a b", a=8)
                    if fwd:
                        S.op("act", lambda e, po3=po3, c=c: e.activation(out=o_sb[:, :, c * 64:(c + 1) * 64], in_=po3, func=AF.Copy),
                             reads=[r_po], writes=r_osb)
                    else:
                        S.op("dve", lambda e, po3=po3, c=c: e.tensor_tensor(
                            out=o_sb[:, :, c * 64:(c + 1) * 64], in0=po3, in1=of_sb[:, :, c * 64:(c + 1) * 64], op=ALU.add),
                            reads=[r_po] + r_ofsb, writes=r_osb)
                if fwd:
                    S.op("pool", lambda e, i=i: e.dma_start(out=ofw[:, :, T * i:T * (i + 1)], in_=o_sb),
                         reads=r_osb, writes=[r_ofw[i]], dma=True, nodep=True)
                    continue
                sg = kT
                S.op("sp", lambda e, i=i: e.dma_start(out=kT, in_=gsc[:, :, T * i:T * (i + 1)]), reads=[r_gsc[i]], writes=r_kT, dma=True)
                for hd in range(8):
                    k = hd % 2
                    S.op("act", lambda e, hd=hd, k=k: e.activation(out=sqb[k][:, 0:T], in_=o_sb[:, hd, :], func=AF.Square),
                         reads=r_osb, writes=[r_sqb[k]])
                    pb, r_pb = nb()
                    S.op("pe", lambda e, pb=pb, k=k: e.matmul(pb[:], lhsT=ones[:], rhs=sqb[k][:, 0:T], start=True, stop=True),
                         reads=[r_sqb[k], r_const], writes=[r_pb])
                    S.op("act", lambda e, pb=pb, k=k: e.activation(out=tmpn[k][:, 0:T], in_=pb[:], func=AF.Ln, scale=1.0 / 128, bias=EPS),
                         reads=[r_pb], writes=[r_tmpn[k]])
                    S.op("act", lambda e, k=k: e.activation(out=tmpn[k][:, 0:T], in_=tmpn[k][:, 0:T], func=AF.Exp, scale=-0.5),
                         reads=[r_tmpn[k]], writes=[r_tmpn[k]])
                    S.op("dve", lambda e, hd=hd, k=k: e.tensor_tensor(out=tmpn[k][:, 0:T], in0=o_sb[:, hd, :], in1=tmpn[k][:, 0:T], op=ALU.mult),
                         reads=r_osb + [r_tmpn[k]], writes=[r_tmpn[k]])
                    S.op("dve", lambda e, hd=hd, k=k: e.scalar_tensor_tensor(
                        out=h[:, hd, 0:T], in0=tmpn[k][:, 0:T], scalar=vcol(V_GNG, hd), in1=sg[:, hd, :], op0=ALU.mult, op1=ALU.mult),
                        reads=[r_tmpn[k], r_vecs, r_kT[hd]] + r_h, writes=[r_h[hd]])
                wout_residual(1, b, r_x, G_HOUT)
                ghook = mhook = None
                if n + 1 < NT:
                    inext = order[n + 1]
                    load_bwd_operands(inext)
                    stn = gate_pipeline(False, None, None, ztile=inext)

                    def ghook(g, stn=stn):
                        if g == 0:
                            stn(-2); stn(-1); stn(0)
                        else:
                            stn(g)

                    def mhook(stn=stn):
                        stn(8); stn(9)
                mlp(1, b, r_x, G_W1_1, G_W2_1, mid_hook=mhook, group_hook=ghook)
                outb = arena[:, 0:8192].bitcast(F32).rearrange("p (c t) -> p c t", c=8)
                rms_norm(xc, r_x, T, lambda c: vcol(V_FG, c), None,
                         lambda c: outb[:, c, :], lambda c: r_ar[2 * c], out_final=True)
                S.op("pool", lambda e, i=i: e.dma_start(out=outT[:, :, T * i:T * (i + 1)], in_=outb),
                     reads=r_ar[0:16], writes=[r_out[i]], dma=True, nodep=True)

        scan_pass(0)
        S.barrier(lambda e: e.memset(ED[:], 0.0))
        S.op("pool", lambda e: e.dma_start(out=cc_in, in_=Sst[:].rearrange("p a b -> p (a b)")), reads=r_Sg, writes=[r_ccin], dma=True)
        S.op("pool", lambda e: e.collective_compute("AllGather", ALU.bypass, replica_groups=[[0, 1], [2, 3], [4, 5], [6, 7]],
                                                     ins=[cc_in], outs=[cc_out]), reads=[r_ccin], writes=[r_ccout])
        Sx = arena[:, 0:4096].bitcast(F32).rearrange("p (r f) -> p r f", r=2)
        r_Sx = r_ar[0:8]
        S.op("pool", lambda e: e.dma_start(out=Sx, in_=cc_out.rearrange("(r p) f -> p r f", p=128)),
             reads=[r_ccout], writes=r_Sx, dma=True)
        Sflat = Sst[:].rearrange("p a b -> p (a b)")
        S.op("dve", lambda e: e.tensor_scalar(out=Sflat, in0=Sx[:, 0, :], scalar1=vcol(V_FLAG, 0), scalar2=None, op0=ALU.mult),
             reads=r_Sx + [r_vecs], writes=r_Sg)
        S.op("dve", lambda e: e.scalar_tensor_tensor(out=Sflat, in0=Sx[:, 1, :], scalar=vcol(V_FLAG, 1), in1=Sflat,
                                                      op0=ALU.mult, op1=ALU.add), reads=r_Sx + [r_vecs] + r_Sg, writes=r_Sg)
        nxt = state["cn"] % 2
        S.op("act", lambda e: e.activation(out=Sp[nxt][:], in_=Sst[:], func=AF.Copy), reads=r_Sg, writes=r_Sp[nxt])
        scan_pass(1)
        S.emit(final_waits=[r_out[0]])
    return nc


_NC = None


def _fm(v):
    return np.ascontiguousarray(np.asarray(v, np.float32).reshape(8, 128).T)


def kernel(x, c, norm1_g, norm2_g, ada_w, ada_b, mlp_w1, mlp_w2, conv_w_in, conv_dw_w, conv_dw_b,
           conv_ln_g, conv_ln_b, conv_w_out, hgrn_w_in, hgrn_lb_logits, hgrn_gn_g, hgrn_w_out, final_g):
    global _NC
    f = lambda a: np.ascontiguousarray(np.asarray(a, np.float32))
    x = f(x); c = f(c)
    ada_w = f(ada_w); mlp_w1 = f(mlp_w1); mlp_w2 = f(mlp_w2)
    cin = f(conv_w_in)[0]; cout = f(conv_w_out)[0]; hout = f(hgrn_w_out)[0]
    hin = f(hgrn_w_in)[0]
    hin_sw = np.ascontiguousarray(np.concatenate(
        [hin[:, 0:1024], hin[:, 2048:3072], hin[:, 1024:2048], hin[:, 3072:]], axis=1))
    dw = f(conv_dw_w)[0]
    cm = np.zeros((128, NCM), np.float32)
    cm[:, CM_ID:CM_ID + 128] = np.eye(128, dtype=np.float32)
    s_ = np.arange(64)[:, None]; t_ = np.arange(64)[None, :]
    fe = np.zeros((128, 64), np.float32); fe[:64] = (s_ <= t_)
    fo = np.zeros((128, 64), np.float32); fo[64:] = (s_ <= t_)
    be = np.zeros((128, 64), np.float32); be[:64] = (s_ >= t_)
    bo = np.zeros((128, 64), np.float32); bo[64:] = (s_ >= t_)
    cm[:, CM_MASK:CM_MASK + 256] = np.concatenate([fe, fo, be, bo], axis=1)
    sm = np.ones(512, np.float32); sm[::64] = 0.0
    cm[:, CM_SCAN:CM_SCAN + 512] = sm[None, :]
    in_maps = []
    for r in range(8):
        b, half = r // 2, r % 2
        if half == 0:
            xs = x[b, 0:LTOK + 16]
        else:
            xs = x[b, ::-1][0:LTOK + 16]
        xTr = np.zeros((128, 8, XW), np.float32)
        xTr[:, :, 16:16 + LTOK + 16] = xs.T.reshape(8, 128, LTOK + 16).transpose(1, 0, 2)
        vv = np.zeros((128, NV), np.float32)
        for l in range(2):
            vv[:, V_N1G + 8 * l:V_N1G + 8 * l + 8] = _fm(norm1_g[l])
            vv[:, V_N2G + 8 * l:V_N2G + 8 * l + 8] = _fm(norm2_g[l])
            vv[:, V_LBL + 8 * l:V_LBL + 8 * l + 8] = _fm(hgrn_lb_logits[l])
            vv[:, V_ADAB + 48 * l:V_ADAB + 48 * l + 48] = np.asarray(ada_b[l], np.float32).reshape(48, 128).T
        vv[:, V_FG:V_FG + 8] = _fm(final_g)
        vv[:, V_DWB:V_DWB + 8] = _fm(conv_dw_b[0])
        vv[:, V_LNG:V_LNG + 8] = _fm(conv_ln_g[0])
        vv[:, V_LNB:V_LNB + 8] = _fm(conv_ln_b[0])
        vv[:, V_GNG:V_GNG + 8] = _fm(hgrn_gn_g[0])
        vv[:, V_FLAG:V_FLAG + 2] = np.array([0.0, 1.0] if half == 0 else [1.0, 0.0], np.float32)[None, :]
        vv[:, V_C:V_C + 8] = _fm(c[b])
        dwr = dw if half == 0 else dw[::-1]
        vv[:, V_DW:V_DW + 248] = dwr.T.reshape(8, 128, 31).transpose(1, 0, 2).reshape(128, 248)
        in_maps.append({
            "xT": xTr, "vecs": vv, "cmat": cm, "ada_w": ada_w, "mlp_w1": mlp_w1, "mlp_w2": mlp_w2,
            "conv_w_in": cin, "conv_w_out": cout, "hgrn_w_in": hin if half == 0 else hin_sw, "hgrn_w_out": hout,
        })
    if _NC is None:
        _NC = build_nc()
    res = run_bass_kernel_spmd(_NC, in_maps, core_ids=list(range(8)))
    out = np.empty((4, 8192, 1024), np.float32)
    for r in range(8):
        b, half = r // 2, r % 2
        o = res.results[r]["outT"]
        tok = o.transpose(2, 1, 0).reshape(LTOK, 1024)
        if half == 0:
            out[b, 0:LTOK] = tok
        else:
            out[b, LTOK:] = tok[::-1]
    return out
```

```python
import contextlib
import numpy as np
import concourse.bass as bass
import concourse.mybir as mybir
from concourse.bass_utils import run_bass_kernel_spmd

F32 = mybir.dt.float32
BF16 = mybir.dt.bfloat16
F32R = mybir.dt.float32r
AF = mybir.ActivationFunctionType
ALU = mybir.AluOpType

ENGS = ["pe", "act", "dve", "pool", "sp"]
D = 1024
NT = 8
T = 512
TE = T + 30
LTOK = NT * T
XW = LTOK + 32
EPS = 1e-6
USE_PE_CONV = True

V_N1G, V_N2G, V_FG, V_DWB, V_LNG, V_LNB, V_LBL, V_GNG, V_FLAG, V_C, V_ADAB, V_DW = (
    0, 16, 32, 40, 48, 56, 64, 80, 88, 90, 98, 194)
NV = 194 + 248
CM_ID, CM_MASK, CM_SCAN, NCM = 0, 128, 384, 896

G_CIN, G_COUT, G_W1_0, G_W2_0, G_HIN, G_HOUT, G_W1_1, G_W2_1, NG = 0, 4, 6, 14, 22, 32, 34, 42, 50


class Res:
    __slots__ = ("name", "writer", "readers", "dma_sem", "dma_cnt")

    def __init__(self, name):
        self.name = name
        self.writer = None
        self.readers = []
        self.dma_sem = None
        self.dma_cnt = 0


class Sched:
    def __init__(self, nc, stack):
        self.nc = nc
        self.stack = stack
        self.ops = {e: [] for e in ENGS}
        self.eng_sem = {e: stack.enter_context(nc.semaphore("s_" + e)) for e in ENGS}
        self.n_res = 0
        self.dma_res = []
        self.bar = None

    def barrier(self, fn):
        deps = []
        for e in ENGS:
            for i in range(len(self.ops[e]) - 1, -1, -1):
                if self.ops[e][i]["dma"] is None:
                    deps.append(("eng", e, i))
                    break
        for r in self.dma_res:
            deps.append(("dma", r, r.dma_cnt))
        idx = len(self.ops["pool"])
        self.ops["pool"].append(dict(fn=fn, deps=deps, signal=False, dma=None))
        self.bar = ("eng", "pool", idx)

    def res(self, name=None):
        self.n_res += 1
        return Res(name or ("r%d" % self.n_res))

    def op(self, eng, fn, reads=(), writes=(), dma=False, nodep=False):
        deps = []
        for r in reads:
            if r.writer is not None:
                deps.append(r.writer)
        for r in writes:
            if nodep:
                continue
            if r.writer is not None:
                deps.append(r.writer)
            deps.extend(r.readers)
        if self.bar is not None:
            deps.append(self.bar)
        idx = len(self.ops[eng])
        rec = dict(fn=fn, deps=deps, signal=False, dma=None)
        if dma:
            r0 = writes[0]
            if r0.dma_sem is None:
                r0.dma_sem = self.stack.enter_context(self.nc.semaphore("d%d" % self.n_res + r0.name))
                self.n_res += 1
                self.dma_res.append(r0)
            r0.dma_cnt += 1
            rec["dma"] = (r0, r0.dma_cnt)
            me = ("dma", r0, r0.dma_cnt)
        else:
            me = ("eng", eng, idx)
        self.ops[eng].append(rec)
        for r in reads:
            r.readers.append(me)
        for r in writes:
            r.writer = me
            r.readers = []
        return me

    def emit(self, final_waits=()):
        nc = self.nc
        for e in ENGS:
            for i, rec in enumerate(self.ops[e]):
                for d in rec["deps"]:
                    if d[0] == "eng":
                        _, de, di = d
                        if de == "pe" and e == "pe":
                            continue
                        self.ops[de][di]["signal"] = True
        cnt = {}
        for e in ENGS:
            c = 0
            for i, rec in enumerate(self.ops[e]):
                if rec["signal"]:
                    c += 1
                    cnt[(e, i)] = c
        handles = {"pe": "tensor", "act": "scalar", "dve": "vector", "pool": "gpsimd", "sp": "sync"}
        with nc.Block() as block:
            for e in ENGS:
                def body(eng_h, e=e):
                    waited = {}
                    for i, rec in enumerate(self.ops[e]):
                        need = {}
                        for d in rec["deps"]:
                            if d[0] == "eng":
                                _, de, di = d
                                if de == "pe" and e == "pe":
                                    continue
                                key = ("e", de)
                                val = cnt[(de, di)]
                                sem = self.eng_sem[de]
                            else:
                                _, r, c = d
                                key = ("d", id(r))
                                val = 16 * c
                                sem = r.dma_sem
                            if need.get(key, (0, None))[0] < val:
                                need[key] = (val, sem)
                        for key, (val, sem) in need.items():
                            if waited.get(key, 0) >= val:
                                continue
                            waited[key] = val
                            eng_h.wait_ge(sem, val)
                        inst = rec["fn"](eng_h)
                        if rec["dma"] is not None:
                            inst.then_inc(rec["dma"][0].dma_sem, 16)
                        elif rec["signal"]:
                            inst.then_inc(self.eng_sem[e], 1)
                    if e == "sp":
                        for r in final_waits:
                            eng_h.wait_ge(r.dma_sem, 16 * r.dma_cnt)
                getattr(block, handles[e])(body)


def build_nc(debug=False):
    nc = bass.Bass("TRN2", target_bir_lowering=False)

    def din(name, shape):
        return nc.dram_tensor(name, shape, F32, kind="ExternalInput").ap()

    xT = din("xT", [128, 8, XW])
    vecs_d = din("vecs", [128, NV])
    cmat_d = din("cmat", [128, NCM])
    ada_w = din("ada_w", [2, D, 6 * D])
    w1_d = din("mlp_w1", [2, D, 4 * D])
    w2_d = din("mlp_w2", [2, 4 * D, D])
    cin_d = din("conv_w_in", [D, 2 * D])
    cout_d = din("conv_w_out", [D, D])
    hin_d = din("hgrn_w_in", [D, 5 * D])
    hout_d = din("hgrn_w_out", [D, D])
    outT = nc.dram_tensor("outT", [128, 8, LTOK], F32, kind="ExternalOutput").ap()

    wsc = nc.dram_tensor("wsc", [NG, 128, 4096], BF16).ap()
    x1s = nc.dram_tensor("x1s", [128, 8, LTOK], F32).ap()
    ofw = nc.dram_tensor("ofw", [128, 8, LTOK], F32).ap()
    qsc = nc.dram_tensor("qsc", [128, 8, LTOK], BF16).ap()
    vsc = nc.dram_tensor("vsc", [NT, 128, 4096], BF16).ap()
    gsc = nc.dram_tensor("gsc", [128, 8, LTOK], BF16).ap()
    zsc = nc.dram_tensor("zsc", [128, 8, LTOK], F32).ap()
    cc_in = nc.dram_tensor("cc_in", [128, 1024], F32).ap()
    cc_out = nc.dram_tensor("cc_out", [256, 1024], F32).ap()

    with contextlib.ExitStack() as st:
        S = Sched(nc, st)

        def sb(name, shape, dt):
            return st.enter_context(nc.sbuf_tensor("sb_" + name, shape, dt))

        NSLOT = 4
        slots = [sb("slot%d" % i, [128, 4096], BF16) for i in range(NSLOT)]
        r_slots = [S.res("slot%d" % i) for i in range(NSLOT)]
        r_slots_sw = [S.res("slotsw%d" % i) for i in range(NSLOT)]
        xt = [sb("xt%d" % i, [128, 8, TE], F32) for i in range(2)]
        r_xt = [S.res("xt%d" % i) for i in range(2)]
        h = sb("h", [128, 8, TE], BF16)
        r_h = [S.res("h%d" % c) for c in range(8)]
        sqb = [sb("sqb%d" % i, [128, TE], BF16) for i in range(2)]
        r_sqb = [S.res("sqb%d" % i) for i in range(2)]
        vA = sb("vA", [128, TE], F32); r_vA = S.res("vA")
        vB = sb("vB", [128, TE], F32); r_vB = S.res("vB")
        vC = sb("vC", [128, TE], F32); r_vC = S.res("vC")
        tmpn = [sb("tmpn%d" % i, [128, TE], F32) for i in range(2)]
        r_tmpn = [S.res("tmpn%d" % i) for i in range(2)]
        arena = sb("arena", [128, 16384], BF16)
        r_ar = [S.res("ar%d" % i) for i in range(32)]
        rt = [sb("rt%d" % i, [128, T], BF16) for i in range(2)]
        r_rt = [S.res("rt%d" % i) for i in range(2)]
        vecs = sb("vecs", [128, NV], F32); r_vecs = S.res("vecs")
        cmat = sb("cmat", [128, NCM], F32); r_cmat = S.res("cmat")
        ident = sb("ident", [128, 128], BF16)
        ones = sb("ones", [128, 128], BF16)
        r_const = S.res("const")
        mod = sb("mod", [128, 96], F32); r_mod = S.res("mod")
        cond = sb("cond", [128, 8], F32); r_cond = S.res("cond")
        cond_bf = sb("cond_bf", [128, 8], BF16)
        gv = sb("gv", [128, 2, 16], F32)
        lbv = sb("lbv", [128, 3, 8], F32)
        r_gv = S.res("gv")
        un = sb("un", [128, 17408], BF16)
        u = un[:, 0:4336].rearrange("p (c t) -> p c t", c=8); r_u = [S.res("u%d" % c) for c in range(8)]
        a_sb = un[:, 4336:6504].rearrange("p (c t) -> p c t", c=4); r_asb = [S.res("asb%d" % c) for c in range(4)]
        sgt = [un[:, 6504 + 1084 * i: 6504 + 1084 * (i + 1)].bitcast(F32) for i in range(2)]
        r_sgt = [S.res("sgt%d" % i) for i in range(2)]
        dg = [un[:, 8704 + 3968 * i: 8704 + 3968 * (i + 1)].rearrange("p (a b) -> p a b", a=31) for i in range(2)]
        r_dg = [S.res("dg%d" % i) for i in range(2)]
        qs = un[:, 0:4096].rearrange("p (c t) -> p c t", c=8); r_qs = [S.res("qs%d" % c) for c in range(8)]
        kT = un[:, 4096:8192].rearrange("p (c t) -> p c t", c=8); r_kT = [S.res("kT%d" % c) for c in range(8)]
        ktm = un[:, 8192:12288].rearrange("p (a b c) -> p a b c", a=4, b=8); r_ktm = [S.res("ktm%d" % c) for c in range(4)]
        vtm = un[:, 12288:16384].rearrange("p (a b) -> p a b", a=4); r_vtm = [S.res("vtm%d" % c) for c in range(4)]
        NGT, NSET = 4, 3
        gtb = [[sb("gt%d_%d" % (q, i), [128, T], F32) for i in range(NGT)] for q in range(NSET)]
        r_gtb = [[S.res("gt%d_%d" % (q, i)) for i in range(NGT)] for q in range(NSET)]
        Bcol = sb("Bcol", [128, 8, 8, 2], F32)
        Sst = sb("Sst", [128, 8, 128], F32); r_Sg = [S.res("S%d" % c) for c in range(8)]
        Sp = [sb("Sp%d" % i, [128, 8, 128], BF16) for i in range(2)]
        r_Sp = [[S.res("Sp%d_%d" % (i, g)) for g in range(2)] for i in range(2)]
        qh = sb("qh", [128, 8, T], BF16); r_qh = [S.res("qh%d" % c) for c in range(8)]
        kh = [sb("kh%d" % i, [128, T], BF16) for i in range(2)]; r_kh = [S.res("kh%d" % i) for i in range(2)]
        scT = [sb("scT%d" % i, [128, 8, 64], BF16) for i in range(2)]
        r_scT = [S.res("scT%d" % i) for i in range(2)]
        EA = sb("EA", [128, 8, 8], F32); EB = sb("EB", [128, 8, 8], F32); EC = sb("EC", [128, 8, 8], F32)
        r_E = [S.res("E%d" % c) for c in range(8)]
        ED = sb("ED", [128, 8, 8], F32)

        banks = [st.enter_context(nc.psum_tensor("pb%d" % i, [128, 512], F32)) for i in range(8)]
        r_banks = [S.res("pb%d" % i) for i in range(8)]
        state = dict(bank=0, slot=0)

        def nb():
            i = state["bank"]
            state["bank"] = (i + 1) % 8
            return banks[i], r_banks[i]

        _mats = [(G_CIN, 4), (G_COUT, 2), (G_W1_0, 8), (G_W2_0, 8), (G_HIN, 10), (G_HOUT, 2), (G_W1_1, 8), (G_W2_1, 8)]
        r_wsc = [None] * NG
        for (g0, n) in _mats:
            rm = S.res("wsc%d" % g0)
            for g in range(g0, g0 + n):
                r_wsc[g] = rm
        r_x1s = [S.res("x1s")] * NT
        r_ofw = [S.res("ofw")] * NT
        r_out = [S.res("out")] * NT
        r_ccin = S.res("ccin"); r_ccout = S.res("ccout")
        r_gsc = [S.res("gsc")] * NT
        r_zsc = [S.res("zsc")] * NT
        r_stgdma = [S.res("stgdma%d" % q) for q in range(3)]
        r_qsc = [S.res("qsc")] * NT
        r_vsc = [S.res("vsc")] * NT

        def wg(g):
            i = state["slot"]
            state["slot"] = (i + 1) % NSLOT
            S.op("sp", lambda e, i=i, g=g: e.dma_start(out=slots[i][:], in_=wsc[g]),
                 reads=[r_wsc[g]], writes=[r_slots[i]], dma=True)
            return slots[i], r_slots[i]

        def vcol(base, c, n=1):
            return vecs[:, base + c: base + c + n]

        S.op("sp", lambda e: e.dma_start(out=vecs[:], in_=vecs_d), writes=[r_vecs], dma=True)
        S.op("sp", lambda e: e.dma_start(out=cmat[:], in_=cmat_d), writes=[r_cmat], dma=True)
        S.op("dve", lambda e: e.tensor_copy(out=ident[:], in_=cmat[:, CM_ID:CM_ID + 128]), reads=[r_cmat], writes=[r_const])
        S.op("dve", lambda e: e.memset(ones[:], 1.0), writes=[r_const])
        masks = cmat[:, CM_MASK:CM_MASK + 256].rearrange("p (a b) -> p a b", a=4)
        scanmask = cmat[:, CM_SCAN:CM_SCAN + 512]

        def conv_k1024(g, src, col0):
            S.op("pool", lambda e: e.dma_start(
                out=wsc[g].rearrange("p (k n) -> p k n", k=8),
                in_=src[:, col0:col0 + 512].rearrange("(k p) n -> p k n", p=128)),
                writes=[r_wsc[g]], dma=True, nodep=True)

        def conv_w2(g, src, row0):
            S.op("pool", lambda e: e.dma_start(
                out=wsc[g].rearrange("p (k n) -> p k n", k=4),
                in_=src[row0:row0 + 512, :].rearrange("(k p) n -> p k n", p=128)),
                writes=[r_wsc[g]], dma=True, nodep=True)

        def convert_layer0():
            for j in range(4):
                conv_k1024(G_CIN + j, cin_d, 512 * j)
            for j in range(2):
                conv_k1024(G_COUT + j, cout_d, 512 * j)
            for j in range(8):
                conv_k1024(G_W1_0 + j, w1_d[0], 512 * j)
            for j in range(8):
                conv_w2(G_W2_0 + j, w2_d[0], 512 * j)

        l1_convs = ([lambda j=j: conv_k1024(G_HIN + j, hin_d, 512 * j) for j in range(10)]
                    + [lambda j=j: conv_k1024(G_HOUT + j, hout_d, 512 * j) for j in range(2)]
                    + [lambda j=j: conv_k1024(G_W1_1 + j, w1_d[1], 512 * j) for j in range(8)]
                    + [lambda j=j: conv_w2(G_W2_1 + j, w2_d[1], 512 * j) for j in range(8)])

        S.op("sp", lambda e: e.dma_start(out=xt[0][:], in_=xT[:, :, 1:1 + TE]), writes=[r_xt[0]], dma=True)

        S.op("act", lambda e: e.activation(out=cond_bf[:], in_=vcol(V_C, 0, 8), func=AF.Silu), reads=[r_vecs], writes=[r_cond])
        def ada_batch(l, k):
            mbank, r_mbank = nb()
            for g2 in range(2):
                grp = 2 * k + g2
                i = state["slot"]
                state["slot"] = (i + 1) % NSLOT
                sl3 = slots[i][:].rearrange("p (k n) -> p k n", k=8)
                r_sl = r_slots[i]
                S.op("pool", lambda e, grp=grp, sl3=sl3: e.dma_start(
                    out=sl3, in_=ada_w[l][:, grp * 512:(grp + 1) * 512].rearrange("(k p) n -> p k n", p=128)),
                    writes=[r_slots_sw[i], r_sl], dma=True)
                for j in range(4):
                    ch = g2 * 4 + j
                    for kc in range(8):
                        S.op("pe", lambda e, sl3=sl3, j=j, kc=kc, ch=ch: e.matmul(
                            mbank[:, ch:ch + 1], lhsT=sl3[:, kc, j * 128:(j + 1) * 128], rhs=cond_bf[:, kc:kc + 1],
                            start=(kc == 0), stop=(kc == 7)),
                            reads=[r_sl, r_cond], writes=[r_mbank])
            c0 = l * 48 + k * 8
            S.op("dve", lambda e: e.tensor_tensor(out=mod[:, c0:c0 + 8], in0=mbank[:, 0:8], in1=vcol(V_ADAB, c0, 8), op=ALU.add),
                 reads=[r_mbank, r_vecs], writes=[r_mod])
            if k == 1:
                S.op("dve", lambda e: e.scalar_tensor_tensor(
                    out=gv[:, l, 0:8], in0=mod[:, c0:c0 + 8], scalar=1.0, in1=vcol(V_N1G, l * 8, 8),
                    op0=ALU.add, op1=ALU.mult), reads=[r_mod, r_vecs], writes=[r_gv])
            if k == 4:
                S.op("dve", lambda e: e.scalar_tensor_tensor(
                    out=gv[:, l, 8:16], in0=mod[:, c0:c0 + 8], scalar=1.0, in1=vcol(V_N2G, l * 8, 8),
                    op0=ALU.add, op1=ALU.mult), reads=[r_mod, r_vecs], writes=[r_gv])

        for k in range(6):
            ada_batch(0, k)
        convert_layer0()
        S.op("dve", lambda e: e.tensor_tensor(out=lbv[:, 2, :], in0=vcol(V_LBL, 8, 8), in1=vcol(V_LBL, 0, 8), op=ALU.subtract),
             reads=[r_vecs], writes=[r_gv])
        S.op("act", lambda e: e.activation(out=lbv[:, 0, :], in_=lbv[:, 2, :], func=AF.Sigmoid), reads=[r_gv], writes=[r_gv])
        S.op("act", lambda e: e.activation(out=lbv[:, 1, :], in_=lbv[:, 0, :], func=AF.Ln, scale=-1.0, bias=1.0),
             reads=[r_gv], writes=[r_gv])

        def MOD(l, k, c):
            return mod[:, l * 48 + k * 8 + c: l * 48 + k * 8 + c + 1]

        def rms_norm(xsrc, r_x, W, gain, shift, out_fn, r_out_fn, out_final=False):
            segs = [(0, W)] if W <= 512 else [(0, W // 2), (W // 2, W)]
            pbs = [nb() for _ in segs]
            for c in range(8):
                k = c % 2
                xa = xsrc(c)
                S.op("act", lambda e, xa=xa, k=k: e.activation(out=sqb[k][:, 0:W], in_=xa, func=AF.Square),
                     reads=[r_x], writes=[r_sqb[k]])
                for (s0, s1), (pb, r_pb) in zip(segs, pbs):
                    S.op("pe", lambda e, c=c, k=k, s0=s0, s1=s1, pb=pb: e.matmul(
                        pb[:, 0:s1 - s0], lhsT=ones[:], rhs=sqb[k][:, s0:s1], start=(c == 0), stop=(c == 7)),
                        reads=[r_sqb[k], r_const], writes=[r_pb])
            for (s0, s1), (pb, r_pb) in zip(segs, pbs):
                S.op("act", lambda e, s0=s0, s1=s1, pb=pb: e.activation(
                    out=vA[:, s0:s1], in_=pb[:, 0:s1 - s0], func=AF.Ln, scale=1.0 / D, bias=EPS),
                    reads=[r_pb], writes=[r_vA])
            S.op("act", lambda e: e.activation(out=vB[:, 0:W], in_=vA[:, 0:W], func=AF.Exp, scale=-0.5),
                 reads=[r_vA], writes=[r_vB])
            for c in range(8):
                k = c % 2
                xa = xsrc(c); ga = gain(c); oa = out_fn(c); r_o = r_out_fn(c)
                if out_final:
                    S.op("dve", lambda e, xa=xa, ga=ga, oa=oa: e.scalar_tensor_tensor(
                        out=oa, in0=xa, scalar=ga, in1=vB[:, 0:W], op0=ALU.mult, op1=ALU.mult),
                        reads=[r_x, r_vB, r_gv, r_vecs], writes=[r_o])
                else:
                    sa = shift(c)
                    S.op("dve", lambda e, xa=xa, ga=ga, k=k: e.scalar_tensor_tensor(
                        out=tmpn[k][:, 0:W], in0=xa, scalar=ga, in1=vB[:, 0:W], op0=ALU.mult, op1=ALU.mult),
                        reads=[r_x, r_vB, r_gv], writes=[r_tmpn[k]])
                    S.op("act", lambda e, oa=oa, sa=sa, k=k: e.activation(
                        out=oa, in_=tmpn[k][:, 0:W], func=AF.Identity, scale=1.0, bias=sa),
                        reads=[r_tmpn[k], r_mod], writes=[r_o])

        def proj_fm(slot, r_slot, j, rhs_fn, r_rhs, W):
            segs = [(0, W)] if W <= 512 else [(0, W // 2), (W // 2, W)]
            outs = []
            for (s0, s1) in segs:
                pb, r_pb = nb()
                for kc in range(8):
                    S.op("pe", lambda e, kc=kc, s0=s0, s1=s1, pb=pb: e.matmul(
                        pb[:, 0:s1 - s0], lhsT=slot.rearrange("p (k n) -> p k n", k=8)[:, kc, j * 128:(j + 1) * 128],
                        rhs=rhs_fn(kc)[:, s0:s1], start=(kc == 0), stop=(kc == 7)),
                        reads=[r_slot] + r_rhs, writes=[r_pb])
                outs.append((pb, r_pb, s0, s1))
            return outs

        def mlp(l, xb, r_x, g_w1, g_w2, mid_hook=None, group_hook=None):
            xc = lambda c: xt[xb][:, c, 15:15 + T]
            rms_norm(xc, r_x, T, lambda c: gv[:, l, 8 + c:9 + c], lambda c: MOD(l, 3, c),
                     lambda c: h[:, c, 0:T], lambda c: r_h[c])
            aT = lambda fc: arena[:, fc * 512:(fc + 1) * 512]
            for g in range(8):
                slot, r_slot = wg(g_w1 + g)
                for j in range(4):
                    fc = 4 * g + j
                    (pb, r_pb, _, _), = proj_fm(slot[:], r_slot, j, lambda kc: h[:, kc, 0:T], r_h, T)
                    k = fc % 2
                    S.op("act", lambda e, pb=pb, k=k: e.activation(out=rt[k][:], in_=pb[:], func=AF.Relu),
                         reads=[r_pb], writes=[r_rt[k]])
                    S.op("dve" if group_hook is not None else "pool",
                         lambda e, fc=fc, k=k: e.tensor_tensor(out=aT(fc), in0=rt[k][:], in1=rt[k][:], op=ALU.mult),
                         reads=[r_rt[k]], writes=[r_ar[fc]])
                if group_hook is not None:
                    group_hook(g)
            if mid_hook is not None:
                mid_hook()
            for g in range(8):
                slot, r_slot = wg(g_w2 + g)
                s3 = slot[:].rearrange("p (k n) -> p k n", k=4)
                for oc in range(8):
                    for fcl in range(4):
                        fc = 4 * g + fcl
                        S.op("pe", lambda e, s3=s3, oc=oc, fcl=fcl, fc=fc, g=g: e.matmul(
                            banks[oc][:], lhsT=s3[:, fcl, oc * 128:(oc + 1) * 128], rhs=aT(fc),
                            start=(g == 0 and fcl == 0), stop=(g == 7 and fcl == 3)),
                            reads=[r_slot, r_ar[fc]], writes=[r_banks[oc]])
            for oc in range(8):
                S.op("dve", lambda e, oc=oc: e.scalar_tensor_tensor(
                    out=xc(oc), in0=banks[oc][:], scalar=MOD(l, 5, oc), in1=xc(oc), op0=ALU.mult, op1=ALU.add),
                    reads=[r_banks[oc], r_mod, r_x], writes=[r_x])

        def wout_residual(l, xb, r_x, g_wout):
            xc = lambda c: xt[xb][:, c, 15:15 + T]
            for half in range(2):
                slot, r_slot = wg(g_wout + half)
                for j in range(4):
                    oc = 4 * half + j
                    (pb, r_pb, _, _), = proj_fm(slot[:], r_slot, j, lambda kc: h[:, kc, 0:T], r_h, T)
                    S.op("dve", lambda e, pb=pb, oc=oc: e.scalar_tensor_tensor(
                        out=xc(oc), in0=pb[:], scalar=MOD(l, 2, oc), in1=xc(oc), op0=ALU.mult, op1=ALU.add),
                        reads=[r_pb, r_mod, r_x], writes=[r_x])

        def load_x0(i):
            b = i % 2
            S.op("sp", lambda e: e.dma_start(out=xt[b][:], in_=xT[:, :, 512 * i + 1: 512 * i + 1 + TE]),
                 writes=[r_xt[b]], dma=True)

        y32 = arena[:, 0:8192].bitcast(F32).rearrange("p (c t) -> p c t", c=8)
        ybf = arena[:, 8192:12288].rearrange("p (c t) -> p c t", c=8)
        ysq = arena[:, 12288:16384].rearrange("p (c t) -> p c t", c=8)
        r_y32 = lambda c: [r_ar[2 * c], r_ar[2 * c + 1]]

        def l0_mixer(i):
            b = i % 2
            r_x = r_xt[b]
            rms_norm(lambda c: xt[b][:, c, :], r_x, TE, lambda c: gv[:, 0, c:c + 1], lambda c: MOD(0, 0, c),
                     lambda c: h[:, c, :], lambda c: r_h[c])
            for half in range(2):
                slA, r_slA = wg(G_CIN + half)
                slG, r_slG = wg(G_CIN + 2 + half)
                for j in range(4):
                    for (pb, r_pb, s0, s1) in proj_fm(slA[:], r_slA, j, lambda kc: h[:, kc, :], r_h, TE):
                        S.op("act", lambda e, pb=pb, s0=s0, s1=s1, j=j: e.activation(
                            out=a_sb[:, j, s0:s1], in_=pb[:, 0:s1 - s0], func=AF.Copy), reads=[r_pb], writes=[r_asb[j]])
                for j in range(4):
                    c = 4 * half + j
                    k = c % 2
                    for (pb, r_pb, s0, s1) in proj_fm(slG[:], r_slG, j, lambda kc: h[:, kc, :], r_h, TE):
                        S.op("act", lambda e, pb=pb, s0=s0, s1=s1, k=k: e.activation(
                            out=sgt[k][:, s0:s1], in_=pb[:, 0:s1 - s0], func=AF.Sigmoid), reads=[r_pb], writes=[r_sgt[k]])
                    S.op("pool", lambda e, c=c, j=j, k=k: e.tensor_tensor(
                        out=u[:, c, :], in0=a_sb[:, j, :], in1=sgt[k], op=ALU.mult),
                        reads=[r_asb[j], r_sgt[k]], writes=[r_u[c]])
                    if i == 0:
                        S.op("pool", lambda e, c=c: e.memset(u[:, c, 0:15], 0.0), writes=[r_u[c]])
            if USE_PE_CONV:
                def build_diag(c):
                    par = c % 2
                    S.op("dve", lambda e: e.tensor_tensor(
                        out=dg[par], in0=ident[:].unsqueeze(1).broadcast_to([128, 31, 128]),
                        in1=vecs[:, V_DW + c * 31: V_DW + c * 31 + 31].unsqueeze(2).broadcast_to([128, 31, 128]), op=ALU.mult),
                        reads=[r_const, r_vecs], writes=[r_dg[par]])

                build_diag(0)
                build_diag(1)
                for c in range(8):
                    par = c % 2
                    pb, r_pb = nb()
                    for tap in range(31):
                        S.op("pe", lambda e, pb=pb, par=par, tap=tap, c=c: e.matmul(
                            pb[:], lhsT=dg[par][:, tap, :], rhs=u[:, c, tap:tap + T], start=(tap == 0), stop=(tap == 30)),
                            reads=[r_dg[par], r_u[c]], writes=[r_pb])
                    if c + 2 < 8:
                        build_diag(c + 2)
                    S.op("act", lambda e, pb=pb, c=c: e.activation(out=y32[:, c, :], in_=pb[:], func=AF.Identity, scale=1.0, bias=vcol(V_DWB, c)),
                         reads=[r_pb, r_vecs], writes=r_y32(c))
                    S.op("act", lambda e, pb=pb, c=c: e.activation(out=ybf[:, c, :], in_=pb[:], func=AF.Identity, scale=1.0, bias=vcol(V_DWB, c)),
                         reads=[r_pb, r_vecs], writes=[r_ar[16 + c]])
                    S.op("pool", lambda e, c=c: e.tensor_tensor(out=ysq[:, c, :], in0=y32[:, c, :], in1=y32[:, c, :], op=ALU.mult),
                         reads=r_y32(c), writes=[r_ar[24 + c]])
            else:
                for tap in range(31):
                    for c in range(8):
                        if tap == 0:
                            S.op("dve", lambda e, c=c: e.tensor_scalar(
                                out=y32[:, c, :], in0=u[:, c, 0:T], scalar1=vcol(V_DW, c * 31), scalar2=vcol(V_DWB, c),
                                op0=ALU.mult, op1=ALU.add), reads=[r_u[c], r_vecs], writes=r_y32(c))
                        else:
                            S.op("dve", lambda e, c=c, tap=tap: e.scalar_tensor_tensor(
                                out=y32[:, c, :], in0=u[:, c, tap:tap + T], scalar=vcol(V_DW, c * 31 + tap), in1=y32[:, c, :],
                                op0=ALU.mult, op1=ALU.add), reads=[r_u[c]] + r_y32(c), writes=r_y32(c))
                for c in range(8):
                    S.op("act", lambda e, c=c: e.activation(out=ybf[:, c, :], in_=y32[:, c, :], func=AF.Copy),
                         reads=r_y32(c), writes=[r_ar[16 + c]])
                    S.op("pool", lambda e, c=c: e.tensor_tensor(out=ysq[:, c, :], in0=y32[:, c, :], in1=y32[:, c, :], op=ALU.mult),
                         reads=r_y32(c), writes=[r_ar[24 + c]])
            pbm, r_pbm = nb()
            pbv, r_pbv = nb()
            for c in range(8):
                S.op("pe", lambda e, c=c, pbm=pbm: e.matmul(pbm[:], lhsT=ones[:], rhs=ybf[:, c, :], start=(c == 0), stop=(c == 7)),
                     reads=[r_ar[16 + c], r_const], writes=[r_pbm])
                S.op("pe", lambda e, c=c, pbv=pbv: e.matmul(pbv[:], lhsT=ones[:], rhs=ysq[:, c, :], start=(c == 0), stop=(c == 7)),
                     reads=[r_ar[24 + c], r_const], writes=[r_pbv])
            if 1 <= i <= 6:
                ada_batch(1, i - 1)
            S.op("act", lambda e, pbm=pbm: e.activation(out=vA[:, 0:T], in_=pbm[:], func=AF.Copy, scale=1.0 / D), reads=[r_pbm], writes=[r_vA])
            S.op("dve", lambda e: e.tensor_tensor(out=vC[:, 0:T], in0=vA[:, 0:T], in1=vA[:, 0:T], op=ALU.mult), reads=[r_vA], writes=[r_vC])
            S.op("dve", lambda e, pbv=pbv: e.scalar_tensor_tensor(out=vC[:, 0:T], in0=pbv[:], scalar=1.0 / D, in1=vC[:, 0:T],
                                                          op0=ALU.mult, op1=ALU.subtract), reads=[r_pbv, r_vC], writes=[r_vC])
            S.op("act", lambda e: e.activation(out=vC[:, 0:T], in_=vC[:, 0:T], func=AF.Ln, scale=1.0, bias=EPS), reads=[r_vC], writes=[r_vC])
            S.op("act", lambda e: e.activation(out=vB[:, 0:T], in_=vC[:, 0:T], func=AF.Exp, scale=-0.5), reads=[r_vC], writes=[r_vB])
            for c in range(8):
                k = c % 2
                S.op("dve", lambda e, c=c, k=k: e.tensor_tensor(out=tmpn[k][:, 0:T], in0=y32[:, c, :], in1=vA[:, 0:T], op=ALU.subtract),
                     reads=r_y32(c) + [r_vA], writes=[r_tmpn[k]])
                S.op("dve", lambda e, k=k: e.tensor_tensor(out=tmpn[k][:, 0:T], in0=tmpn[k][:, 0:T], in1=vB[:, 0:T], op=ALU.mult),
                     reads=[r_tmpn[k], r_vB], writes=[r_tmpn[k]])
                S.op("act", lambda e, c=c, k=k: e.activation(out=h[:, c, 0:T], in_=tmpn[k][:, 0:T], func=AF.Silu,
                                                              scale=vcol(V_LNG, c), bias=vcol(V_LNB, c)),
                     reads=[r_tmpn[k], r_vecs], writes=[r_h[c]])
            wout_residual(0, b, r_x, G_COUT)

        def l0_mlp(i):
            b = i % 2
            r_x = r_xt[b]
            hook = None
            if i >= 1:
                hook = lambda i=i: [f() for f in l1_convs[4 * (i - 1):4 * i]]
            mlp(0, b, r_x, G_W1_0, G_W2_0, mid_hook=hook)
            S.op("pool", lambda e, i=i, b=b: e.dma_start(out=x1s[:, :, T * i:T * (i + 1)], in_=xt[b][:, :, 15:15 + T]),
                 reads=[r_x], writes=[r_x1s[i]], dma=True, nodep=True)

        load_x0(1)
        l0_mixer(0)
        l0_mixer(1)
        l0_mlp(0)
        load_x0(2)
        l0_mlp(1)
        load_x0(3)
        for i in range(2, NT):
            l0_mixer(i)
            l0_mlp(i)
            if i + 2 < NT:
                load_x0(i + 2)

        S.barrier(lambda e: e.memset(ED[:], 0.0))
        S.op("dve", lambda e: e.memset(Sst[:], 0.0), writes=r_Sg)
        S.op("dve", lambda e: e.memset(Sp[0][:], 0.0), writes=r_Sp[0])
        state["cn"] = 0

        def load_x1(i, b):
            S.op("sp", lambda e: e.dma_start(out=xt[b][:, :, 15:15 + T], in_=x1s[:, :, T * i:T * (i + 1)]),
                 reads=[r_x1s[i]], writes=[r_xt[b]], dma=True)

        o_sb = arena[:, 0:8192].bitcast(F32).rearrange("p (c t) -> p c t", c=8)
        of_sb = arena[:, 8192:16384].bitcast(F32).rearrange("p (c t) -> p c t", c=8)
        r_osb = r_ar[0:16]
        r_ofsb = r_ar[16:32]

        def gate_pipeline(fwd, zslots, hk, ztile=None):
            tstate = {}

            zstate = {}

            def Z(hd):
                if ztile is not None:
                    g_e = gtb[hd % NSET][0]
                    r_e = r_gtb[hd % NSET][0]
                    S.op("act", lambda e: e.dma_start(out=g_e[:], in_=zsc[:, hd, T * ztile:T * (ztile + 1)]),
                         reads=[r_zsc[ztile]], writes=[r_e], dma=True)
                    zstate[hd] = (g_e, r_e)
                    return
                slot, r_slot = zslots[hd // 4]
                (pz, r_pz, _, _), = proj_fm(slot[:], r_slot, hd % 4, hk, r_h, T)
                zstate[hd] = (pz, r_pz)

            def A1(hd):
                g_e, g_w, g_lf, g_B = gtb[hd % NSET]
                r_e, r_w, r_lf, r_B = r_gtb[hd % NSET]
                pz, r_pz = zstate[hd]
                lb_c = lbv[:, 0, hd:hd + 1]
                S.op("act", lambda e: e.activation(out=g_e[:], in_=pz[:], func=AF.Exp), reads=[r_pz, r_e], writes=[r_e])
                S.op("act", lambda e: e.activation(out=g_w[:], in_=g_e[:], func=AF.Ln, scale=1.0, bias=1.0), reads=[r_e], writes=[r_w])
                S.op("act", lambda e: e.activation(out=g_lf[:], in_=g_e[:], func=AF.Ln, scale=1.0, bias=lb_c), reads=[r_e, r_gv], writes=[r_lf])

            def D1(hd):
                g_e, g_w, g_lf, g_B = gtb[hd % NSET]
                r_e, r_w, r_lf, r_B = r_gtb[hd % NSET]
                B3 = g_B[:].rearrange("p (c t) -> p c t", t=64)
                S.op("pool", lambda e: e.tensor_tensor(out=g_lf[:], in0=g_lf[:], in1=g_w[:], op=ALU.subtract), reads=[r_lf, r_w], writes=[r_lf])
                S.op("dve", lambda e: e.tensor_tensor_scan(out=g_B[:], data0=scanmask, data1=g_lf[:], initial=0.0, op0=ALU.mult, op1=ALU.add),
                     reads=[r_cmat, r_lf], writes=[r_B])
                if fwd:
                    S.op("dve", lambda e: e.tensor_copy(out=Bcol[:, hd, :, :], in_=B3[:, :, 31::32]), reads=[r_B], writes=[r_E[hd]])
                    S.op("dve", lambda e: e.tensor_tensor(out=B3, in0=B3, in1=Bcol[:, hd, :, 0:1].broadcast_to([128, 8, 64]), op=ALU.subtract),
                         reads=[r_B, r_E[hd]], writes=[r_B])
                    S.op("dve", lambda e: e.tensor_tensor(out=g_w[:], in0=g_w[:], in1=g_B[:], op=ALU.add), reads=[r_w, r_B], writes=[r_w])
                else:
                    S.op("dve", lambda e: e.tensor_copy(out=Bcol[:, hd, :, 1:2], in_=B3[:, :, 63:64]), reads=[r_B], writes=[r_E[hd]])
                    S.op("dve", lambda e: e.tensor_tensor(out=g_B[:], in0=g_B[:], in1=g_lf[:], op=ALU.subtract), reads=[r_B, r_lf], writes=[r_B])
                    S.op("dve", lambda e: e.tensor_copy(out=Bcol[:, hd, :, 0:1], in_=B3[:, :, 32:33]), reads=[r_B], writes=[r_E[hd]])
                    S.op("dve", lambda e: e.tensor_tensor(out=B3, in0=B3, in1=Bcol[:, hd, :, 0:1].broadcast_to([128, 8, 64]), op=ALU.subtract),
                         reads=[r_B, r_E[hd]], writes=[r_B])
                    S.op("dve", lambda e: e.tensor_tensor(out=g_w[:], in0=g_w[:], in1=g_B[:], op=ALU.subtract), reads=[r_w, r_B], writes=[r_w])
                    S.op("dve", lambda e: e.tensor_tensor(out=ED[:, hd, :], in0=Bcol[:, hd, :, 1], in1=Bcol[:, hd, :, 0], op=ALU.subtract),
                         reads=[r_E[hd]], writes=[r_E[hd]])

            def A2(hd):
                g_e, g_w, g_lf, g_B = gtb[hd % NSET]
                r_e, r_w, r_lf, r_B = r_gtb[hd % NSET]
                l1m_c = lbv[:, 1, hd:hd + 1]
                B3 = g_B[:].rearrange("p (c t) -> p c t", t=64)
                S.op("act", lambda e: e.activation(out=g_e[:], in_=g_B[:], func=AF.Exp, scale=(1.0 if fwd else -1.0)), reads=[r_B], writes=[r_e])
                S.op("act", lambda e: e.activation(out=kT[:, hd, :], in_=g_w[:], func=AF.Exp, scale=-1.0, bias=l1m_c),
                     reads=[r_w, r_gv], writes=[r_kT[hd]])
                S.op("act", lambda e: e.activation(out=EB[:, hd, :], in_=Bcol[:, hd, :, 1], func=AF.Exp), reads=[r_E[hd]], writes=[r_E[hd]])
                if fwd:
                    S.op("act", lambda e: e.activation(out=EA[:, hd, :], in_=Bcol[:, hd, :, 0], func=AF.Exp), reads=[r_E[hd]], writes=[r_E[hd]])
                    S.op("act", lambda e: e.activation(out=EC[:, hd, :], in_=B3[:, :, 63], func=AF.Exp), reads=[r_B], writes=[r_E[hd]])
                else:
                    S.op("act", lambda e: e.activation(out=EC[:, hd, :], in_=Bcol[:, hd, :, 0], func=AF.Exp), reads=[r_E[hd]], writes=[r_E[hd]])
                    S.op("act", lambda e: e.activation(out=EA[:, hd, :], in_=ED[:, hd, :], func=AF.Exp), reads=[r_E[hd]], writes=[r_E[hd]])
                S.op("pool", lambda e: e.tensor_tensor(out=qs[:, hd, :], in0=qs[:, hd, :], in1=g_e[:], op=ALU.mult),
                     reads=[r_qs[hd], r_e], writes=[r_qs[hd]])
                S.op("pool", lambda e: e.tensor_tensor(
                    out=qh[:, hd, :].rearrange("p (c t) -> p c t", t=64), in0=qs[:, hd, :].rearrange("p (c t) -> p c t", t=64),
                    in1=EA[:, hd, :].unsqueeze(2).broadcast_to([128, 8, 64]), op=ALU.mult),
                    reads=[r_qs[hd], r_E[hd]], writes=[r_qh[hd]])
                kk = hd % 2
                S.op("dve", lambda e: e.tensor_tensor(
                    out=kh[kk][:].rearrange("p (c t) -> p c t", t=64), in0=kT[:, hd, :].rearrange("p (c t) -> p c t", t=64),
                    in1=EC[:, hd, :].unsqueeze(2).broadcast_to([128, 8, 64]), op=ALU.mult),
                    reads=[r_kT[hd], r_E[hd]], writes=[r_kh[kk]])

            def TT(hd):
                kk = hd % 2
                pb, r_pb = nb()
                pbb = pb[:].bitcast(BF16)
                for blk in range(4):
                    S.op("pe", lambda e, blk=blk: e.transpose(
                        pbb[:, blk * 128:(blk + 1) * 128], kh[kk][:, blk * 128:(blk + 1) * 128], ident[:]),
                        reads=[r_kh[kk], r_const], writes=[r_pb])
                tstate[hd] = (pbb, r_pb)

            def A3(hd):
                pbb, r_pb = tstate[hd]
                if fwd:
                    S.op("dve", lambda e: e.tensor_copy(out=ktm[:, :, hd, :], in_=pbb[:, 0:512].rearrange("p (a b) -> p a b", a=4)),
                         reads=[r_pb], writes=r_ktm)
                else:
                    S.op("act", lambda e: e.activation(out=ktm[:, :, hd, :], in_=pbb[:, 0:512].rearrange("p (a b) -> p a b", a=4), func=AF.Copy),
                         reads=[r_pb], writes=r_ktm)

            def step(k):
                if k == -2:
                    Z(0); Z(1); Z(2)
                if ztile is None and 0 <= k + 3 < 8 and k + 3 >= 3:
                    Z(k + 3)
                if 0 <= k + 2 < 8:
                    A1(k + 2)
                if 0 <= k - 1 < 8:
                    TT(k - 1)
                if 0 <= k + 1 < 8:
                    D1(k + 1)
                if 0 <= k - 2 < 8:
                    A3(k - 2)
                if 0 <= k < 8:
                    A2(k)
                if ztile is not None and 0 <= k + 3 < 8 and k + 3 >= 3:
                    Z(k + 3)

            return step

        def load_bwd_operands(i):
            S.op("sp", lambda e: e.dma_start(out=qs, in_=qsc[:, :, T * i:T * (i + 1)]), reads=[r_qsc[i]], writes=r_qs, dma=True)
            S.op("sp", lambda e: e.dma_start(out=un[:, 12288:16384], in_=vsc[i]), reads=[r_vsc[i]], writes=r_vtm, dma=True)

        def scan_pass(direction):
            fwd = direction == 0
            order = list(range(NT)) if fwd else list(range(NT - 1, -1, -1))
            zg = G_HIN + 2 if fwd else G_HIN + 4
            load_x1(order[0], 0)
            for n, i in enumerate(order):
                b = n % 2
                if n + 1 < NT:
                    load_x1(order[n + 1], (n + 1) % 2)
                r_x = r_xt[b]
                xc = lambda c: xt[b][:, c, 15:15 + T]
                if not fwd:
                    S.op("pool", lambda e, i=i: e.dma_start(out=of_sb, in_=ofw[:, :, T * i:T * (i + 1)]),
                         reads=[r_ofw[i]], writes=r_ofsb, dma=True)
                    if n == 0:
                        load_bwd_operands(i)
                        st0 = gate_pipeline(False, None, None, ztile=i)
                        for k in range(-2, 10):
                            st0(k)
                hk = lambda kc: h[:, kc, 0:T]
                if fwd:
                    rms_norm(xc, r_x, T, lambda c: gv[:, 1, c:c + 1], lambda c: MOD(1, 0, c),
                             lambda c: h[:, c, 0:T], lambda c: r_h[c])
                    for half in range(2):
                        slot, r_slot = wg(G_HIN + half)
                        for j in range(4):
                            hd = 4 * half + j
                            (pb, r_pb, _, _), = proj_fm(slot[:], r_slot, j, hk, r_h, T)
                            S.op("act", lambda e, pb=pb, hd=hd: e.activation(out=qs[:, hd, :], in_=pb[:], func=AF.Silu),
                                 reads=[r_pb], writes=[r_qs[hd]])
                    for half in range(2):
                        slot, r_slot = wg(G_HIN + 6 + half)
                        s3 = slot[:].rearrange("p (k n) -> p k n", k=8)
                        for blk in range(4):
                            pb, r_pb = nb()
                            for kc in range(8):
                                S.op("pe", lambda e, s3=s3, blk=blk, kc=kc, pb=pb: e.matmul(
                                    pb[:], lhsT=h[:, kc, blk * 128:(blk + 1) * 128], rhs=s3[:, kc, :],
                                    start=(kc == 0), stop=(kc == 7)), reads=[r_slot] + r_h, writes=[r_pb])
                            S.op("act", lambda e, pb=pb, blk=blk, half=half: e.activation(
                                out=vtm[:, blk, half * 512:(half + 1) * 512], in_=pb[:], func=AF.Copy),
                                reads=[r_pb], writes=[r_vtm[blk]])
                    S.op("sp", lambda e, i=i: e.dma_start(out=qsc[:, :, T * i:T * (i + 1)], in_=qs), reads=r_qs, writes=[r_qsc[i]], dma=True, nodep=True)
                    S.op("sp", lambda e, i=i: e.dma_start(out=vsc[i], in_=un[:, 12288:16384]), reads=r_vtm, writes=[r_vsc[i]], dma=True, nodep=True)
                    for half in range(2):
                        slot, r_slot = wg(G_HIN + 8 + half)
                        for j in range(4):
                            hd = 4 * half + j
                            (pb, r_pb, _, _), = proj_fm(slot[:], r_slot, j, hk, r_h, T)
                            S.op("act", lambda e, pb=pb, hd=hd: e.activation(out=kT[:, hd, :], in_=pb[:], func=AF.Silu),
                                 reads=[r_pb], writes=[r_kT[hd]])
                    S.op("sp", lambda e, i=i: e.dma_start(out=gsc[:, :, T * i:T * (i + 1)], in_=kT), reads=r_kT, writes=[r_gsc[i]], dma=True, nodep=True)
                    for half in range(2):
                        slot, r_slot = wg(G_HIN + 4 + half)
                        for j in range(4):
                            hd = 4 * half + j
                            (pb, r_pb, _, _), = proj_fm(slot[:], r_slot, j, hk, r_h, T)
                            stg, r_stg = gtb[hd % NSET][0], r_gtb[hd % NSET][0]
                            S.op("dve", lambda e, pb=pb, stg=stg: e.tensor_copy(out=stg[:], in_=pb[:]), reads=[r_pb], writes=[r_stg])
                            S.op("sp", lambda e, i=i, hd=hd, stg=stg: e.dma_start(out=zsc[:, hd, T * i:T * (i + 1)], in_=stg[:]),
                                 reads=[r_stg], writes=[r_stgdma[hd % NSET], r_zsc[i]], dma=True, nodep=True)
                    zslots = [wg(zg + half) for half in range(2)]
                    stf = gate_pipeline(True, zslots, hk)
                    for k in range(-2, 10):
                        stf(k)
                corder = list(range(8)) if fwd else list(range(7, -1, -1))
                for cn, c in enumerate(corder):
                    blk, par = c // 2, c % 2
                    kk = cn % 2
                    gcn = state["cn"]; state["cn"] += 1
                    sp_cur, sp_nxt = gcn % 2, (gcn + 1) % 2
                    mk = masks[:, (0 if fwd else 2) + par, :]
                    psc, r_psc = nb()
                    for hd in range(8):
                        S.op("pe", lambda e, psc=psc, hd=hd, blk=blk, c=c: e.matmul(
                            psc[:, hd * 64:(hd + 1) * 64], lhsT=kT[:, hd, blk * 128:(blk + 1) * 128],
                            rhs=qs[:, hd, c * 64:(c + 1) * 64], start=True, stop=True),
                            reads=[r_kT[hd], r_qs[hd]], writes=[r_psc])
                    S.op("dve", lambda e, psc=psc, kk=kk, mk=mk: e.tensor_tensor(
                        out=scT[kk][:], in0=psc[:].rearrange("p (a b) -> p a b", a=8),
                        in1=mk.unsqueeze(1).broadcast_to([128, 8, 64]), op=ALU.mult),
                        reads=[r_psc, r_cmat], writes=[r_scT[kk]])
                    pds = [nb(), nb()]
                    for hd in range(8):
                        pd, r_pd = pds[hd // 4]
                        p0 = 64 * par
                        S.op("pe", lambda e, pd=pd, hd=hd, blk=blk, p0=p0: e.matmul(
                            pd[:, (hd % 4) * 128:(hd % 4 + 1) * 128], lhsT=ktm[p0:p0 + 64, blk, hd, :],
                            rhs=vtm[p0:p0 + 64, blk, hd * 128:(hd + 1) * 128], start=True, stop=True),
                            reads=r_ktm + [r_vtm[blk]], writes=[r_pd])
                    po, r_po = nb()
                    for hd in range(8):
                        S.op("pe", lambda e, po=po, hd=hd, c=c, sp_cur=sp_cur: e.matmul(
                            po[:, hd * 64:(hd + 1) * 64], lhsT=Sp[sp_cur][:, hd, :], rhs=qh[:, hd, c * 64:(c + 1) * 64],
                            start=True, stop=False), reads=[r_Sp[sp_cur][hd // 4], r_qh[hd]], writes=[r_po])
                        S.op("pe", lambda e, po=po, hd=hd, blk=blk, kk=kk: e.matmul(
                            po[:, hd * 64:(hd + 1) * 64], lhsT=vtm[:, blk, hd * 128:(hd + 1) * 128], rhs=scT[kk][:, hd, :],
                            start=False, stop=True), reads=[r_vtm[blk], r_scT[kk]], writes=[r_po])
                    for hh in range(2):
                        pd, r_pd = pds[hh]
                        for j in range(4):
                            hd = 4 * hh + j
                            S.op("dve", lambda e, pd=pd, hd=hd, j=j, c=c: e.scalar_tensor_tensor(
                                out=Sst[:, hd, :], in0=Sst[:, hd, :], scalar=EB[:, hd, c:c + 1], in1=pd[:, j * 128:(j + 1) * 128],
                                op0=ALU.mult, op1=ALU.add), reads=[r_pd, r_Sg[hd], r_E[hd]], writes=[r_Sg[hd]])
                        S.op("act", lambda e, hh=hh, sp_nxt=sp_nxt: e.activation(
                            out=Sp[sp_nxt][:, 4 * hh:4 * hh + 4, :], in_=Sst[:, 4 * hh:4 * hh + 4, :], func=AF.Copy),
                            reads=r_Sg[4 * hh:4 * hh + 4], writes=[r_Sp[sp_nxt][hh]])
                    po3 = po[:].rearrange("p (a b) -> p a b", a=8)
                    if fwd:
                        S.op("act", lambda e, po3=po3, c=c: e.activation(out=o_sb[:, :, c * 64:(c + 1) * 64], in_=po3, func=AF.Copy),
                             reads=[r_po], writes=r_osb)
                    else:
                        S.op("dve", lambda e, po3=po3, c=c: e.tensor_tensor(
                            out=o_sb[:, :, c * 64:(c + 1) * 64], in0=po3, in1=of_sb[:, :, c * 64:(c + 1) * 64], op=ALU.add),
                            reads=[r_po] + r_ofsb, writes=r_osb)
                if fwd:
                    S.op("pool", lambda e, i=i: e.dma_start(out=ofw[:, :, T * i:T * (i + 1)], in_=o_sb),
                         reads=r_osb, writes=[r_ofw[i]], dma=True, nodep=True)
                    continue
                sg = kT
                S.op("sp", lambda e, i=i: e.dma_start(out=kT, in_=gsc[:, :, T * i:T * (i + 1)]), reads=[r_gsc[i]], writes=r_kT, dma=True)
                for hd in range(8):
                    k = hd % 2
                    S.op("act", lambda e, hd=hd, k=k: e.activation(out=sqb[k][:, 0:T], in_=o_sb[:, hd, :], func=AF.Square),
                         reads=r_osb, writes=[r_sqb[k]])
                    pb, r_pb = nb()
                    S.op("pe", lambda e, pb=pb, k=k: e.matmul(pb[:], lhsT=ones[:], rhs=sqb[k][:, 0:T], start=True, stop=True),
                         reads=[r_sqb[k], r_const], writes=[r_pb])
                    S.op("act", lambda e, pb=pb, k=k: e.activation(out=tmpn[k][:, 0:T], in_=pb[:], func=AF.Ln, scale=1.0 / 128, bias=EPS),
                         reads=[r_pb], writes=[r_tmpn[k]])
                    S.op("act", lambda e, k=k: e.activation(out=tmpn[k][:, 0:T], in_=tmpn[k][:, 0:T], func=AF.Exp, scale=-0.5),
                         reads=[r_tmpn[k]], writes=[r_tmpn[k]])
                    S.op("dve", lambda e, hd=hd, k=k: e.tensor_tensor(out=tmpn[k][:, 0:T], in0=o_sb[:, hd, :], in1=tmpn[k][:, 0:T], op=ALU.mult),
                         reads=r_osb + [r_tmpn[k]], writes=[r_tmpn[k]])
                    S.op("dve", lambda e, hd=hd, k=k: e.scalar_tensor_tensor(
                        out=h[:, hd, 0:T], in0=tmpn[k][:, 0:T], scalar=vcol(V_GNG, hd), in1=sg[:, hd, :], op0=ALU.mult, op1=ALU.mult),
                        reads=[r_tmpn[k], r_vecs, r_kT[hd]] + r_h, writes=[r_h[hd]])
                wout_residual(1, b, r_x, G_HOUT)
                ghook = mhook = None
                if n + 1 < NT:
                    inext = order[n + 1]
                    load_bwd_operands(inext)
                    stn = gate_pipeline(False, None, None, ztile=inext)

                    def ghook(g, stn=stn):
                        if g == 0:
                            stn(-2); stn(-1); stn(0)
                        else:
                            stn(g)

                    def mhook(stn=stn):
                        stn(8); stn(9)
                mlp(1, b, r_x, G_W1_1, G_W2_1, mid_hook=mhook, group_hook=ghook)
                outb = arena[:, 0:8192].bitcast(F32).rearrange("p (c t) -> p c t", c=8)
                rms_norm(xc, r_x, T, lambda c: vcol(V_FG, c), None,
                         lambda c: outb[:, c, :], lambda c: r_ar[2 * c], out_final=True)
                S.op("pool", lambda e, i=i: e.dma_start(out=outT[:, :, T * i:T * (i + 1)], in_=outb),
                     reads=r_ar[0:16], writes=[r_out[i]], dma=True, nodep=True)

        scan_pass(0)
        S.barrier(lambda e: e.memset(ED[:], 0.0))
        S.op("pool", lambda e: e.dma_start(out=cc_in, in_=Sst[:].rearrange("p a b -> p (a b)")), reads=r_Sg, writes=[r_ccin], dma=True)
        S.op("pool", lambda e: e.collective_compute("AllGather", ALU.bypass, replica_groups=[[0, 1], [2, 3], [4, 5], [6, 7]],
                                                     ins=[cc_in], outs=[cc_out]), reads=[r_ccin], writes=[r_ccout])
        Sx = arena[:, 0:4096].bitcast(F32).rearrange("p (r f) -> p r f", r=2)
        r_Sx = r_ar[0:8]
        S.op("pool", lambda e: e.dma_start(out=Sx, in_=cc_out.rearrange("(r p) f -> p r f", p=128)),
             reads=[r_ccout], writes=r_Sx, dma=True)
        Sflat = Sst[:].rearrange("p a b -> p (a b)")
        S.op("dve", lambda e: e.tensor_scalar(out=Sflat, in0=Sx[:, 0, :], scalar1=vcol(V_FLAG, 0), scalar2=None, op0=ALU.mult),
             reads=r_Sx + [r_vecs], writes=r_Sg)
        S.op("dve", lambda e: e.scalar_tensor_tensor(out=Sflat, in0=Sx[:, 1, :], scalar=vcol(V_FLAG, 1), in1=Sflat,
                                                      op0=ALU.mult, op1=ALU.add), reads=r_Sx + [r_vecs] + r_Sg, writes=r_Sg)
        nxt = state["cn"] % 2
        S.op("act", lambda e: e.activation(out=Sp[nxt][:], in_=Sst[:], func=AF.Copy), reads=r_Sg, writes=r_Sp[nxt])
        scan_pass(1)
        S.emit(final_waits=[r_out[0]])
    return nc


_NC = None


def _fm(v):
    return np.ascontiguousarray(np.asarray(v, np.float32).reshape(8, 128).T)


def kernel(x, c, norm1_g, norm2_g, ada_w, ada_b, mlp_w1, mlp_w2, conv_w_in, conv_dw_w, conv_dw_b,
           conv_ln_g, conv_ln_b, conv_w_out, hgrn_w_in, hgrn_lb_logits, hgrn_gn_g, hgrn_w_out, final_g):
    global _NC
    f = lambda a: np.ascontiguousarray(np.asarray(a, np.float32))
    x = f(x); c = f(c)
    ada_w = f(ada_w); mlp_w1 = f(mlp_w1); mlp_w2 = f(mlp_w2)
    cin = f(conv_w_in)[0]; cout = f(conv_w_out)[0]; hout = f(hgrn_w_out)[0]
    hin = f(hgrn_w_in)[0]
    hin_sw = np.ascontiguousarray(np.concatenate(
        [hin[:, 0:1024], hin[:, 2048:3072], hin[:, 1024:2048], hin[:, 3072:]], axis=1))
    dw = f(conv_dw_w)[0]
    cm = np.zeros((128, NCM), np.float32)
    cm[:, CM_ID:CM_ID + 128] = np.eye(128, dtype=np.float32)
    s_ = np.arange(64)[:, None]; t_ = np.arange(64)[None, :]
    fe = np.zeros((128, 64), np.float32); fe[:64] = (s_ <= t_)
    fo = np.zeros((128, 64), np.float32); fo[64:] = (s_ <= t_)
    be = np.zeros((128, 64), np.float32); be[:64] = (s_ >= t_)
    bo = np.zeros((128, 64), np.float32); bo[64:] = (s_ >= t_)
    cm[:, CM_MASK:CM_MASK + 256] = np.concatenate([fe, fo, be, bo], axis=1)
    sm = np.ones(512, np.float32); sm[::64] = 0.0
    cm[:, CM_SCAN:CM_SCAN + 512] = sm[None, :]
    in_maps = []
    for r in range(8):
        b, half = r // 2, r % 2
        if half == 0:
            xs = x[b, 0:LTOK + 16]
        else:
            xs = x[b, ::-1][0:LTOK + 16]
        xTr = np.zeros((128, 8, XW), np.float32)
        xTr[:, :, 16:16 + LTOK + 16] = xs.T.reshape(8, 128, LTOK + 16).transpose(1, 0, 2)
        vv = np.zeros((128, NV), np.float32)
        for l in range(2):
            vv[:, V_N1G + 8 * l:V_N1G + 8 * l + 8] = _fm(norm1_g[l])
            vv[:, V_N2G + 8 * l:V_N2G + 8 * l + 8] = _fm(norm2_g[l])
            vv[:, V_LBL + 8 * l:V_LBL + 8 * l + 8] = _fm(hgrn_lb_logits[l])
            vv[:, V_ADAB + 48 * l:V_ADAB + 48 * l + 48] = np.asarray(ada_b[l], np.float32).reshape(48, 128).T
        vv[:, V_FG:V_FG + 8] = _fm(final_g)
        vv[:, V_DWB:V_DWB + 8] = _fm(conv_dw_b[0])
        vv[:, V_LNG:V_LNG + 8] = _fm(conv_ln_g[0])
        vv[:, V_LNB:V_LNB + 8] = _fm(conv_ln_b[0])
        vv[:, V_GNG:V_GNG + 8] = _fm(hgrn_gn_g[0])
        vv[:, V_FLAG:V_FLAG + 2] = np.array([0.0, 1.0] if half == 0 else [1.0, 0.0], np.float32)[None, :]
        vv[:, V_C:V_C + 8] = _fm(c[b])
        dwr = dw if half == 0 else dw[::-1]
        vv[:, V_DW:V_DW + 248] = dwr.T.reshape(8, 128, 31).transpose(1, 0, 2).reshape(128, 248)
        in_maps.append({
            "xT": xTr, "vecs": vv, "cmat": cm, "ada_w": ada_w, "mlp_w1": mlp_w1, "mlp_w2": mlp_w2,
            "conv_w_in": cin, "conv_w_out": cout, "hgrn_w_in": hin if half == 0 else hin_sw, "hgrn_w_out": hout,
        })
    if _NC is None:
        _NC = build_nc()
    res = run_bass_kernel_spmd(_NC, in_maps, core_ids=list(range(8)))
    out = np.empty((4, 8192, 1024), np.float32)
    for r in range(8):
        b, half = r // 2, r % 2
        o = res.results[r]["outT"]
        tok = o.transpose(2, 1, 0).reshape(LTOK, 1024)
        if half == 0:
            out[b, 0:LTOK] = tok
        else:
            out[b, LTOK:] = tok[::-1]
    return out
```

```python
import contextlib
import numpy as np
import concourse.bass as bass
import concourse.mybir as mybir
from concourse.bass_utils import run_bass_kernel_spmd

F32 = mybir.dt.float32
BF16 = mybir.dt.bfloat16
F32R = mybir.dt.float32r
AF = mybir.ActivationFunctionType
ALU = mybir.AluOpType

ENGS = ["pe", "act", "dve", "pool", "sp"]
D = 1024
NT = 8
T = 512
TE = T + 30
LTOK = NT * T
XW = LTOK + 32
EPS = 1e-6
USE_PE_CONV = True

V_N1G, V_N2G, V_FG, V_DWB, V_LNG, V_LNB, V_LBL, V_GNG, V_FLAG, V_C, V_ADAB, V_DW = (
    0, 16, 32, 40, 48, 56, 64, 80, 88, 90, 98, 194)
NV = 194 + 248
CM_ID, CM_MASK, CM_SCAN, NCM = 0, 128, 384, 896

G_CIN, G_COUT, G_W1_0, G_W2_0, G_HIN, G_HOUT, G_W1_1, G_W2_1, NG = 0, 4, 6, 14, 22, 32, 34, 42, 50


class Res:
    __slots__ = ("name", "writer", "readers", "dma_sem", "dma_cnt")

    def __init__(self, name):
        self.name = name
        self.writer = None
        self.readers = []
        self.dma_sem = None
        self.dma_cnt = 0


class Sched:
    def __init__(self, nc, stack):
        self.nc = nc
        self.stack = stack
        self.ops = {e: [] for e in ENGS}
        self.eng_sem = {e: stack.enter_context(nc.semaphore("s_" + e)) for e in ENGS}
        self.n_res = 0
        self.dma_res = []
        self.bar = None

    def barrier(self, fn):
        deps = []
        for e in ENGS:
            for i in range(len(self.ops[e]) - 1, -1, -1):
                if self.ops[e][i]["dma"] is None:
                    deps.append(("eng", e, i))
                    break
        for r in self.dma_res:
            deps.append(("dma", r, r.dma_cnt))
        idx = len(self.ops["pool"])
        self.ops["pool"].append(dict(fn=fn, deps=deps, signal=False, dma=None))
        self.bar = ("eng", "pool", idx)

    def res(self, name=None):
        self.n_res += 1
        return Res(name or ("r%d" % self.n_res))

    def op(self, eng, fn, reads=(), writes=(), dma=False, nodep=False):
        deps = []
        for r in reads:
            if r.writer is not None:
                deps.append(r.writer)
        for r in writes:
            if nodep:
                continue
            if r.writer is not None:
                deps.append(r.writer)
            deps.extend(r.readers)
        if self.bar is not None:
            deps.append(self.bar)
        idx = len(self.ops[eng])
        rec = dict(fn=fn, deps=deps, signal=False, dma=None)
        if dma:
            r0 = writes[0]
            if r0.dma_sem is None:
                r0.dma_sem = self.stack.enter_context(self.nc.semaphore("d%d" % self.n_res + r0.name))
                self.n_res += 1
                self.dma_res.append(r0)
            r0.dma_cnt += 1
            rec["dma"] = (r0, r0.dma_cnt)
            me = ("dma", r0, r0.dma_cnt)
        else:
            me = ("eng", eng, idx)
        self.ops[eng].append(rec)
        for r in reads:
            r.readers.append(me)
        for r in writes:
            r.writer = me
            r.readers = []
        return me

    def emit(self, final_waits=()):
        nc = self.nc
        for e in ENGS:
            for i, rec in enumerate(self.ops[e]):
                for d in rec["deps"]:
                    if d[0] == "eng":
                        _, de, di = d
                        if de == "pe" and e == "pe":
                            continue
                        self.ops[de][di]["signal"] = True
        cnt = {}
        for e in ENGS:
            c = 0
            for i, rec in enumerate(self.ops[e]):
                if rec["signal"]:
                    c += 1
                    cnt[(e, i)] = c
        handles = {"pe": "tensor", "act": "scalar", "dve": "vector", "pool": "gpsimd", "sp": "sync"}
        with nc.Block() as block:
            for e in ENGS:
                def body(eng_h, e=e):
                    waited = {}
                    for i, rec in enumerate(self.ops[e]):
                        need = {}
                        for d in rec["deps"]:
                            if d[0] == "eng":
                                _, de, di = d
                                if de == "pe" and e == "pe":
                                    continue
                                key = ("e", de)
                                val = cnt[(de, di)]
                                sem = self.eng_sem[de]
                            else:
                                _, r, c = d
                                key = ("d", id(r))
                                val = 16 * c
                                sem = r.dma_sem
                            if need.get(key, (0, None))[0] < val:
                                need[key] = (val, sem)
                        for key, (val, sem) in need.items():
                            if waited.get(key, 0) >= val:
                                continue
                            waited[key] = val
                            eng_h.wait_ge(sem, val)
                        inst = rec["fn"](eng_h)
                        if rec["dma"] is not None:
                            inst.then_inc(rec["dma"][0].dma_sem, 16)
                        elif rec["signal"]:
                            inst.then_inc(self.eng_sem[e], 1)
                    if e == "sp":
                        for r in final_waits:
                            eng_h.wait_ge(r.dma_sem, 16 * r.dma_cnt)
                getattr(block, handles[e])(body)


def build_nc(debug=False):
    nc = bass.Bass("TRN2", target_bir_lowering=False)

    def din(name, shape):
        return nc.dram_tensor(name, shape, F32, kind="ExternalInput").ap()

    xT = din("xT", [128, 8, XW])
    vecs_d = din("vecs", [128, NV])
    cmat_d = din("cmat", [128, NCM])
    ada_w = din("ada_w", [2, D, 6 * D])
    w1_d = din("mlp_w1", [2, D, 4 * D])
    w2_d = din("mlp_w2", [2, 4 * D, D])
    cin_d = din("conv_w_in", [D, 2 * D])
    cout_d = din("conv_w_out", [D, D])
    hin_d = din("hgrn_w_in", [D, 5 * D])
    hout_d = din("hgrn_w_out", [D, D])
    outT = nc.dram_tensor("outT", [128, 8, LTOK], F32, kind="ExternalOutput").ap()

    wsc = nc.dram_tensor("wsc", [NG, 128, 4096], BF16).ap()
    x1s = nc.dram_tensor("x1s", [128, 8, LTOK], F32).ap()
    ofw = nc.dram_tensor("ofw", [128, 8, LTOK], F32).ap()
    qsc = nc.dram_tensor("qsc", [128, 8, LTOK], BF16).ap()
    vsc = nc.dram_tensor("vsc", [NT, 128, 4096], BF16).ap()
    gsc = nc.dram_tensor("gsc", [128, 8, LTOK], BF16).ap()
    zsc = nc.dram_tensor("zsc", [128, 8, LTOK], F32).ap()
    cc_in = nc.dram_tensor("cc_in", [128, 1024], F32).ap()
    cc_out = nc.dram_tensor("cc_out", [256, 1024], F32).ap()

    with contextlib.ExitStack() as st:
        S = Sched(nc, st)

        def sb(name, shape, dt):
            return st.enter_context(nc.sbuf_tensor("sb_" + name, shape, dt))

        NSLOT = 4
        slots = [sb("slot%d" % i, [128, 4096], BF16) for i in range(NSLOT)]
        r_slots = [S.res("slot%d" % i) for i in range(NSLOT)]
        r_slots_sw = [S.res("slotsw%d" % i) for i in range(NSLOT)]
        xt = [sb("xt%d" % i, [128, 8, TE], F32) for i in range(2)]
        r_xt = [S.res("xt%d" % i) for i in range(2)]
        h = sb("h", [128, 8, TE], BF16)
        r_h = [S.res("h%d" % c) for c in range(8)]
        sqb = [sb("sqb%d" % i, [128, TE], BF16) for i in range(2)]
        r_sqb = [S.res("sqb%d" % i) for i in range(2)]
        vA = sb("vA", [128, TE], F32); r_vA = S.res("vA")
        vB = sb("vB", [128, TE], F32); r_vB = S.res("vB")
        vC = sb("vC", [128, TE], F32); r_vC = S.res("vC")
        tmpn = [sb("tmpn%d" % i, [128, TE], F32) for i in range(2)]
        r_tmpn = [S.res("tmpn%d" % i) for i in range(2)]
        arena = sb("arena", [128, 16384], BF16)
        r_ar = [S.res("ar%d" % i) for i in range(32)]
        rt = [sb("rt%d" % i, [128, T], BF16) for i in range(2)]
        r_rt = [S.res("rt%d" % i) for i in range(2)]
        vecs = sb("vecs", [128, NV], F32); r_vecs = S.res("vecs")
        cmat = sb("cmat", [128, NCM], F32); r_cmat = S.res("cmat")
        ident = sb("ident", [128, 128], BF16)
        ones = sb("ones", [128, 128], BF16)
        r_const = S.res("const")
        mod = sb("mod", [128, 96], F32); r_mod = S.res("mod")
        cond = sb("cond", [128, 8], F32); r_cond = S.res("cond")
        cond_bf = sb("cond_bf", [128, 8], BF16)
        gv = sb("gv", [128, 2, 16], F32)
        lbv = sb("lbv", [128, 3, 8], F32)
        r_gv = S.res("gv")
        un = sb("un", [128, 17408], BF16)
        u = un[:, 0:4336].rearrange("p (c t) -> p c t", c=8); r_u = [S.res("u%d" % c) for c in range(8)]
        a_sb = un[:, 4336:6504].rearrange("p (c t) -> p c t", c=4); r_asb = [S.res("asb%d" % c) for c in range(4)]
        sgt = [un[:, 6504 + 1084 * i: 6504 + 1084 * (i + 1)].bitcast(F32) for i in range(2)]
        r_sgt = [S.res("sgt%d" % i) for i in range(2)]
        dg = [un[:, 8704 + 3968 * i: 8704 + 3968 * (i + 1)].rearrange("p (a b) -> p a b", a=31) for i in range(2)]
        r_dg = [S.res("dg%d" % i) for i in range(2)]
        qs = un[:, 0:4096].rearrange("p (c t) -> p c t", c=8); r_qs = [S.res("qs%d" % c) for c in range(8)]
        kT = un[:, 4096:8192].rearrange("p (c t) -> p c t", c=8); r_kT = [S.res("kT%d" % c) for c in range(8)]
        ktm = un[:, 8192:12288].rearrange("p (a b c) -> p a b c", a=4, b=8); r_ktm = [S.res("ktm%d" % c) for c in range(4)]
        vtm = un[:, 12288:16384].rearrange("p (a b) -> p a b", a=4); r_vtm = [S.res("vtm%d" % c) for c in range(4)]
        NGT, NSET = 4, 3
        gtb = [[sb("gt%d_%d" % (q, i), [128, T], F32) for i in range(NGT)] for q in range(NSET)]
        r_gtb = [[S.res("gt%d_%d" % (q, i)) for i in range(NGT)] for q in range(NSET)]
        Bcol = sb("Bcol", [128, 8, 8, 2], F32)
        Sst = sb("Sst", [128, 8, 128], F32); r_Sg = [S.res("S%d" % c) for c in range(8)]
        Sp = [sb("Sp%d" % i, [128, 8, 128], BF16) for i in range(2)]
        r_Sp = [[S.res("Sp%d_%d" % (i, g)) for g in range(2)] for i in range(2)]
        qh = sb("qh", [128, 8, T], BF16); r_qh = [S.res("qh%d" % c) for c in range(8)]
        kh = [sb("kh%d" % i, [128, T], BF16) for i in range(2)]; r_kh = [S.res("kh%d" % i) for i in range(2)]
        scT = [sb("scT%d" % i, [128, 8, 64], BF16) for i in range(2)]
        r_scT = [S.res("scT%d" % i) for i in range(2)]
        EA = sb("EA", [128, 8, 8], F32); EB = sb("EB", [128, 8, 8], F32); EC = sb("EC", [128, 8, 8], F32)
        r_E = [S.res("E%d" % c) for c in range(8)]
        ED = sb("ED", [128, 8, 8], F32)

        banks = [st.enter_context(nc.psum_tensor("pb%d" % i, [128, 512], F32)) for i in range(8)]
        r_banks = [S.res("pb%d" % i) for i in range(8)]
        state = dict(bank=0, slot=0)

        def nb():
            i = state["bank"]
            state["bank"] = (i + 1) % 8
            return banks[i], r_banks[i]

        _mats = [(G_CIN, 4), (G_COUT, 2), (G_W1_0, 8), (G_W2_0, 8), (G_HIN, 10), (G_HOUT, 2), (G_W1_1, 8), (G_W2_1, 8)]
        r_wsc = [None] * NG
        for (g0, n) in _mats:
            rm = S.res("wsc%d" % g0)
            for g in range(g0, g0 + n):
                r_wsc[g] = rm
        r_x1s = [S.res("x1s")] * NT
        r_ofw = [S.res("ofw")] * NT
        r_out = [S.res("out")] * NT
        r_ccin = S.res("ccin"); r_ccout = S.res("ccout")
        r_gsc = [S.res("gsc")] * NT
        r_zsc = [S.res("zsc")] * NT
        r_stgdma = [S.res("stgdma%d" % q) for q in range(3)]
        r_qsc = [S.res("qsc")] * NT
        r_vsc = [S.res("vsc")] * NT

        def wg(g):
            i = state["slot"]
            state["slot"] = (i + 1) % NSLOT
            S.op("sp", lambda e, i=i, g=g: e.dma_start(out=slots[i][:], in_=wsc[g]),
                 reads=[r_wsc[g]], writes=[r_slots[i]], dma=True)
            return slots[i], r_slots[i]

        def vcol(base, c, n=1):
            return vecs[:, base + c: base + c + n]

        S.op("sp", lambda e: e.dma_start(out=vecs[:], in_=vecs_d), writes=[r_vecs], dma=True)
        S.op("sp", lambda e: e.dma_start(out=cmat[:], in_=cmat_d), writes=[r_cmat], dma=True)
        S.op("dve", lambda e: e.tensor_copy(out=ident[:], in_=cmat[:, CM_ID:CM_ID + 128]), reads=[r_cmat], writes=[r_const])
        S.op("dve", lambda e: e.memset(ones[:], 1.0), writes=[r_const])
        masks = cmat[:, CM_MASK:CM_MASK + 256].rearrange("p (a b) -> p a b", a=4)
        scanmask = cmat[:, CM_SCAN:CM_SCAN + 512]

        def conv_k1024(g, src, col0):
            S.op("pool", lambda e: e.dma_start(
                out=wsc[g].rearrange("p (k n) -> p k n", k=8),
                in_=src[:, col0:col0 + 512].rearrange("(k p) n -> p k n", p=128)),
                writes=[r_wsc[g]], dma=True, nodep=True)

        def conv_w2(g, src, row0):
            S.op("pool", lambda e: e.dma_start(
                out=wsc[g].rearrange("p (k n) -> p k n", k=4),
                in_=src[row0:row0 + 512, :].rearrange("(k p) n -> p k n", p=128)),
                writes=[r_wsc[g]], dma=True, nodep=True)

        def convert_layer0():
            for j in range(4):
                conv_k1024(G_CIN + j, cin_d, 512 * j)
            for j in range(2):
                conv_k1024(G_COUT + j, cout_d, 512 * j)
            for j in range(8):
                conv_k1024(G_W1_0 + j, w1_d[0], 512 * j)
            for j in range(8):
                conv_w2(G_W2_0 + j, w2_d[0], 512 * j)

        l1_convs = ([lambda j=j: conv_k1024(G_HIN + j, hin_d, 512 * j) for j in range(10)]
                    + [lambda j=j: conv_k1024(G_HOUT + j, hout_d, 512 * j) for j in range(2)]
                    + [lambda j=j: conv_k1024(G_W1_1 + j, w1_d[1], 512 * j) for j in range(8)]
                    + [lambda j=j: conv_w2(G_W2_1 + j, w2_d[1], 512 * j) for j in range(8)])

        S.op("sp", lambda e: e.dma_start(out=xt[0][:], in_=xT[:, :, 1:1 + TE]), writes=[r_xt[0]], dma=True)

        S.op("act", lambda e: e.activation(out=cond_bf[:], in_=vcol(V_C, 0, 8), func=AF.Silu), reads=[r_vecs], writes=[r_cond])
        def ada_batch(l, k):
            mbank, r_mbank = nb()
            for g2 in range(2):
                grp = 2 * k + g2
                i = state["slot"]
                state["slot"] = (i + 1) % NSLOT
                sl3 = slots[i][:].rearrange("p (k n) -> p k n", k=8)
                r_sl = r_slots[i]
                S.op("pool", lambda e, grp=grp, sl3=sl3: e.dma_start(
                    out=sl3, in_=ada_w[l][:, grp * 512:(grp + 1) * 512].rearrange("(k p) n -> p k n", p=128)),
                    writes=[r_slots_sw[i], r_sl], dma=True)
                for j in range(4):
                    ch = g2 * 4 + j
                    for kc in range(8):
                        S.op("pe", lambda e, sl3=sl3, j=j, kc=kc, ch=ch: e.matmul(
                            mbank[:, ch:ch + 1], lhsT=sl3[:, kc, j * 128:(j + 1) * 128], rhs=cond_bf[:, kc:kc + 1],
                            start=(kc == 0), stop=(kc == 7)),
                            reads=[r_sl, r_cond], writes=[r_mbank])
            c0 = l * 48 + k * 8
            S.op("dve", lambda e: e.tensor_tensor(out=mod[:, c0:c0 + 8], in0=mbank[:, 0:8], in1=vcol(V_ADAB, c0, 8), op=ALU.add),
                 reads=[r_mbank, r_vecs], writes=[r_mod])
            if k == 1:
                S.op("dve", lambda e: e.scalar_tensor_tensor(
                    out=gv[:, l, 0:8], in0=mod[:, c0:c0 + 8], scalar=1.0, in1=vcol(V_N1G, l * 8, 8),
                    op0=ALU.add, op1=ALU.mult), reads=[r_mod, r_vecs], writes=[r_gv])
            if k == 4:
                S.op("dve", lambda e: e.scalar_tensor_tensor(
                    out=gv[:, l, 8:16], in0=mod[:, c0:c0 + 8], scalar=1.0, in1=vcol(V_N2G, l * 8, 8),
                    op0=ALU.add, op1=ALU.mult), reads=[r_mod, r_vecs], writes=[r_gv])

        for k in range(6):
            ada_batch(0, k)
        convert_layer0()
        S.op("dve", lambda e: e.tensor_tensor(out=lbv[:, 2, :], in0=vcol(V_LBL, 8, 8), in1=vcol(V_LBL, 0, 8), op=ALU.subtract),
             reads=[r_vecs], writes=[r_gv])
        S.op("act", lambda e: e.activation(out=lbv[:, 0, :], in_=lbv[:, 2, :], func=AF.Sigmoid), reads=[r_gv], writes=[r_gv])
        S.op("act", lambda e: e.activation(out=lbv[:, 1, :], in_=lbv[:, 0, :], func=AF.Ln, scale=-1.0, bias=1.0),
             reads=[r_gv], writes=[r_gv])

        def MOD(l, k, c):
            return mod[:, l * 48 + k * 8 + c: l * 48 + k * 8 + c + 1]

        def rms_norm(xsrc, r_x, W, gain, shift, out_fn, r_out_fn, out_final=False):
            segs = [(0, W)] if W <= 512 else [(0, W // 2), (W // 2, W)]
            pbs = [nb() for _ in segs]
            for c in range(8):
                k = c % 2
                xa = xsrc(c)
                S.op("act", lambda e, xa=xa, k=k: e.activation(out=sqb[k][:, 0:W], in_=xa, func=AF.Square),
                     reads=[r_x], writes=[r_sqb[k]])
                for (s0, s1), (pb, r_pb) in zip(segs, pbs):
                    S.op("pe", lambda e, c=c, k=k, s0=s0, s1=s1, pb=pb: e.matmul(
                        pb[:, 0:s1 - s0], lhsT=ones[:], rhs=sqb[k][:, s0:s1], start=(c == 0), stop=(c == 7)),
                        reads=[r_sqb[k], r_const], writes=[r_pb])
            for (s0, s1), (pb, r_pb) in zip(segs, pbs):
                S.op("act", lambda e, s0=s0, s1=s1, pb=pb: e.activation(
                    out=vA[:, s0:s1], in_=pb[:, 0:s1 - s0], func=AF.Ln, scale=1.0 / D, bias=EPS),
                    reads=[r_pb], writes=[r_vA])
            S.op("act", lambda e: e.activation(out=vB[:, 0:W], in_=vA[:, 0:W], func=AF.Exp, scale=-0.5),
                 reads=[r_vA], writes=[r_vB])
            for c in range(8):
                k = c % 2
                xa = xsrc(c); ga = gain(c); oa = out_fn(c); r_o = r_out_fn(c)
                if out_final:
                    S.op("dve", lambda e, xa=xa, ga=ga, oa=oa: e.scalar_tensor_tensor(
                        out=oa, in0=xa, scalar=ga, in1=vB[:, 0:W], op0=ALU.mult, op1=ALU.mult),
                        reads=[r_x, r_vB, r_gv, r_vecs], writes=[r_o])
                else:
                    sa = shift(c)
                    S.op("dve", lambda e, xa=xa, ga=ga, k=k: e.scalar_tensor_tensor(
                        out=tmpn[k][:, 0:W], in0=xa, scalar=ga, in1=vB[:, 0:W], op0=ALU.mult, op1=ALU.mult),
                        reads=[r_x, r_vB, r_gv], writes=[r_tmpn[k]])
                    S.op("act", lambda e, oa=oa, sa=sa, k=k: e.activation(
                        out=oa, in_=tmpn[k][:, 0:W], func=AF.Identity, scale=1.0, bias=sa),
                        reads=[r_tmpn[k], r_mod], writes=[r_o])

        def proj_fm(slot, r_slot, j, rhs_fn, r_rhs, W):
            segs = [(0, W)] if W <= 512 else [(0, W // 2), (W // 2, W)]
            outs = []
            for (s0, s1) in segs:
                pb, r_pb = nb()
                for kc in range(8):
                    S.op("pe", lambda e, kc=kc, s0=s0, s1=s1, pb=pb: e.matmul(
                        pb[:, 0:s1 - s0], lhsT=slot.rearrange("p (k n) -> p k n", k=8)[:, kc, j * 128:(j + 1) * 128],
                        rhs=rhs_fn(kc)[:, s0:s1], start=(kc == 0), stop=(kc == 7)),
                        reads=[r_slot] + r_rhs, writes=[r_pb])
                outs.append((pb, r_pb, s0, s1))
            return outs

        def mlp(l, xb, r_x, g_w1, g_w2, mid_hook=None, group_hook=None):
            xc = lambda c: xt[xb][:, c, 15:15 + T]
            rms_norm(xc, r_x, T, lambda c: gv[:, l, 8 + c:9 + c], lambda c: MOD(l, 3, c),
                     lambda c: h[:, c, 0:T], lambda c: r_h[c])
            aT = lambda fc: arena[:, fc * 512:(fc + 1) * 512]
            for g in range(8):
                slot, r_slot = wg(g_w1 + g)
                for j in range(4):
                    fc = 4 * g + j
                    (pb, r_pb, _, _), = proj_fm(slot[:], r_slot, j, lambda kc: h[:, kc, 0:T], r_h, T)
                    k = fc % 2
                    S.op("act", lambda e, pb=pb, k=k: e.activation(out=rt[k][:], in_=pb[:], func=AF.Relu),
                         reads=[r_pb], writes=[r_rt[k]])
                    S.op("dve" if group_hook is not None else "pool",
                         lambda e, fc=fc, k=k: e.tensor_tensor(out=aT(fc), in0=rt[k][:], in1=rt[k][:], op=ALU.mult),
                         reads=[r_rt[k]], writes=[r_ar[fc]])
                if group_hook is not None:
                    group_hook(g)
            if mid_hook is not None:
                mid_hook()
            for g in range(8):
                slot, r_slot = wg(g_w2 + g)
                s3 = slot[:].rearrange("p (k n) -> p k n", k=4)
                for oc in range(8):
                    for fcl in range(4):
                        fc = 4 * g + fcl
                        S.op("pe", lambda e, s3=s3, oc=oc, fcl=fcl, fc=fc, g=g: e.matmul(
                            banks[oc][:], lhsT=s3[:, fcl, oc * 128:(oc + 1) * 128], rhs=aT(fc),
                            start=(g == 0 and fcl == 0), stop=(g == 7 and fcl == 3)),
                            reads=[r_slot, r_ar[fc]], writes=[r_banks[oc]])
            for oc in range(8):
                S.op("dve", lambda e, oc=oc: e.scalar_tensor_tensor(
                    out=xc(oc), in0=banks[oc][:], scalar=MOD(l, 5, oc), in1=xc(oc), op0=ALU.mult, op1=ALU.add),
                    reads=[r_banks[oc], r_mod, r_x], writes=[r_x])

        def wout_residual(l, xb, r_x, g_wout):
            xc = lambda c: xt[xb][:, c, 15:15 + T]
            for half in range(2):
                slot, r_slot = wg(g_wout + half)
                for j in range(4):
                    oc = 4 * half + j
                    (pb, r_pb, _, _), = proj_fm(slot[:], r_slot, j, lambda kc: h[:, kc, 0:T], r_h, T)
                    S.op("dve", lambda e, pb=pb, oc=oc: e.scalar_tensor_tensor(
                        out=xc(oc), in0=pb[:], scalar=MOD(l, 2, oc), in1=xc(oc), op0=ALU.mult, op1=ALU.add),
                        reads=[r_pb, r_mod, r_x], writes=[r_x])

        def load_x0(i):
            b = i % 2
            S.op("sp", lambda e: e.dma_start(out=xt[b][:], in_=xT[:, :, 512 * i + 1: 512 * i + 1 + TE]),
                 writes=[r_xt[b]], dma=True)

        y32 = arena[:, 0:8192].bitcast(F32).rearrange("p (c t) -> p c t", c=8)
        ybf = arena[:, 8192:12288].rearrange("p (c t) -> p c t", c=8)
        ysq = arena[:, 12288:16384].rearrange("p (c t) -> p c t", c=8)
        r_y32 = lambda c: [r_ar[2 * c], r_ar[2 * c + 1]]

        def l0_mixer(i):
            b = i % 2
            r_x = r_xt[b]
            rms_norm(lambda c: xt[b][:, c, :], r_x, TE, lambda c: gv[:, 0, c:c + 1], lambda c: MOD(0, 0, c),
                     lambda c: h[:, c, :], lambda c: r_h[c])
            for half in range(2):
                slA, r_slA = wg(G_CIN + half)
                slG, r_slG = wg(G_CIN + 2 + half)
                for j in range(4):
                    for (pb, r_pb, s0, s1) in proj_fm(slA[:], r_slA, j, lambda kc: h[:, kc, :], r_h, TE):
                        S.op("act", lambda e, pb=pb, s0=s0, s1=s1, j=j: e.activation(
                            out=a_sb[:, j, s0:s1], in_=pb[:, 0:s1 - s0], func=AF.Copy), reads=[r_pb], writes=[r_asb[j]])
                for j in range(4):
                    c = 4 * half + j
                    k = c % 2
                    for (pb, r_pb, s0, s1) in proj_fm(slG[:], r_slG, j, lambda kc: h[:, kc, :], r_h, TE):
                        S.op("act", lambda e, pb=pb, s0=s0, s1=s1, k=k: e.activation(
                            out=sgt[k][:, s0:s1], in_=pb[:, 0:s1 - s0], func=AF.Sigmoid), reads=[r_pb], writes=[r_sgt[k]])
                    S.op("pool", lambda e, c=c, j=j, k=k: e.tensor_tensor(
                        out=u[:, c, :], in0=a_sb[:, j, :], in1=sgt[k], op=ALU.mult),
                        reads=[r_asb[j], r_sgt[k]], writes=[r_u[c]])
                    if i == 0:
                        S.op("pool", lambda e, c=c: e.memset(u[:, c, 0:15], 0.0), writes=[r_u[c]])
            if USE_PE_CONV:
                def build_diag(c):
                    par = c % 2
                    S.op("dve", lambda e: e.tensor_tensor(
                        out=dg[par], in0=ident[:].unsqueeze(1).broadcast_to([128, 31, 128]),
                        in1=vecs[:, V_DW + c * 31: V_DW + c * 31 + 31].unsqueeze(2).broadcast_to([128, 31, 128]), op=ALU.mult),
                        reads=[r_const, r_vecs], writes=[r_dg[par]])

                build_diag(0)
                build_diag(1)
                for c in range(8):
                    par = c % 2
                    pb, r_pb = nb()
                    for tap in range(31):
                        S.op("pe", lambda e, pb=pb, par=par, tap=tap, c=c: e.matmul(
                            pb[:], lhsT=dg[par][:, tap, :], rhs=u[:, c, tap:tap + T], start=(tap == 0), stop=(tap == 30)),
                            reads=[r_dg[par], r_u[c]], writes=[r_pb])
                    if c + 2 < 8:
                        build_diag(c + 2)
                    S.op("act", lambda e, pb=pb, c=c: e.activation(out=y32[:, c, :], in_=pb[:], func=AF.Identity, scale=1.0, bias=vcol(V_DWB, c)),
                         reads=[r_pb, r_vecs], writes=r_y32(c))
                    S.op("act", lambda e, pb=pb, c=c: e.activation(out=ybf[:, c, :], in_=pb[:], func=AF.Identity, scale=1.0, bias=vcol(V_DWB, c)),
                         reads=[r_pb, r_vecs], writes=[r_ar[16 + c]])
                    S.op("pool", lambda e, c=c: e.tensor_tensor(out=ysq[:, c, :], in0=y32[:, c, :], in1=y32[:, c, :], op=ALU.mult),
                         reads=r_y32(c), writes=[r_ar[24 + c]])
            else:
                for tap in range(31):
                    for c in range(8):
                        if tap == 0:
                            S.op("dve", lambda e, c=c: e.tensor_scalar(
                                out=y32[:, c, :], in0=u[:, c, 0:T], scalar1=vcol(V_DW, c * 31), scalar2=vcol(V_DWB, c),
                                op0=ALU.mult, op1=ALU.add), reads=[r_u[c], r_vecs], writes=r_y32(c))
                        else:
                            S.op("dve", lambda e, c=c, tap=tap: e.scalar_tensor_tensor(
                                out=y32[:, c, :], in0=u[:, c, tap:tap + T], scalar=vcol(V_DW, c * 31 + tap), in1=y32[:, c, :],
                                op0=ALU.mult, op1=ALU.add), reads=[r_u[c]] + r_y32(c), writes=r_y32(c))
                for c in range(8):
                    S.op("act", lambda e, c=c: e.activation(out=ybf[:, c, :], in_=y32[:, c, :], func=AF.Copy),
                         reads=r_y32(c), writes=[r_ar[16 + c]])
                    S.op("pool", lambda e, c=c: e.tensor_tensor(out=ysq[:, c, :], in0=y32[:, c, :], in1=y32[:, c, :], op=ALU.mult),
                         reads=r_y32(c), writes=[r_ar[24 + c]])
            pbm, r_pbm = nb()
            pbv, r_pbv = nb()
            for c in range(8):
                S.op("pe", lambda e, c=c, pbm=pbm: e.matmul(pbm[:], lhsT=ones[:], rhs=ybf[:, c, :], start=(c == 0), stop=(c == 7)),
                     reads=[r_ar[16 + c], r_const], writes=[r_pbm])
                S.op("pe", lambda e, c=c, pbv=pbv: e.matmul(pbv[:], lhsT=ones[:], rhs=ysq[:, c, :], start=(c == 0), stop=(c == 7)),
                     reads=[r_ar[24 + c], r_const], writes=[r_pbv])
            if 1 <= i <= 6:
                ada_batch(1, i - 1)
            S.op("act", lambda e, pbm=pbm: e.activation(out=vA[:, 0:T], in_=pbm[:], func=AF.Copy, scale=1.0 / D), reads=[r_pbm], writes=[r_vA])
            S.op("dve", lambda e: e.tensor_tensor(out=vC[:, 0:T], in0=vA[:, 0:T], in1=vA[:, 0:T], op=ALU.mult), reads=[r_vA], writes=[r_vC])
            S.op("dve", lambda e, pbv=pbv: e.scalar_tensor_tensor(out=vC[:, 0:T], in0=pbv[:], scalar=1.0 / D, in1=vC[:, 0:T],
                                                          op0=ALU.mult, op1=ALU.subtract), reads=[r_pbv, r_vC], writes=[r_vC])
            S.op("act", lambda e: e.activation(out=vC[:, 0:T], in_=vC[:, 0:T], func=AF.Ln, scale=1.0, bias=EPS), reads=[r_vC], writes=[r_vC])
            S.op("act", lambda e: e.activation(out=vB[:, 0:T], in_=vC[:, 0:T], func=AF.Exp, scale=-0.5), reads=[r_vC], writes=[r_vB])
            for c in range(8):
                k = c % 2
                S.op("dve", lambda e, c=c, k=k: e.tensor_tensor(out=tmpn[k][:, 0:T], in0=y32[:, c, :], in1=vA[:, 0:T], op=ALU.subtract),
                     reads=r_y32(c) + [r_vA], writes=[r_tmpn[k]])
                S.op("dve", lambda e, k=k: e.tensor_tensor(out=tmpn[k][:, 0:T], in0=tmpn[k][:, 0:T], in1=vB[:, 0:T], op=ALU.mult),
                     reads=[r_tmpn[k], r_vB], writes=[r_tmpn[k]])
                S.op("act", lambda e, c=c, k=k: e.activation(out=h[:, c, 0:T], in_=tmpn[k][:, 0:T], func=AF.Silu,
                                                              scale=vcol(V_LNG, c), bias=vcol(V_LNB, c)),
                     reads=[r_tmpn[k], r_vecs], writes=[r_h[c]])
            wout_residual(0, b, r_x, G_COUT)

        def l0_mlp(i):
            b = i % 2
            r_x = r_xt[b]
            hook = None
            if i >= 1:
                hook = lambda i=i: [f() for f in l1_convs[4 * (i - 1):4 * i]]
            mlp(0, b, r_x, G_W1_0, G_W2_0, mid_hook=hook)
            S.op("pool", lambda e, i=i, b=b: e.dma_start(out=x1s[:, :, T * i:T * (i + 1)], in_=xt[b][:, :, 15:15 + T]),
                 reads=[r_x], writes=[r_x1s[i]], dma=True, nodep=True)

        load_x0(1)
        l0_mixer(0)
        l0_mixer(1)
        l0_mlp(0)
        load_x0(2)
        l0_mlp(1)
        load_x0(3)
        for i in range(2, NT):
            l0_mixer(i)
            l0_mlp(i)
            if i + 2 < NT:
                load_x0(i + 2)

        S.barrier(lambda e: e.memset(ED[:], 0.0))
        S.op("dve", lambda e: e.memset(Sst[:], 0.0), writes=r_Sg)
        S.op("dve", lambda e: e.memset(Sp[0][:], 0.0), writes=r_Sp[0])
        state["cn"] = 0

        def load_x1(i, b):
            S.op("sp", lambda e: e.dma_start(out=xt[b][:, :, 15:15 + T], in_=x1s[:, :, T * i:T * (i + 1)]),
                 reads=[r_x1s[i]], writes=[r_xt[b]], dma=True)

        o_sb = arena[:, 0:8192].bitcast(F32).rearrange("p (c t) -> p c t", c=8)
        of_sb = arena[:, 8192:16384].bitcast(F32).rearrange("p (c t) -> p c t", c=8)
        r_osb = r_ar[0:16]
        r_ofsb = r_ar[16:32]

        def gate_pipeline(fwd, zslots, hk, ztile=None):
            tstate = {}

            zstate = {}

            def Z(hd):
                if ztile is not None:
                    g_e = gtb[hd % NSET][0]
                    r_e = r_gtb[hd % NSET][0]
                    S.op("act", lambda e: e.dma_start(out=g_e[:], in_=zsc[:, hd, T * ztile:T * (ztile + 1)]),
                         reads=[r_zsc[ztile]], writes=[r_e], dma=True)
                    zstate[hd] = (g_e, r_e)
                    return
                slot, r_slot = zslots[hd // 4]
                (pz, r_pz, _, _), = proj_fm(slot[:], r_slot, hd % 4, hk, r_h, T)
                zstate[hd] = (pz, r_pz)

            def A1(hd):
                g_e, g_w, g_lf, g_B = gtb[hd % NSET]
                r_e, r_w, r_lf, r_B = r_gtb[hd % NSET]
                pz, r_pz = zstate[hd]
                lb_c = lbv[:, 0, hd:hd + 1]
                S.op("act", lambda e: e.activation(out=g_e[:], in_=pz[:], func=AF.Exp), reads=[r_pz, r_e], writes=[r_e])
                S.op("act", lambda e: e.activation(out=g_w[:], in_=g_e[:], func=AF.Ln, scale=1.0, bias=1.0), reads=[r_e], writes=[r_w])
                S.op("act", lambda e: e.activation(out=g_lf[:], in_=g_e[:], func=AF.Ln, scale=1.0, bias=lb_c), reads=[r_e, r_gv], writes=[r_lf])

            def D1(hd):
                g_e, g_w, g_lf, g_B = gtb[hd % NSET]
                r_e, r_w, r_lf, r_B = r_gtb[hd % NSET]
                B3 = g_B[:].rearrange("p (c t) -> p c t", t=64)
                S.op("pool", lambda e: e.tensor_tensor(out=g_lf[:], in0=g_lf[:], in1=g_w[:], op=ALU.subtract), reads=[r_lf, r_w], writes=[r_lf])
                S.op("dve", lambda e: e.tensor_tensor_scan(out=g_B[:], data0=scanmask, data1=g_lf[:], initial=0.0, op0=ALU.mult, op1=ALU.add),
                     reads=[r_cmat, r_lf], writes=[r_B])
                if fwd:
                    S.op("dve", lambda e: e.tensor_copy(out=Bcol[:, hd, :, :], in_=B3[:, :, 31::32]), reads=[r_B], writes=[r_E[hd]])
                    S.op("dve", lambda e: e.tensor_tensor(out=B3, in0=B3, in1=Bcol[:, hd, :, 0:1].broadcast_to([128, 8, 64]), op=ALU.subtract),
                         reads=[r_B, r_E[hd]], writes=[r_B])
                    S.op("dve", lambda e: e.tensor_tensor(out=g_w[:], in0=g_w[:], in1=g_B[:], op=ALU.add), reads=[r_w, r_B], writes=[r_w])
                else:
                    S.op("dve", lambda e: e.tensor_copy(out=Bcol[:, hd, :, 1:2], in_=B3[:, :, 63:64]), reads=[r_B], writes=[r_E[hd]])
                    S.op("dve", lambda e: e.tensor_tensor(out=g_B[:], in0=g_B[:], in1=g_lf[:], op=ALU.subtract), reads=[r_B, r_lf], writes=[r_B])
                    S.op("dve", lambda e: e.tensor_copy(out=Bcol[:, hd, :, 0:1], in_=B3[:, :, 32:33]), reads=[r_B], writes=[r_E[hd]])
                    S.op("dve", lambda e: e.tensor_tensor(out=B3, in0=B3, in1=Bcol[:, hd, :, 0:1].broadcast_to([128, 8, 64]), op=ALU.subtract),
                         reads=[r_B, r_E[hd]], writes=[r_B])
                    S.op("dve", lambda e: e.tensor_tensor(out=g_w[:], in0=g_w[:], in1=g_B[:], op=ALU.subtract), reads=[r_w, r_B], writes=[r_w])
                    S.op("dve", lambda e: e.tensor_tensor(out=ED[:, hd, :], in0=Bcol[:, hd, :, 1], in1=Bcol[:, hd, :, 0], op=ALU.subtract),
                         reads=[r_E[hd]], writes=[r_E[hd]])

            def A2(hd):
                g_e, g_w, g_lf, g_B = gtb[hd % NSET]
                r_e, r_w, r_lf, r_B = r_gtb[hd % NSET]
                l1m_c = lbv[:, 1, hd:hd + 1]
                B3 = g_B[:].rearrange("p (c t) -> p c t", t=64)
                S.op("act", lambda e: e.activation(out=g_e[:], in_=g_B[:], func=AF.Exp, scale=(1.0 if fwd else -1.0)), reads=[r_B], writes=[r_e])
                S.op("act", lambda e: e.activation(out=kT[:, hd, :], in_=g_w[:], func=AF.Exp, scale=-1.0, bias=l1m_c),
                     reads=[r_w, r_gv], writes=[r_kT[hd]])
                S.op("act", lambda e: e.activation(out=EB[:, hd, :], in_=Bcol[:, hd, :, 1], func=AF.Exp), reads=[r_E[hd]], writes=[r_E[hd]])
                if fwd:
                    S.op("act", lambda e: e.activation(out=EA[:, hd, :], in_=Bcol[:, hd, :, 0], func=AF.Exp), reads=[r_E[hd]], writes=[r_E[hd]])
                    S.op("act", lambda e: e.activation(out=EC[:, hd, :], in_=B3[:, :, 63], func=AF.Exp), reads=[r_B], writes=[r_E[hd]])
                else:
                    S.op("act", lambda e: e.activation(out=EC[:, hd, :], in_=Bcol[:, hd, :, 0], func=AF.Exp), reads=[r_E[hd]], writes=[r_E[hd]])
                    S.op("act", lambda e: e.activation(out=EA[:, hd, :], in_=ED[:, hd, :], func=AF.Exp), reads=[r_E[hd]], writes=[r_E[hd]])
                S.op("pool", lambda e: e.tensor_tensor(out=qs[:, hd, :], in0=qs[:, hd, :], in1=g_e[:], op=ALU.mult),
                     reads=[r_qs[hd], r_e], writes=[r_qs[hd]])
                S.op("pool", lambda e: e.tensor_tensor(
                    out=qh[:, hd, :].rearrange("p (c t) -> p c t", t=64), in0=qs[:, hd, :].rearrange("p (c t) -> p c t", t=64),
                    in1=EA[:, hd, :].unsqueeze(2).broadcast_to([128, 8, 64]), op=ALU.mult),
                    reads=[r_qs[hd], r_E[hd]], writes=[r_qh[hd]])
                kk = hd % 2
                S.op("dve", lambda e: e.tensor_tensor(
                    out=kh[kk][:].rearrange("p (c t) -> p c t", t=64), in0=kT[:, hd, :].rearrange("p (c t) -> p c t", t=64),
                    in1=EC[:, hd, :].unsqueeze(2).broadcast_to([128, 8, 64]), op=ALU.mult),
                    reads=[r_kT[hd], r_E[hd]], writes=[r_kh[kk]])

            def TT(hd):
                kk = hd % 2
                pb, r_pb = nb()
                pbb = pb[:].bitcast(BF16)
                for blk in range(4):
                    S.op("pe", lambda e, blk=blk: e.transpose(
                        pbb[:, blk * 128:(blk + 1) * 128], kh[kk][:, blk * 128:(blk + 1) * 128], ident[:]),
                        reads=[r_kh[kk], r_const], writes=[r_pb])
                tstate[hd] = (pbb, r_pb)

            def A3(hd):
                pbb, r_pb = tstate[hd]
                if fwd:
                    S.op("dve", lambda e: e.tensor_copy(out=ktm[:, :, hd, :], in_=pbb[:, 0:512].rearrange("p (a b) -> p a b", a=4)),
                         reads=[r_pb], writes=r_ktm)
                else:
                    S.op("act", lambda e: e.activation(out=ktm[:, :, hd, :], in_=pbb[:, 0:512].rearrange("p (a b) -> p a b", a=4), func=AF.Copy),
                         reads=[r_pb], writes=r_ktm)

            def step(k):
                if k == -2:
                    Z(0); Z(1); Z(2)
                if ztile is None and 0 <= k + 3 < 8 and k + 3 >= 3:
                    Z(k + 3)
                if 0 <= k + 2 < 8:
                    A1(k + 2)
                if 0 <= k - 1 < 8:
                    TT(k - 1)
                if 0 <= k + 1 < 8:
                    D1(k + 1)
                if 0 <= k - 2 < 8:
                    A3(k - 2)
                if 0 <= k < 8:
                    A2(k)
                if ztile is not None and 0 <= k + 3 < 8 and k + 3 >= 3:
                    Z(k + 3)

            return step

        def load_bwd_operands(i):
            S.op("sp", lambda e: e.dma_start(out=qs, in_=qsc[:, :, T * i:T * (i + 1)]), reads=[r_qsc[i]], writes=r_qs, dma=True)
            S.op("sp", lambda e: e.dma_start(out=un[:, 12288:16384], in_=vsc[i]), reads=[r_vsc[i]], writes=r_vtm, dma=True)

        def scan_pass(direction):
            fwd = direction == 0
            order = list(range(NT)) if fwd else list(range(NT - 1, -1, -1))
            zg = G_HIN + 2 if fwd else G_HIN + 4
            load_x1(order[0], 0)
            for n, i in enumerate(order):
                b = n % 2
                if n + 1 < NT:
                    load_x1(order[n + 1], (n + 1) % 2)
                r_x = r_xt[b]
                xc = lambda c: xt[b][:, c, 15:15 + T]
                if not fwd:
                    if n == 0:
                        load_bwd_operands(i)
                        st0 = gate_pipeline(False, None, None, ztile=i)
                        for k in range(-2, 10):
                            st0(k)
                hk = lambda kc: h[:, kc, 0:T]
                if fwd:
                    rms_norm(xc, r_x, T, lambda c: gv[:, 1, c:c + 1], lambda c: MOD(1, 0, c),
                             lambda c: h[:, c, 0:T], lambda c: r_h[c])
                    for half in range(2):
                        slot, r_slot = wg(G_HIN + half)
                        for j in range(4):
                            hd = 4 * half + j
                            (pb, r_pb, _, _), = proj_fm(slot[:], r_slot, j, hk, r_h, T)
                            S.op("act", lambda e, pb=pb, hd=hd: e.activation(out=qs[:, hd, :], in_=pb[:], func=AF.Silu),
                                 reads=[r_pb], writes=[r_qs[hd]])
                    for half in range(2):
                        slot, r_slot = wg(G_HIN + 6 + half)
                        s3 = slot[:].rearrange("p (k n) -> p k n", k=8)
                        for blk in range(4):
                            pb, r_pb = nb()
                            for kc in range(8):
                                S.op("pe", lambda e, s3=s3, blk=blk, kc=kc, pb=pb: e.matmul(
                                    pb[:], lhsT=h[:, kc, blk * 128:(blk + 1) * 128], rhs=s3[:, kc, :],
                                    start=(kc == 0), stop=(kc == 7)), reads=[r_slot] + r_h, writes=[r_pb])
                            S.op("act", lambda e, pb=pb, blk=blk, half=half: e.activation(
                                out=vtm[:, blk, half * 512:(half + 1) * 512], in_=pb[:], func=AF.Copy),
                                reads=[r_pb], writes=[r_vtm[blk]])
                    S.op("sp", lambda e, i=i: e.dma_start(out=qsc[:, :, T * i:T * (i + 1)], in_=qs), reads=r_qs, writes=[r_qsc[i]], dma=True, nodep=True)
                    S.op("sp", lambda e, i=i: e.dma_start(out=vsc[i], in_=un[:, 12288:16384]), reads=r_vtm, writes=[r_vsc[i]], dma=True, nodep=True)
                    for half in range(2):
                        slot, r_slot = wg(G_HIN + 8 + half)
                        for j in range(4):
                            hd = 4 * half + j
                            (pb, r_pb, _, _), = proj_fm(slot[:], r_slot, j, hk, r_h, T)
                            S.op("act", lambda e, pb=pb, hd=hd: e.activation(out=kT[:, hd, :], in_=pb[:], func=AF.Silu),
                                 reads=[r_pb], writes=[r_kT[hd]])
                    S.op("sp", lambda e, i=i: e.dma_start(out=gsc[:, :, T * i:T * (i + 1)], in_=kT), reads=r_kT, writes=[r_gsc[i]], dma=True, nodep=True)
                    for half in range(2):
                        slot, r_slot = wg(G_HIN + 4 + half)
                        for j in range(4):
                            hd = 4 * half + j
                            (pb, r_pb, _, _), = proj_fm(slot[:], r_slot, j, hk, r_h, T)
                            stg, r_stg = gtb[hd % NSET][0], r_gtb[hd % NSET][0]
                            S.op("dve", lambda e, pb=pb, stg=stg: e.tensor_copy(out=stg[:], in_=pb[:]), reads=[r_pb], writes=[r_stg])
                            S.op("sp", lambda e, i=i, hd=hd, stg=stg: e.dma_start(out=zsc[:, hd, T * i:T * (i + 1)], in_=stg[:]),
                                 reads=[r_stg], writes=[r_stgdma[hd % NSET], r_zsc[i]], dma=True, nodep=True)
                    zslots = [wg(zg + half) for half in range(2)]
                    stf = gate_pipeline(True, zslots, hk)
                    for k in range(-2, 10):
                        stf(k)
                corder = list(range(8)) if fwd else list(range(7, -1, -1))

                def load_of_chunk(cn2, i=i, corder=corder):
                    c2 = corder[cn2]
                    kb = cn2 % 2
                    S.op("act", lambda e, i=i: e.dma_start(
                        out=tmpn[kb][:, 0:512].rearrange("p (a b) -> p a b", a=8),
                        in_=ofw[:, :, T * i + 64 * c2: T * i + 64 * (c2 + 1)]),
                        reads=[r_ofw[i]], writes=[r_tmpn[kb]], dma=True)

                if not fwd:
                    load_of_chunk(0)
                    load_of_chunk(1)
                for cn, c in enumerate(corder):
                    blk, par = c // 2, c % 2
                    kk = cn % 2
                    gcn = state["cn"]; state["cn"] += 1
                    sp_cur, sp_nxt = gcn % 2, (gcn + 1) % 2
                    mk = masks[:, (0 if fwd else 2) + par, :]
                    psc, r_psc = nb()
                    for hd in range(8):
                        S.op("pe", lambda e, psc=psc, hd=hd, blk=blk, c=c: e.matmul(
                            psc[:, hd * 64:(hd + 1) * 64], lhsT=kT[:, hd, blk * 128:(blk + 1) * 128],
                            rhs=qs[:, hd, c * 64:(c + 1) * 64], start=True, stop=True),
                            reads=[r_kT[hd], r_qs[hd]], writes=[r_psc])
                    S.op("dve", lambda e, psc=psc, kk=kk, mk=mk: e.tensor_tensor(
                        out=scT[kk][:], in0=psc[:].rearrange("p (a b) -> p a b", a=8),
                        in1=mk.unsqueeze(1).broadcast_to([128, 8, 64]), op=ALU.mult),
                        reads=[r_psc, r_cmat], writes=[r_scT[kk]])
                    pds = [nb(), nb()]
                    for hd in range(8):
                        pd, r_pd = pds[hd // 4]
                        p0 = 64 * par
                        S.op("pe", lambda e, pd=pd, hd=hd, blk=blk, p0=p0: e.matmul(
                            pd[:, (hd % 4) * 128:(hd % 4 + 1) * 128], lhsT=ktm[p0:p0 + 64, blk, hd, :],
                            rhs=vtm[p0:p0 + 64, blk, hd * 128:(hd + 1) * 128], start=True, stop=True),
                            reads=r_ktm + [r_vtm[blk]], writes=[r_pd])
                    po, r_po = nb()
                    for hd in range(8):
                        S.op("pe", lambda e, po=po, hd=hd, c=c, sp_cur=sp_cur: e.matmul(
                            po[:, hd * 64:(hd + 1) * 64], lhsT=Sp[sp_cur][:, hd, :], rhs=qh[:, hd, c * 64:(c + 1) * 64],
                            start=True, stop=False), reads=[r_Sp[sp_cur][hd // 4], r_qh[hd]], writes=[r_po])
                        S.op("pe", lambda e, po=po, hd=hd, blk=blk, kk=kk: e.matmul(
                            po[:, hd * 64:(hd + 1) * 64], lhsT=vtm[:, blk, hd * 128:(hd + 1) * 128], rhs=scT[kk][:, hd, :],
                            start=False, stop=True), reads=[r_vtm[blk], r_scT[kk]], writes=[r_po])
                    for hh in range(2):
                        pd, r_pd = pds[hh]
                        for j in range(4):
                            hd = 4 * hh + j
                            S.op("dve", lambda e, pd=pd, hd=hd, j=j, c=c: e.scalar_tensor_tensor(
                                out=Sst[:, hd, :], in0=Sst[:, hd, :], scalar=EB[:, hd, c:c + 1], in1=pd[:, j * 128:(j + 1) * 128],
                                op0=ALU.mult, op1=ALU.add), reads=[r_pd, r_Sg[hd], r_E[hd]], writes=[r_Sg[hd]])
                        S.op("act", lambda e, hh=hh, sp_nxt=sp_nxt: e.activation(
                            out=Sp[sp_nxt][:, 4 * hh:4 * hh + 4, :], in_=Sst[:, 4 * hh:4 * hh + 4, :], func=AF.Copy),
                            reads=r_Sg[4 * hh:4 * hh + 4], writes=[r_Sp[sp_nxt][hh]])
                    po3 = po[:].rearrange("p (a b) -> p a b", a=8)
                    if fwd:
                        S.op("act", lambda e, po3=po3, c=c: e.activation(out=o_sb[:, :, c * 64:(c + 1) * 64], in_=po3, func=AF.Copy),
                             reads=[r_po], writes=r_osb)
                    else:
                        S.op("dve", lambda e, po3=po3, c=c, kk=kk: e.tensor_tensor(
                            out=o_sb[:, :, c * 64:(c + 1) * 64], in0=po3, in1=tmpn[kk][:, 0:512].rearrange("p (a b) -> p a b", a=8), op=ALU.add),
                            reads=[r_po, r_tmpn[kk]], writes=r_osb)
                        if cn + 2 < 8:
                            load_of_chunk(cn + 2)
                if fwd:
                    S.op("pool", lambda e, i=i: e.dma_start(out=ofw[:, :, T * i:T * (i + 1)], in_=o_sb),
                         reads=r_osb, writes=[r_ofw[i]], dma=True, nodep=True)
                    continue
                sg = kT
                S.op("sp", lambda e, i=i: e.dma_start(out=kT, in_=gsc[:, :, T * i:T * (i + 1)]), reads=[r_gsc[i]], writes=r_kT, dma=True)
                for hd in range(8):
                    k = hd % 2
                    S.op("act", lambda e, hd=hd, k=k: e.activation(out=sqb[k][:, 0:T], in_=o_sb[:, hd, :], func=AF.Square),
                         reads=r_osb, writes=[r_sqb[k]])
                    pb, r_pb = nb()
                    S.op("pe", lambda e, pb=pb, k=k: e.matmul(pb[:], lhsT=ones[:], rhs=sqb[k][:, 0:T], start=True, stop=True),
                         reads=[r_sqb[k], r_const], writes=[r_pb])
                    S.op("act", lambda e, pb=pb, k=k: e.activation(out=tmpn[k][:, 0:T], in_=pb[:], func=AF.Ln, scale=1.0 / 128, bias=EPS),
                         reads=[r_pb], writes=[r_tmpn[k]])
                    S.op("act", lambda e, k=k: e.activation(out=tmpn[k][:, 0:T], in_=tmpn[k][:, 0:T], func=AF.Exp, scale=-0.5),
                         reads=[r_tmpn[k]], writes=[r_tmpn[k]])
                    S.op("dve", lambda e, hd=hd, k=k: e.tensor_tensor(out=tmpn[k][:, 0:T], in0=o_sb[:, hd, :], in1=tmpn[k][:, 0:T], op=ALU.mult),
                         reads=r_osb + [r_tmpn[k]], writes=[r_tmpn[k]])
                    S.op("dve", lambda e, hd=hd, k=k: e.scalar_tensor_tensor(
                        out=h[:, hd, 0:T], in0=tmpn[k][:, 0:T], scalar=vcol(V_GNG, hd), in1=sg[:, hd, :], op0=ALU.mult, op1=ALU.mult),
                        reads=[r_tmpn[k], r_vecs, r_kT[hd]] + r_h, writes=[r_h[hd]])
                wout_residual(1, b, r_x, G_HOUT)
                ghook = mhook = None
                if n + 1 < NT:
                    inext = order[n + 1]
                    load_bwd_operands(inext)
                    stn = gate_pipeline(False, None, None, ztile=inext)

                    def ghook(g, stn=stn):
                        if g == 0:
                            stn(-2); stn(-1); stn(0)
                        else:
                            stn(g)

                    def mhook(stn=stn):
                        stn(8); stn(9)
                mlp(1, b, r_x, G_W1_1, G_W2_1, mid_hook=mhook, group_hook=ghook)
                outb = arena[:, 0:8192].bitcast(F32).rearrange("p (c t) -> p c t", c=8)
                rms_norm(xc, r_x, T, lambda c: vcol(V_FG, c), None,
                         lambda c: outb[:, c, :], lambda c: r_ar[2 * c], out_final=True)
                S.op("pool", lambda e, i=i: e.dma_start(out=outT[:, :, T * i:T * (i + 1)], in_=outb),
                     reads=r_ar[0:16], writes=[r_out[i]], dma=True, nodep=True)

        scan_pass(0)
        S.barrier(lambda e: e.memset(ED[:], 0.0))
        S.op("pool", lambda e: e.dma_start(out=cc_in, in_=Sst[:].rearrange("p a b -> p (a b)")), reads=r_Sg, writes=[r_ccin], dma=True)
        S.op("pool", lambda e: e.collective_compute("AllGather", ALU.bypass, replica_groups=[[0, 1], [2, 3], [4, 5], [6, 7]],
                                                     ins=[cc_in], outs=[cc_out]), reads=[r_ccin], writes=[r_ccout])
        Sx = arena[:, 0:4096].bitcast(F32).rearrange("p (r f) -> p r f", r=2)
        r_Sx = r_ar[0:8]
        S.op("pool", lambda e: e.dma_start(out=Sx, in_=cc_out.rearrange("(r p) f -> p r f", p=128)),
             reads=[r_ccout], writes=r_Sx, dma=True)
        Sflat = Sst[:].rearrange("p a b -> p (a b)")
        S.op("dve", lambda e: e.tensor_scalar(out=Sflat, in0=Sx[:, 0, :], scalar1=vcol(V_FLAG, 0), scalar2=None, op0=ALU.mult),
             reads=r_Sx + [r_vecs], writes=r_Sg)
        S.op("dve", lambda e: e.scalar_tensor_tensor(out=Sflat, in0=Sx[:, 1, :], scalar=vcol(V_FLAG, 1), in1=Sflat,
                                                      op0=ALU.mult, op1=ALU.add), reads=r_Sx + [r_vecs] + r_Sg, writes=r_Sg)
        nxt = state["cn"] % 2
        S.op("act", lambda e: e.activation(out=Sp[nxt][:], in_=Sst[:], func=AF.Copy), reads=r_Sg, writes=r_Sp[nxt])
        scan_pass(1)
        S.emit(final_waits=[r_out[0]])
    return nc


_NC = None


def _fm(v):
    return np.ascontiguousarray(np.asarray(v, np.float32).reshape(8, 128).T)


def kernel(x, c, norm1_g, norm2_g, ada_w, ada_b, mlp_w1, mlp_w2, conv_w_in, conv_dw_w, conv_dw_b,
           conv_ln_g, conv_ln_b, conv_w_out, hgrn_w_in, hgrn_lb_logits, hgrn_gn_g, hgrn_w_out, final_g):
    global _NC
    f = lambda a: np.ascontiguousarray(np.asarray(a, np.float32))
    x = f(x); c = f(c)
    ada_w = f(ada_w); mlp_w1 = f(mlp_w1); mlp_w2 = f(mlp_w2)
    cin = f(conv_w_in)[0]; cout = f(conv_w_out)[0]; hout = f(hgrn_w_out)[0]
    hin = f(hgrn_w_in)[0]
    hin_sw = np.ascontiguousarray(np.concatenate(
        [hin[:, 0:1024], hin[:, 2048:3072], hin[:, 1024:2048], hin[:, 3072:]], axis=1))
    dw = f(conv_dw_w)[0]
    cm = np.zeros((128, NCM), np.float32)
    cm[:, CM_ID:CM_ID + 128] = np.eye(128, dtype=np.float32)
    s_ = np.arange(64)[:, None]; t_ = np.arange(64)[None, :]
    fe = np.zeros((128, 64), np.float32); fe[:64] = (s_ <= t_)
    fo = np.zeros((128, 64), np.float32); fo[64:] = (s_ <= t_)
    be = np.zeros((128, 64), np.float32); be[:64] = (s_ >= t_)
    bo = np.zeros((128, 64), np.float32); bo[64:] = (s_ >= t_)
    cm[:, CM_MASK:CM_MASK + 256] = np.concatenate([fe, fo, be, bo], axis=1)
    sm = np.ones(512, np.float32); sm[::64] = 0.0
    cm[:, CM_SCAN:CM_SCAN + 512] = sm[None, :]
    in_maps = []
    for r in range(8):
        b, half = r // 2, r % 2
        if half == 0:
            xs = x[b, 0:LTOK + 16]
        else:
            xs = x[b, ::-1][0:LTOK + 16]
        xTr = np.zeros((128, 8, XW), np.float32)
        xTr[:, :, 16:16 + LTOK + 16] = xs.T.reshape(8, 128, LTOK + 16).transpose(1, 0, 2)
        vv = np.zeros((128, NV), np.float32)
        for l in range(2):
            vv[:, V_N1G + 8 * l:V_N1G + 8 * l + 8] = _fm(norm1_g[l])
            vv[:, V_N2G + 8 * l:V_N2G + 8 * l + 8] = _fm(norm2_g[l])
            vv[:, V_LBL + 8 * l:V_LBL + 8 * l + 8] = _fm(hgrn_lb_logits[l])
            vv[:, V_ADAB + 48 * l:V_ADAB + 48 * l + 48] = np.asarray(ada_b[l], np.float32).reshape(48, 128).T
        vv[:, V_FG:V_FG + 8] = _fm(final_g)
        vv[:, V_DWB:V_DWB + 8] = _fm(conv_dw_b[0])
        vv[:, V_LNG:V_LNG + 8] = _fm(conv_ln_g[0])
        vv[:, V_LNB:V_LNB + 8] = _fm(conv_ln_b[0])
        vv[:, V_GNG:V_GNG + 8] = _fm(hgrn_gn_g[0])
        vv[:, V_FLAG:V_FLAG + 2] = np.array([0.0, 1.0] if half == 0 else [1.0, 0.0], np.float32)[None, :]
        vv[:, V_C:V_C + 8] = _fm(c[b])
        dwr = dw if half == 0 else dw[::-1]
        vv[:, V_DW:V_DW + 248] = dwr.T.reshape(8, 128, 31).transpose(1, 0, 2).reshape(128, 248)
        in_maps.append({
            "xT": xTr, "vecs": vv, "cmat": cm, "ada_w": ada_w, "mlp_w1": mlp_w1, "mlp_w2": mlp_w2,
            "conv_w_in": cin, "conv_w_out": cout, "hgrn_w_in": hin if half == 0 else hin_sw, "hgrn_w_out": hout,
        })
    if _NC is None:
        _NC = build_nc()
    res = run_bass_kernel_spmd(_NC, in_maps, core_ids=list(range(8)))
    out = np.empty((4, 8192, 1024), np.float32)
    for r in range(8):
        b, half = r // 2, r % 2
        o = res.results[r]["outT"]
        tok = o.transpose(2, 1, 0).reshape(LTOK, 1024)
        if half == 0:
            out[b, 0:LTOK] = tok
        else:
            out[b, LTOK:] = tok[::-1]
    return out
```

```python
import contextlib
import numpy as np
import concourse.bass as bass
import concourse.mybir as mybir
from concourse.bass_utils import run_bass_kernel_spmd

F32 = mybir.dt.float32
BF16 = mybir.dt.bfloat16
F32R = mybir.dt.float32r
AF = mybir.ActivationFunctionType
ALU = mybir.AluOpType

ENGS = ["pe", "act", "dve", "pool", "sp"]
D = 1024
NT = 8
T = 512
TE = T + 30
LTOK = NT * T
XW = LTOK + 32
EPS = 1e-6
USE_PE_CONV = True

V_N1G, V_N2G, V_FG, V_DWB, V_LNG, V_LNB, V_LBL, V_GNG, V_FLAG, V_C, V_ADAB, V_DW = (
    0, 16, 32, 40, 48, 56, 64, 80, 88, 90, 98, 194)
NV = 194 + 248
CM_ID, CM_MASK, CM_SCAN, NCM = 0, 128, 384, 896

G_CIN, G_COUT, G_W1_0, G_W2_0, G_HIN, G_HOUT, G_W1_1, G_W2_1, NG = 0, 4, 6, 14, 22, 32, 34, 42, 50


class Res:
    __slots__ = ("name", "writer", "readers", "dma_sem", "dma_cnt")

    def __init__(self, name):
        self.name = name
        self.writer = None
        self.readers = []
        self.dma_sem = None
        self.dma_cnt = 0


class Sched:
    def __init__(self, nc, stack):
        self.nc = nc
        self.stack = stack
        self.ops = {e: [] for e in ENGS}
        self.eng_sem = {e: stack.enter_context(nc.semaphore("s_" + e)) for e in ENGS}
        self.n_res = 0
        self.dma_res = []
        self.bar = None

    def barrier(self, fn):
        deps = []
        for e in ENGS:
            for i in range(len(self.ops[e]) - 1, -1, -1):
                if self.ops[e][i]["dma"] is None:
                    deps.append(("eng", e, i))
                    break
        for r in self.dma_res:
            deps.append(("dma", r, r.dma_cnt))
        idx = len(self.ops["pool"])
        self.ops["pool"].append(dict(fn=fn, deps=deps, signal=False, dma=None))
        self.bar = ("eng", "pool", idx)

    def res(self, name=None):
        self.n_res += 1
        return Res(name or ("r%d" % self.n_res))

    def op(self, eng, fn, reads=(), writes=(), dma=False, nodep=False):
        deps = []
        for r in reads:
            if r.writer is not None:
                deps.append(r.writer)
        for r in writes:
            if nodep:
                continue
            if r.writer is not None:
                deps.append(r.writer)
            deps.extend(r.readers)
        if self.bar is not None:
            deps.append(self.bar)
        idx = len(self.ops[eng])
        rec = dict(fn=fn, deps=deps, signal=False, dma=None)
        if dma:
            r0 = writes[0]
            if r0.dma_sem is None:
                r0.dma_sem = self.stack.enter_context(self.nc.semaphore("d%d" % self.n_res + r0.name))
                self.n_res += 1
                self.dma_res.append(r0)
            r0.dma_cnt += 1
            rec["dma"] = (r0, r0.dma_cnt)
            me = ("dma", r0, r0.dma_cnt)
        else:
            me = ("eng", eng, idx)
        self.ops[eng].append(rec)
        for r in reads:
            r.readers.append(me)
        for r in writes:
            r.writer = me
            r.readers = []
        return me

    def emit(self, final_waits=()):
        nc = self.nc
        for e in ENGS:
            for i, rec in enumerate(self.ops[e]):
                for d in rec["deps"]:
                    if d[0] == "eng":
                        _, de, di = d
                        if de == "pe" and e == "pe":
                            continue
                        self.ops[de][di]["signal"] = True
        cnt = {}
        for e in ENGS:
            c = 0
            for i, rec in enumerate(self.ops[e]):
                if rec["signal"]:
                    c += 1
                    cnt[(e, i)] = c
        handles = {"pe": "tensor", "act": "scalar", "dve": "vector", "pool": "gpsimd", "sp": "sync"}
        with nc.Block() as block:
            for e in ENGS:
                def body(eng_h, e=e):
                    waited = {}
                    for i, rec in enumerate(self.ops[e]):
                        need = {}
                        for d in rec["deps"]:
                            if d[0] == "eng":
                                _, de, di = d
                                if de == "pe" and e == "pe":
                                    continue
                                key = ("e", de)
                                val = cnt[(de, di)]
                                sem = self.eng_sem[de]
                            else:
                                _, r, c = d
                                key = ("d", id(r))
                                val = 16 * c
                                sem = r.dma_sem
                            if need.get(key, (0, None))[0] < val:
                                need[key] = (val, sem)
                        for key, (val, sem) in need.items():
                            if waited.get(key, 0) >= val:
                                continue
                            waited[key] = val
                            eng_h.wait_ge(sem, val)
                        inst = rec["fn"](eng_h)
                        if rec["dma"] is not None:
                            inst.then_inc(rec["dma"][0].dma_sem, 16)
                        elif rec["signal"]:
                            inst.then_inc(self.eng_sem[e], 1)
                    if e == "sp":
                        for r in final_waits:
                            eng_h.wait_ge(r.dma_sem, 16 * r.dma_cnt)
                getattr(block, handles[e])(body)


def build_nc(debug=False):
    nc = bass.Bass("TRN2", target_bir_lowering=False)

    def din(name, shape):
        return nc.dram_tensor(name, shape, F32, kind="ExternalInput").ap()

    xT = din("xT", [128, 8, XW])
    vecs_d = din("vecs", [128, NV])
    cmat_d = din("cmat", [128, NCM])
    ada_w = din("ada_w", [2, D, 6 * D])
    w1_d = din("mlp_w1", [2, D, 4 * D])
    w2_d = din("mlp_w2", [2, 4 * D, D])
    cin_d = din("conv_w_in", [D, 2 * D])
    cout_d = din("conv_w_out", [D, D])
    hin_d = din("hgrn_w_in", [D, 5 * D])
    hout_d = din("hgrn_w_out", [D, D])
    outT = nc.dram_tensor("outT", [128, 8, LTOK], F32, kind="ExternalOutput").ap()

    wsc = nc.dram_tensor("wsc", [NG, 128, 4096], BF16).ap()
    x1s = nc.dram_tensor("x1s", [128, 8, LTOK], F32).ap()
    ofw = nc.dram_tensor("ofw", [128, 8, LTOK], F32).ap()
    qsc = nc.dram_tensor("qsc", [128, 8, LTOK], BF16).ap()
    vsc = nc.dram_tensor("vsc", [NT, 128, 4096], BF16).ap()
    gsc = nc.dram_tensor("gsc", [128, 8, LTOK], BF16).ap()
    zsc = nc.dram_tensor("zsc", [128, 8, LTOK], F32).ap()
    cc_in = nc.dram_tensor("cc_in", [128, 1024], F32).ap()
    cc_out = nc.dram_tensor("cc_out", [256, 1024], F32).ap()

    with contextlib.ExitStack() as st:
        S = Sched(nc, st)

        def sb(name, shape, dt):
            return st.enter_context(nc.sbuf_tensor("sb_" + name, shape, dt))

        NSLOT = 4
        slots = [sb("slot%d" % i, [128, 4096], BF16) for i in range(NSLOT)]
        r_slots = [S.res("slot%d" % i) for i in range(NSLOT)]
        r_slots_sw = [S.res("slotsw%d" % i) for i in range(NSLOT)]
        xt = [sb("xt%d" % i, [128, 8, TE], F32) for i in range(2)]
        r_xt = [S.res("xt%d" % i) for i in range(2)]
        h = sb("h", [128, 8, TE], BF16)
        r_h = [S.res("h%d" % c) for c in range(8)]
        sqb = [sb("sqb%d" % i, [128, TE], BF16) for i in range(2)]
        r_sqb = [S.res("sqb%d" % i) for i in range(2)]
        vA = sb("vA", [128, TE], F32); r_vA = S.res("vA")
        vB = sb("vB", [128, TE], F32); r_vB = S.res("vB")
        vC = sb("vC", [128, TE], F32); r_vC = S.res("vC")
        tmpn = [sb("tmpn%d" % i, [128, TE], F32) for i in range(2)]
        r_tmpn = [S.res("tmpn%d" % i) for i in range(2)]
        arena = sb("arena", [128, 16384], BF16)
        r_ar = [S.res("ar%d" % i) for i in range(32)]
        rt = [sb("rt%d" % i, [128, T], BF16) for i in range(2)]
        r_rt = [S.res("rt%d" % i) for i in range(2)]
        vecs = sb("vecs", [128, NV], F32); r_vecs = S.res("vecs")
        cmat = sb("cmat", [128, NCM], F32); r_cmat = S.res("cmat")
        ident = sb("ident", [128, 128], BF16)
        ones = sb("ones", [128, 128], BF16)
        r_const = S.res("const")
        mod = sb("mod", [128, 96], F32); r_mod = S.res("mod")
        cond = sb("cond", [128, 8], F32); r_cond = S.res("cond")
        cond_bf = sb("cond_bf", [128, 8], BF16)
        gv = sb("gv", [128, 2, 16], F32)
        lbv = sb("lbv", [128, 3, 8], F32)
        r_gv = S.res("gv")
        un = sb("un", [128, 17408], BF16)
        u = un[:, 0:4336].rearrange("p (c t) -> p c t", c=8); r_u = [S.res("u%d" % c) for c in range(8)]
        a_sb = un[:, 4336:6504].rearrange("p (c t) -> p c t", c=4); r_asb = [S.res("asb%d" % c) for c in range(4)]
        sgt = [un[:, 6504 + 1084 * i: 6504 + 1084 * (i + 1)].bitcast(F32) for i in range(2)]
        r_sgt = [S.res("sgt%d" % i) for i in range(2)]
        dg = [un[:, 8704 + 3968 * i: 8704 + 3968 * (i + 1)].rearrange("p (a b) -> p a b", a=31) for i in range(2)]
        r_dg = [S.res("dg%d" % i) for i in range(2)]
        qs = un[:, 0:4096].rearrange("p (c t) -> p c t", c=8); r_qs = [S.res("qs%d" % c) for c in range(8)]
        kT = un[:, 4096:8192].rearrange("p (c t) -> p c t", c=8); r_kT = [S.res("kT%d" % c) for c in range(8)]
        ktm = un[:, 8192:12288].rearrange("p (a b c) -> p a b c", a=4, b=8); r_ktm = [S.res("ktm%d" % c) for c in range(4)]
        vtm = un[:, 12288:16384].rearrange("p (a b) -> p a b", a=4); r_vtm = [S.res("vtm%d" % c) for c in range(4)]
        NGT, NSET = 4, 3
        gtb = [[sb("gt%d_%d" % (q, i), [128, T], F32) for i in range(NGT)] for q in range(NSET)]
        r_gtb = [[S.res("gt%d_%d" % (q, i)) for i in range(NGT)] for q in range(NSET)]
        Bcol = sb("Bcol", [128, 8, 8, 2], F32)
        Sst = sb("Sst", [128, 8, 128], F32); r_Sg = [S.res("S%d" % c) for c in range(8)]
        Sp = [sb("Sp%d" % i, [128, 8, 128], BF16) for i in range(2)]
        r_Sp = [[S.res("Sp%d_%d" % (i, g)) for g in range(2)] for i in range(2)]
        qh = sb("qh", [128, 8, T], BF16); r_qh = [S.res("qh%d" % c) for c in range(8)]
        kh = [sb("kh%d" % i, [128, T], BF16) for i in range(2)]; r_kh = [S.res("kh%d" % i) for i in range(2)]
        scT = [sb("scT%d" % i, [128, 8, 64], BF16) for i in range(2)]
        r_scT = [S.res("scT%d" % i) for i in range(2)]
        EA = sb("EA", [128, 8, 8], F32); EB = sb("EB", [128, 8, 8], F32); EC = sb("EC", [128, 8, 8], F32)
        r_E = [S.res("E%d" % c) for c in range(8)]
        ED = sb("ED", [128, 8, 8], F32)

        banks = [st.enter_context(nc.psum_tensor("pb%d" % i, [128, 512], F32)) for i in range(8)]
        r_banks = [S.res("pb%d" % i) for i in range(8)]
        state = dict(bank=0, slot=0)

        def nb():
            i = state["bank"]
            state["bank"] = (i + 1) % 8
            return banks[i], r_banks[i]

        _mats = [(G_CIN, 4), (G_COUT, 2), (G_W1_0, 8), (G_W2_0, 8), (G_HIN, 10), (G_HOUT, 2), (G_W1_1, 8), (G_W2_1, 8)]
        r_wsc = [None] * NG
        for (g0, n) in _mats:
            rm = S.res("wsc%d" % g0)
            for g in range(g0, g0 + n):
                r_wsc[g] = rm
        r_x1s = [S.res("x1s")] * NT
        r_ofw = [S.res("ofw")] * NT
        r_out = [S.res("out")] * NT
        r_ccin = S.res("ccin"); r_ccout = S.res("ccout")
        r_gsc = [S.res("gsc")] * NT
        r_zsc = [S.res("zsc")] * NT
        r_stgdma = [S.res("stgdma%d" % q) for q in range(3)]
        r_qsc = [S.res("qsc")] * NT
        r_vsc = [S.res("vsc")] * NT

        def wg(g):
            i = state["slot"]
            state["slot"] = (i + 1) % NSLOT
            S.op("sp", lambda e, i=i, g=g: e.dma_start(out=slots[i][:], in_=wsc[g]),
                 reads=[r_wsc[g]], writes=[r_slots[i]], dma=True)
            return slots[i], r_slots[i]

        def vcol(base, c, n=1):
            return vecs[:, base + c: base + c + n]

        S.op("sp", lambda e: e.dma_start(out=vecs[:], in_=vecs_d), writes=[r_vecs], dma=True)
        S.op("sp", lambda e: e.dma_start(out=cmat[:], in_=cmat_d), writes=[r_cmat], dma=True)
        S.op("dve", lambda e: e.tensor_copy(out=ident[:], in_=cmat[:, CM_ID:CM_ID + 128]), reads=[r_cmat], writes=[r_const])
        S.op("dve", lambda e: e.memset(ones[:], 1.0), writes=[r_const])
        masks = cmat[:, CM_MASK:CM_MASK + 256].rearrange("p (a b) -> p a b", a=4)
        scanmask = cmat[:, CM_SCAN:CM_SCAN + 512]

        def conv_k1024(g, src, col0):
            S.op("pool", lambda e: e.dma_start(
                out=wsc[g].rearrange("p (k n) -> p k n", k=8),
                in_=src[:, col0:col0 + 512].rearrange("(k p) n -> p k n", p=128)),
                writes=[r_wsc[g]], dma=True, nodep=True)

        def conv_w2(g, src, row0):
            S.op("pool", lambda e: e.dma_start(
                out=wsc[g].rearrange("p (k n) -> p k n", k=4),
                in_=src[row0:row0 + 512, :].rearrange("(k p) n -> p k n", p=128)),
                writes=[r_wsc[g]], dma=True, nodep=True)

        def convert_layer0():
            for j in range(4):
                conv_k1024(G_CIN + j, cin_d, 512 * j)
            for j in range(2):
                conv_k1024(G_COUT + j, cout_d, 512 * j)
            for j in range(8):
                conv_k1024(G_W1_0 + j, w1_d[0], 512 * j)
            for j in range(8):
                conv_w2(G_W2_0 + j, w2_d[0], 512 * j)

        l1_convs = ([lambda j=j: conv_k1024(G_HIN + j, hin_d, 512 * j) for j in range(10)]
                    + [lambda j=j: conv_k1024(G_HOUT + j, hout_d, 512 * j) for j in range(2)]
                    + [lambda j=j: conv_k1024(G_W1_1 + j, w1_d[1], 512 * j) for j in range(8)]
                    + [lambda j=j: conv_w2(G_W2_1 + j, w2_d[1], 512 * j) for j in range(8)])

        S.op("sp", lambda e: e.dma_start(out=xt[0][:], in_=xT[:, :, 1:1 + TE]), writes=[r_xt[0]], dma=True)

        S.op("act", lambda e: e.activation(out=cond_bf[:], in_=vcol(V_C, 0, 8), func=AF.Silu), reads=[r_vecs], writes=[r_cond])
        def ada_batch(l, k):
            mbank, r_mbank = nb()
            for g2 in range(2):
                grp = 2 * k + g2
                i = state["slot"]
                state["slot"] = (i + 1) % NSLOT
                sl3 = slots[i][:].rearrange("p (k n) -> p k n", k=8)
                r_sl = r_slots[i]
                S.op("pool", lambda e, grp=grp, sl3=sl3: e.dma_start(
                    out=sl3, in_=ada_w[l][:, grp * 512:(grp + 1) * 512].rearrange("(k p) n -> p k n", p=128)),
                    writes=[r_slots_sw[i], r_sl], dma=True)
                for j in range(4):
                    ch = g2 * 4 + j
                    for kc in range(8):
                        S.op("pe", lambda e, sl3=sl3, j=j, kc=kc, ch=ch: e.matmul(
                            mbank[:, ch:ch + 1], lhsT=sl3[:, kc, j * 128:(j + 1) * 128], rhs=cond_bf[:, kc:kc + 1],
                            start=(kc == 0), stop=(kc == 7)),
                            reads=[r_sl, r_cond], writes=[r_mbank])
            c0 = l * 48 + k * 8
            S.op("dve", lambda e: e.tensor_tensor(out=mod[:, c0:c0 + 8], in0=mbank[:, 0:8], in1=vcol(V_ADAB, c0, 8), op=ALU.add),
                 reads=[r_mbank, r_vecs], writes=[r_mod])
            if k == 1:
                S.op("dve", lambda e: e.scalar_tensor_tensor(
                    out=gv[:, l, 0:8], in0=mod[:, c0:c0 + 8], scalar=1.0, in1=vcol(V_N1G, l * 8, 8),
                    op0=ALU.add, op1=ALU.mult), reads=[r_mod, r_vecs], writes=[r_gv])
            if k == 4:
                S.op("dve", lambda e: e.scalar_tensor_tensor(
                    out=gv[:, l, 8:16], in0=mod[:, c0:c0 + 8], scalar=1.0, in1=vcol(V_N2G, l * 8, 8),
                    op0=ALU.add, op1=ALU.mult), reads=[r_mod, r_vecs], writes=[r_gv])

        for k in range(6):
            ada_batch(0, k)
        convert_layer0()
        S.op("dve", lambda e: e.tensor_tensor(out=lbv[:, 2, :], in0=vcol(V_LBL, 8, 8), in1=vcol(V_LBL, 0, 8), op=ALU.subtract),
             reads=[r_vecs], writes=[r_gv])
        S.op("act", lambda e: e.activation(out=lbv[:, 0, :], in_=lbv[:, 2, :], func=AF.Sigmoid), reads=[r_gv], writes=[r_gv])
        S.op("act", lambda e: e.activation(out=lbv[:, 1, :], in_=lbv[:, 0, :], func=AF.Ln, scale=-1.0, bias=1.0),
             reads=[r_gv], writes=[r_gv])

        def MOD(l, k, c):
            return mod[:, l * 48 + k * 8 + c: l * 48 + k * 8 + c + 1]

        def rms_norm(xsrc, r_x, W, gain, shift, out_fn, r_out_fn, out_final=False):
            segs = [(0, W)] if W <= 512 else [(0, W // 2), (W // 2, W)]
            pbs = [nb() for _ in segs]
            for c in range(8):
                k = c % 2
                xa = xsrc(c)
                S.op("act", lambda e, xa=xa, k=k: e.activation(out=sqb[k][:, 0:W], in_=xa, func=AF.Square),
                     reads=[r_x], writes=[r_sqb[k]])
                for (s0, s1), (pb, r_pb) in zip(segs, pbs):
                    S.op("pe", lambda e, c=c, k=k, s0=s0, s1=s1, pb=pb: e.matmul(
                        pb[:, 0:s1 - s0], lhsT=ones[:], rhs=sqb[k][:, s0:s1], start=(c == 0), stop=(c == 7)),
                        reads=[r_sqb[k], r_const], writes=[r_pb])
            for (s0, s1), (pb, r_pb) in zip(segs, pbs):
                S.op("act", lambda e, s0=s0, s1=s1, pb=pb: e.activation(
                    out=vA[:, s0:s1], in_=pb[:, 0:s1 - s0], func=AF.Ln, scale=1.0 / D, bias=EPS),
                    reads=[r_pb], writes=[r_vA])
            S.op("act", lambda e: e.activation(out=vB[:, 0:W], in_=vA[:, 0:W], func=AF.Exp, scale=-0.5),
                 reads=[r_vA], writes=[r_vB])
            for c in range(8):
                k = c % 2
                xa = xsrc(c); ga = gain(c); oa = out_fn(c); r_o = r_out_fn(c)
                if out_final:
                    S.op("dve", lambda e, xa=xa, ga=ga, oa=oa: e.scalar_tensor_tensor(
                        out=oa, in0=xa, scalar=ga, in1=vB[:, 0:W], op0=ALU.mult, op1=ALU.mult),
                        reads=[r_x, r_vB, r_gv, r_vecs], writes=[r_o])
                else:
                    sa = shift(c)
                    S.op("dve", lambda e, xa=xa, ga=ga, k=k: e.scalar_tensor_tensor(
                        out=tmpn[k][:, 0:W], in0=xa, scalar=ga, in1=vB[:, 0:W], op0=ALU.mult, op1=ALU.mult),
                        reads=[r_x, r_vB, r_gv], writes=[r_tmpn[k]])
                    S.op("act", lambda e, oa=oa, sa=sa, k=k: e.activation(
                        out=oa, in_=tmpn[k][:, 0:W], func=AF.Identity, scale=1.0, bias=sa),
                        reads=[r_tmpn[k], r_mod], writes=[r_o])

        def proj_fm(slot, r_slot, j, rhs_fn, r_rhs, W):
            segs = [(0, W)] if W <= 512 else [(0, W // 2), (W // 2, W)]
            outs = []
            for (s0, s1) in segs:
                pb, r_pb = nb()
                for kc in range(8):
                    S.op("pe", lambda e, kc=kc, s0=s0, s1=s1, pb=pb: e.matmul(
                        pb[:, 0:s1 - s0], lhsT=slot.rearrange("p (k n) -> p k n", k=8)[:, kc, j * 128:(j + 1) * 128],
                        rhs=rhs_fn(kc)[:, s0:s1], start=(kc == 0), stop=(kc == 7)),
                        reads=[r_slot] + r_rhs, writes=[r_pb])
                outs.append((pb, r_pb, s0, s1))
            return outs

        def mlp(l, xb, r_x, g_w1, g_w2, mid_hook=None, group_hook=None):
            xc = lambda c: xt[xb][:, c, 15:15 + T]
            rms_norm(xc, r_x, T, lambda c: gv[:, l, 8 + c:9 + c], lambda c: MOD(l, 3, c),
                     lambda c: h[:, c, 0:T], lambda c: r_h[c])
            aT = lambda fc: arena[:, fc * 512:(fc + 1) * 512]
            for g in range(8):
                slot, r_slot = wg(g_w1 + g)
                for j in range(4):
                    fc = 4 * g + j
                    (pb, r_pb, _, _), = proj_fm(slot[:], r_slot, j, lambda kc: h[:, kc, 0:T], r_h, T)
                    k = fc % 2
                    S.op("act", lambda e, pb=pb, k=k: e.activation(out=rt[k][:], in_=pb[:], func=AF.Relu),
                         reads=[r_pb], writes=[r_rt[k]])
                    S.op("dve" if group_hook is not None else "pool",
                         lambda e, fc=fc, k=k: e.tensor_tensor(out=aT(fc), in0=rt[k][:], in1=rt[k][:], op=ALU.mult),
                         reads=[r_rt[k]], writes=[r_ar[fc]])
                if group_hook is not None:
                    group_hook(g)
            if mid_hook is not None:
                mid_hook()
            for g in range(8):
                slot, r_slot = wg(g_w2 + g)
                s3 = slot[:].rearrange("p (k n) -> p k n", k=4)
                for oc in range(8):
                    for fcl in range(4):
                        fc = 4 * g + fcl
                        S.op("pe", lambda e, s3=s3, oc=oc, fcl=fcl, fc=fc, g=g: e.matmul(
                            banks[oc][:], lhsT=s3[:, fcl, oc * 128:(oc + 1) * 128], rhs=aT(fc),
                            start=(g == 0 and fcl == 0), stop=(g == 7 and fcl == 3)),
                            reads=[r_slot, r_ar[fc]], writes=[r_banks[oc]])
            for oc in range(8):
                S.op("dve", lambda e, oc=oc: e.scalar_tensor_tensor(
                    out=xc(oc), in0=banks[oc][:], scalar=MOD(l, 5, oc), in1=xc(oc), op0=ALU.mult, op1=ALU.add),
                    reads=[r_banks[oc], r_mod, r_x], writes=[r_x])

        def wout_residual(l, xb, r_x, g_wout):
            xc = lambda c: xt[xb][:, c, 15:15 + T]
            for half in range(2):
                slot, r_slot = wg(g_wout + half)
                for j in range(4):
                    oc = 4 * half + j
                    (pb, r_pb, _, _), = proj_fm(slot[:], r_slot, j, lambda kc: h[:, kc, 0:T], r_h, T)
                    S.op("dve", lambda e, pb=pb, oc=oc: e.scalar_tensor_tensor(
                        out=xc(oc), in0=pb[:], scalar=MOD(l, 2, oc), in1=xc(oc), op0=ALU.mult, op1=ALU.add),
                        reads=[r_pb, r_mod, r_x], writes=[r_x])

        def load_x0(i):
            b = i % 2
            S.op("sp", lambda e: e.dma_start(out=xt[b][:], in_=xT[:, :, 512 * i + 1: 512 * i + 1 + TE]),
                 writes=[r_xt[b]], dma=True)

        y32 = arena[:, 0:8192].bitcast(F32).rearrange("p (c t) -> p c t", c=8)
        ybf = arena[:, 8192:12288].rearrange("p (c t) -> p c t", c=8)
        ysq = arena[:, 12288:16384].rearrange("p (c t) -> p c t", c=8)
        r_y32 = lambda c: [r_ar[2 * c], r_ar[2 * c + 1]]

        def l0_mixer(i):
            b = i % 2
            r_x = r_xt[b]
            rms_norm(lambda c: xt[b][:, c, :], r_x, TE, lambda c: gv[:, 0, c:c + 1], lambda c: MOD(0, 0, c),
                     lambda c: h[:, c, :], lambda c: r_h[c])
            for half in range(2):
                slA, r_slA = wg(G_CIN + half)
                slG, r_slG = wg(G_CIN + 2 + half)
                for j in range(4):
                    for (pb, r_pb, s0, s1) in proj_fm(slA[:], r_slA, j, lambda kc: h[:, kc, :], r_h, TE):
                        S.op("act", lambda e, pb=pb, s0=s0, s1=s1, j=j: e.activation(
                            out=a_sb[:, j, s0:s1], in_=pb[:, 0:s1 - s0], func=AF.Copy), reads=[r_pb], writes=[r_asb[j]])
                for j in range(4):
                    c = 4 * half + j
                    k = c % 2
                    for (pb, r_pb, s0, s1) in proj_fm(slG[:], r_slG, j, lambda kc: h[:, kc, :], r_h, TE):
                        S.op("act", lambda e, pb=pb, s0=s0, s1=s1, k=k: e.activation(
                            out=sgt[k][:, s0:s1], in_=pb[:, 0:s1 - s0], func=AF.Sigmoid), reads=[r_pb], writes=[r_sgt[k]])
                    S.op("pool", lambda e, c=c, j=j, k=k: e.tensor_tensor(
                        out=u[:, c, :], in0=a_sb[:, j, :], in1=sgt[k], op=ALU.mult),
                        reads=[r_asb[j], r_sgt[k]], writes=[r_u[c]])
                    if i == 0:
                        S.op("pool", lambda e, c=c: e.memset(u[:, c, 0:15], 0.0), writes=[r_u[c]])
            if USE_PE_CONV:
                def build_diag(c):
                    par = c % 2
                    S.op("dve", lambda e: e.tensor_tensor(
                        out=dg[par], in0=ident[:].unsqueeze(1).broadcast_to([128, 31, 128]),
                        in1=vecs[:, V_DW + c * 31: V_DW + c * 31 + 31].unsqueeze(2).broadcast_to([128, 31, 128]), op=ALU.mult),
                        reads=[r_const, r_vecs], writes=[r_dg[par]])

                build_diag(0)
                build_diag(1)
                for c in range(8):
                    par = c % 2
                    pb, r_pb = nb()
                    for tap in range(31):
                        S.op("pe", lambda e, pb=pb, par=par, tap=tap, c=c: e.matmul(
                            pb[:], lhsT=dg[par][:, tap, :], rhs=u[:, c, tap:tap + T], start=(tap == 0), stop=(tap == 30)),
                            reads=[r_dg[par], r_u[c]], writes=[r_pb])
                    if c + 2 < 8:
                        build_diag(c + 2)
                    S.op("act", lambda e, pb=pb, c=c: e.activation(out=y32[:, c, :], in_=pb[:], func=AF.Identity, scale=1.0, bias=vcol(V_DWB, c)),
                         reads=[r_pb, r_vecs], writes=r_y32(c))
                    S.op("act", lambda e, pb=pb, c=c: e.activation(out=ybf[:, c, :], in_=pb[:], func=AF.Identity, scale=1.0, bias=vcol(V_DWB, c)),
                         reads=[r_pb, r_vecs], writes=[r_ar[16 + c]])
                    S.op("pool", lambda e, c=c: e.tensor_tensor(out=ysq[:, c, :], in0=y32[:, c, :], in1=y32[:, c, :], op=ALU.mult),
                         reads=r_y32(c), writes=[r_ar[24 + c]])
            else:
                for tap in range(31):
                    for c in range(8):
                        if tap == 0:
                            S.op("dve", lambda e, c=c: e.tensor_scalar(
                                out=y32[:, c, :], in0=u[:, c, 0:T], scalar1=vcol(V_DW, c * 31), scalar2=vcol(V_DWB, c),
                                op0=ALU.mult, op1=ALU.add), reads=[r_u[c], r_vecs], writes=r_y32(c))
                        else:
                            S.op("dve", lambda e, c=c, tap=tap: e.scalar_tensor_tensor(
                                out=y32[:, c, :], in0=u[:, c, tap:tap + T], scalar=vcol(V_DW, c * 31 + tap), in1=y32[:, c, :],
                                op0=ALU.mult, op1=ALU.add), reads=[r_u[c]] + r_y32(c), writes=r_y32(c))
                for c in range(8):
                    S.op("act", lambda e, c=c: e.activation(out=ybf[:, c, :], in_=y32[:, c, :], func=AF.Copy),
                         reads=r_y32(c), writes=[r_ar[16 + c]])
                    S.op("pool", lambda e, c=c: e.tensor_tensor(out=ysq[:, c, :], in0=y32[:, c, :], in1=y32[:, c, :], op=ALU.mult),
                         reads=r_y32(c), writes=[r_ar[24 + c]])
            pbm, r_pbm = nb()
            pbv, r_pbv = nb()
            for c in range(8):
                S.op("pe", lambda e, c=c, pbm=pbm: e.matmul(pbm[:], lhsT=ones[:], rhs=ybf[:, c, :], start=(c == 0), stop=(c == 7)),
                     reads=[r_ar[16 + c], r_const], writes=[r_pbm])
                S.op("pe", lambda e, c=c, pbv=pbv: e.matmul(pbv[:], lhsT=ones[:], rhs=ysq[:, c, :], start=(c == 0), stop=(c == 7)),
                     reads=[r_ar[24 + c], r_const], writes=[r_pbv])
            if 1 <= i <= 6:
                ada_batch(1, i - 1)
            S.op("act", lambda e, pbm=pbm: e.activation(out=vA[:, 0:T], in_=pbm[:], func=AF.Copy, scale=1.0 / D), reads=[r_pbm], writes=[r_vA])
            S.op("dve", lambda e: e.tensor_tensor(out=vC[:, 0:T], in0=vA[:, 0:T], in1=vA[:, 0:T], op=ALU.mult), reads=[r_vA], writes=[r_vC])
            S.op("dve", lambda e, pbv=pbv: e.scalar_tensor_tensor(out=vC[:, 0:T], in0=pbv[:], scalar=1.0 / D, in1=vC[:, 0:T],
                                                          op0=ALU.mult, op1=ALU.subtract), reads=[r_pbv, r_vC], writes=[r_vC])
            S.op("act", lambda e: e.activation(out=vC[:, 0:T], in_=vC[:, 0:T], func=AF.Ln, scale=1.0, bias=EPS), reads=[r_vC], writes=[r_vC])
            S.op("act", lambda e: e.activation(out=vB[:, 0:T], in_=vC[:, 0:T], func=AF.Exp, scale=-0.5), reads=[r_vC], writes=[r_vB])
            for c in range(8):
                k = c % 2
                S.op("dve", lambda e, c=c, k=k: e.tensor_tensor(out=tmpn[k][:, 0:T], in0=y32[:, c, :], in1=vA[:, 0:T], op=ALU.subtract),
                     reads=r_y32(c) + [r_vA], writes=[r_tmpn[k]])
                S.op("dve", lambda e, k=k: e.tensor_tensor(out=tmpn[k][:, 0:T], in0=tmpn[k][:, 0:T], in1=vB[:, 0:T], op=ALU.mult),
                     reads=[r_tmpn[k], r_vB], writes=[r_tmpn[k]])
                S.op("act", lambda e, c=c, k=k: e.activation(out=h[:, c, 0:T], in_=tmpn[k][:, 0:T], func=AF.Silu,
                                                              scale=vcol(V_LNG, c), bias=vcol(V_LNB, c)),
                     reads=[r_tmpn[k], r_vecs], writes=[r_h[c]])
            wout_residual(0, b, r_x, G_COUT)

        def l0_mlp(i):
            b = i % 2
            r_x = r_xt[b]
            hook = None
            if i >= 1:
                hook = lambda i=i: [f() for f in l1_convs[4 * (i - 1):4 * i]]
            mlp(0, b, r_x, G_W1_0, G_W2_0, mid_hook=hook)
            S.op("pool", lambda e, i=i, b=b: e.dma_start(out=x1s[:, :, T * i:T * (i + 1)], in_=xt[b][:, :, 15:15 + T]),
                 reads=[r_x], writes=[r_x1s[i]], dma=True, nodep=True)

        load_x0(1)
        l0_mixer(0)
        l0_mixer(1)
        l0_mlp(0)
        load_x0(2)
        l0_mlp(1)
        load_x0(3)
        for i in range(2, NT):
            l0_mixer(i)
            l0_mlp(i)
            if i + 2 < NT:
                load_x0(i + 2)

        S.barrier(lambda e: e.memset(ED[:], 0.0))
        S.op("dve", lambda e: e.memset(Sst[:], 0.0), writes=r_Sg)
        S.op("dve", lambda e: e.memset(Sp[0][:], 0.0), writes=r_Sp[0])
        state["cn"] = 0

        def load_x1(i, b):
            S.op("sp", lambda e: e.dma_start(out=xt[b][:, :, 15:15 + T], in_=x1s[:, :, T * i:T * (i + 1)]),
                 reads=[r_x1s[i]], writes=[r_xt[b]], dma=True)

        o_sb = arena[:, 0:8192].bitcast(F32).rearrange("p (c t) -> p c t", c=8)
        of_sb = arena[:, 8192:16384].bitcast(F32).rearrange("p (c t) -> p c t", c=8)
        r_osb = r_ar[0:16]
        r_ofsb = r_ar[16:32]

        def gate_pipeline(fwd, zslots, hk, ztile=None):
            tstate = {}

            zstate = {}

            def Z(hd):
                if ztile is not None:
                    g_e = gtb[hd % NSET][0]
                    r_e = r_gtb[hd % NSET][0]
                    S.op("act", lambda e: e.dma_start(out=g_e[:], in_=zsc[:, hd, T * ztile:T * (ztile + 1)]),
                         reads=[r_zsc[ztile]], writes=[r_e], dma=True)
                    zstate[hd] = (g_e, r_e)
                    return
                slot, r_slot = zslots[hd // 4]
                (pz, r_pz, _, _), = proj_fm(slot[:], r_slot, hd % 4, hk, r_h, T)
                zstate[hd] = (pz, r_pz)

            def A1(hd):
                g_e, g_w, g_lf, g_B = gtb[hd % NSET]
                r_e, r_w, r_lf, r_B = r_gtb[hd % NSET]
                pz, r_pz = zstate[hd]
                lb_c = lbv[:, 0, hd:hd + 1]
                S.op("act", lambda e: e.activation(out=g_e[:], in_=pz[:], func=AF.Exp), reads=[r_pz, r_e], writes=[r_e])
                S.op("act", lambda e: e.activation(out=g_w[:], in_=g_e[:], func=AF.Ln, scale=1.0, bias=1.0), reads=[r_e], writes=[r_w])
                S.op("act", lambda e: e.activation(out=g_lf[:], in_=g_e[:], func=AF.Ln, scale=1.0, bias=lb_c), reads=[r_e, r_gv], writes=[r_lf])

            def D1(hd):
                g_e, g_w, g_lf, g_B = gtb[hd % NSET]
                r_e, r_w, r_lf, r_B = r_gtb[hd % NSET]
                B3 = g_B[:].rearrange("p (c t) -> p c t", t=64)
                S.op("pool", lambda e: e.tensor_tensor(out=g_lf[:], in0=g_lf[:], in1=g_w[:], op=ALU.subtract), reads=[r_lf, r_w], writes=[r_lf])
                S.op("dve", lambda e: e.tensor_tensor_scan(out=g_B[:], data0=scanmask, data1=g_lf[:], initial=0.0, op0=ALU.mult, op1=ALU.add),
                     reads=[r_cmat, r_lf], writes=[r_B])
                if fwd:
                    S.op("dve", lambda e: e.tensor_copy(out=Bcol[:, hd, :, :], in_=B3[:, :, 31::32]), reads=[r_B], writes=[r_E[hd]])
                    S.op("dve", lambda e: e.tensor_tensor(out=B3, in0=B3, in1=Bcol[:, hd, :, 0:1].broadcast_to([128, 8, 64]), op=ALU.subtract),
                         reads=[r_B, r_E[hd]], writes=[r_B])
                    S.op("dve", lambda e: e.tensor_tensor(out=g_w[:], in0=g_w[:], in1=g_B[:], op=ALU.add), reads=[r_w, r_B], writes=[r_w])
                else:
                    S.op("dve", lambda e: e.tensor_copy(out=Bcol[:, hd, :, 1:2], in_=B3[:, :, 63:64]), reads=[r_B], writes=[r_E[hd]])
                    S.op("dve", lambda e: e.tensor_tensor(out=g_B[:], in0=g_B[:], in1=g_lf[:], op=ALU.subtract), reads=[r_B, r_lf], writes=[r_B])
                    S.op("dve", lambda e: e.tensor_copy(out=Bcol[:, hd, :, 0:1], in_=B3[:, :, 32:33]), reads=[r_B], writes=[r_E[hd]])
                    S.op("dve", lambda e: e.tensor_tensor(out=B3, in0=B3, in1=Bcol[:, hd, :, 0:1].broadcast_to([128, 8, 64]), op=ALU.subtract),
                         reads=[r_B, r_E[hd]], writes=[r_B])
                    S.op("dve", lambda e: e.tensor_tensor(out=g_w[:], in0=g_w[:], in1=g_B[:], op=ALU.subtract), reads=[r_w, r_B], writes=[r_w])
                    S.op("dve", lambda e: e.tensor_tensor(out=ED[:, hd, :], in0=Bcol[:, hd, :, 1], in1=Bcol[:, hd, :, 0], op=ALU.subtract),
                         reads=[r_E[hd]], writes=[r_E[hd]])

            def A2(hd):
                g_e, g_w, g_lf, g_B = gtb[hd % NSET]
                r_e, r_w, r_lf, r_B = r_gtb[hd % NSET]
                l1m_c = lbv[:, 1, hd:hd + 1]
                B3 = g_B[:].rearrange("p (c t) -> p c t", t=64)
                S.op("act", lambda e: e.activation(out=g_e[:], in_=g_B[:], func=AF.Exp, scale=(1.0 if fwd else -1.0)), reads=[r_B], writes=[r_e])
                S.op("act", lambda e: e.activation(out=kT[:, hd, :], in_=g_w[:], func=AF.Exp, scale=-1.0, bias=l1m_c),
                     reads=[r_w, r_gv], writes=[r_kT[hd]])
                S.op("act", lambda e: e.activation(out=EB[:, hd, :], in_=Bcol[:, hd, :, 1], func=AF.Exp), reads=[r_E[hd]], writes=[r_E[hd]])
                if fwd:
                    S.op("act", lambda e: e.activation(out=EA[:, hd, :], in_=Bcol[:, hd, :, 0], func=AF.Exp), reads=[r_E[hd]], writes=[r_E[hd]])
                    S.op("act", lambda e: e.activation(out=EC[:, hd, :], in_=B3[:, :, 63], func=AF.Exp), reads=[r_B], writes=[r_E[hd]])
                else:
                    S.op("act", lambda e: e.activation(out=EC[:, hd, :], in_=Bcol[:, hd, :, 0], func=AF.Exp), reads=[r_E[hd]], writes=[r_E[hd]])
                    S.op("act", lambda e: e.activation(out=EA[:, hd, :], in_=ED[:, hd, :], func=AF.Exp), reads=[r_E[hd]], writes=[r_E[hd]])
                S.op("pool", lambda e: e.tensor_tensor(out=qs[:, hd, :], in0=qs[:, hd, :], in1=g_e[:], op=ALU.mult),
                     reads=[r_qs[hd], r_e], writes=[r_qs[hd]])
                S.op("pool", lambda e: e.tensor_tensor(
                    out=qh[:, hd, :].rearrange("p (c t) -> p c t", t=64), in0=qs[:, hd, :].rearrange("p (c t) -> p c t", t=64),
                    in1=EA[:, hd, :].unsqueeze(2).broadcast_to([128, 8, 64]), op=ALU.mult),
                    reads=[r_qs[hd], r_E[hd]], writes=[r_qh[hd]])
                kk = hd % 2
                S.op("dve", lambda e: e.tensor_tensor(
                    out=kh[kk][:].rearrange("p (c t) -> p c t", t=64), in0=kT[:, hd, :].rearrange("p (c t) -> p c t", t=64),
                    in1=EC[:, hd, :].unsqueeze(2).broadcast_to([128, 8, 64]), op=ALU.mult),
                    reads=[r_kT[hd], r_E[hd]], writes=[r_kh[kk]])

            def TT(hd):
                kk = hd % 2
                pb, r_pb = nb()
                pbb = pb[:].bitcast(BF16)
                for blk in range(4):
                    S.op("pe", lambda e, blk=blk: e.transpose(
                        pbb[:, blk * 128:(blk + 1) * 128], kh[kk][:, blk * 128:(blk + 1) * 128], ident[:]),
                        reads=[r_kh[kk], r_const], writes=[r_pb])
                tstate[hd] = (pbb, r_pb)

            def A3(hd):
                pbb, r_pb = tstate[hd]
                S.op("act", lambda e: e.activation(out=ktm[:, :, hd, :], in_=pbb[:, 0:512].rearrange("p (a b) -> p a b", a=4), func=AF.Copy),
                     reads=[r_pb], writes=r_ktm)

            def step(k):
                if k == -2:
                    Z(0); Z(1); Z(2)
                if ztile is None and 0 <= k + 3 < 8 and k + 3 >= 3:
                    Z(k + 3)
                if 0 <= k + 2 < 8:
                    A1(k + 2)
                if 0 <= k - 1 < 8:
                    TT(k - 1)
                if 0 <= k + 1 < 8:
                    D1(k + 1)
                if 0 <= k - 2 < 8:
                    A3(k - 2)
                if 0 <= k < 8:
                    A2(k)
                if ztile is not None and 0 <= k + 3 < 8 and k + 3 >= 3:
                    Z(k + 3)

            return step

        def load_bwd_operands(i):
            S.op("sp", lambda e: e.dma_start(out=qs, in_=qsc[:, :, T * i:T * (i + 1)]), reads=[r_qsc[i]], writes=r_qs, dma=True)
            S.op("sp", lambda e: e.dma_start(out=un[:, 12288:16384], in_=vsc[i]), reads=[r_vsc[i]], writes=r_vtm, dma=True)

        def scan_pass(direction):
            fwd = direction == 0
            order = list(range(NT)) if fwd else list(range(NT - 1, -1, -1))
            zg = G_HIN + 2 if fwd else G_HIN + 4
            load_x1(order[0], 0)
            for n, i in enumerate(order):
                b = n % 2
                if n + 1 < NT:
                    load_x1(order[n + 1], (n + 1) % 2)
                r_x = r_xt[b]
                xc = lambda c: xt[b][:, c, 15:15 + T]
                if not fwd:
                    if n == 0:
                        load_bwd_operands(i)
                        st0 = gate_pipeline(False, None, None, ztile=i)
                        for k in range(-2, 10):
                            st0(k)
                hk = lambda kc: h[:, kc, 0:T]
                if fwd:
                    rms_norm(xc, r_x, T, lambda c: gv[:, 1, c:c + 1], lambda c: MOD(1, 0, c),
                             lambda c: h[:, c, 0:T], lambda c: r_h[c])
                    for half in range(2):
                        slot, r_slot = wg(G_HIN + half)
                        for j in range(4):
                            hd = 4 * half + j
                            (pb, r_pb, _, _), = proj_fm(slot[:], r_slot, j, hk, r_h, T)
                            S.op("act", lambda e, pb=pb, hd=hd: e.activation(out=qs[:, hd, :], in_=pb[:], func=AF.Silu),
                                 reads=[r_pb], writes=[r_qs[hd]])
                    for half in range(2):
                        slot, r_slot = wg(G_HIN + 6 + half)
                        s3 = slot[:].rearrange("p (k n) -> p k n", k=8)
                        for blk in range(4):
                            pb, r_pb = nb()
                            for kc in range(8):
                                S.op("pe", lambda e, s3=s3, blk=blk, kc=kc, pb=pb: e.matmul(
                                    pb[:], lhsT=h[:, kc, blk * 128:(blk + 1) * 128], rhs=s3[:, kc, :],
                                    start=(kc == 0), stop=(kc == 7)), reads=[r_slot] + r_h, writes=[r_pb])
                            S.op("act", lambda e, pb=pb, blk=blk, half=half: e.activation(
                                out=vtm[:, blk, half * 512:(half + 1) * 512], in_=pb[:], func=AF.Copy),
                                reads=[r_pb], writes=[r_vtm[blk]])
                    S.op("pool", lambda e, i=i: e.dma_start(out=qsc[:, :, T * i:T * (i + 1)], in_=qs), reads=r_qs, writes=[r_qsc[i]], dma=True, nodep=True)
                    S.op("pool", lambda e, i=i: e.dma_start(out=vsc[i], in_=un[:, 12288:16384]), reads=r_vtm, writes=[r_vsc[i]], dma=True, nodep=True)
                    for half in range(2):
                        slot, r_slot = wg(G_HIN + 8 + half)
                        for j in range(4):
                            hd = 4 * half + j
                            (pb, r_pb, _, _), = proj_fm(slot[:], r_slot, j, hk, r_h, T)
                            S.op("act", lambda e, pb=pb, hd=hd: e.activation(out=kT[:, hd, :], in_=pb[:], func=AF.Silu),
                                 reads=[r_pb], writes=[r_kT[hd]])
                    S.op("pool", lambda e, i=i: e.dma_start(out=gsc[:, :, T * i:T * (i + 1)], in_=kT), reads=r_kT, writes=[r_gsc[i]], dma=True, nodep=True)
                    for half in range(2):
                        slot, r_slot = wg(G_HIN + 4 + half)
                        for j in range(4):
                            hd = 4 * half + j
                            (pb, r_pb, _, _), = proj_fm(slot[:], r_slot, j, hk, r_h, T)
                            stg, r_stg = gtb[hd % NSET][0], r_gtb[hd % NSET][0]
                            S.op("dve", lambda e, pb=pb, stg=stg: e.tensor_copy(out=stg[:], in_=pb[:]), reads=[r_pb], writes=[r_stg])
                            S.op("pool", lambda e, i=i, hd=hd, stg=stg: e.dma_start(out=zsc[:, hd, T * i:T * (i + 1)], in_=stg[:]),
                                 reads=[r_stg], writes=[r_stgdma[hd % NSET], r_zsc[i]], dma=True, nodep=True)
                    zslots = [wg(zg + half) for half in range(2)]
                    stf = gate_pipeline(True, zslots, hk)
                    for k in range(-2, 10):
                        stf(k)
                corder = list(range(8)) if fwd else list(range(7, -1, -1))

                def load_of_chunk(cn2, i=i, corder=corder):
                    c2 = corder[cn2]
                    kb = cn2 % 2
                    S.op("act", lambda e, i=i: e.dma_start(
                        out=tmpn[kb][:, 0:512].rearrange("p (a b) -> p a b", a=8),
                        in_=ofw[:, :, T * i + 64 * c2: T * i + 64 * (c2 + 1)]),
                        reads=[r_ofw[i]], writes=[r_tmpn[kb]], dma=True)

                if not fwd:
                    load_of_chunk(0)
                    load_of_chunk(1)
                for cn, c in enumerate(corder):
                    blk, par = c // 2, c % 2
                    kk = cn % 2
                    gcn = state["cn"]; state["cn"] += 1
                    sp_cur, sp_nxt = gcn % 2, (gcn + 1) % 2
                    mk = masks[:, (0 if fwd else 2) + par, :]
                    psc, r_psc = nb()
                    for hd in range(8):
                        S.op("pe", lambda e, psc=psc, hd=hd, blk=blk, c=c: e.matmul(
                            psc[:, hd * 64:(hd + 1) * 64], lhsT=kT[:, hd, blk * 128:(blk + 1) * 128],
                            rhs=qs[:, hd, c * 64:(c + 1) * 64], start=True, stop=True),
                            reads=[r_kT[hd], r_qs[hd]], writes=[r_psc])
                    S.op("dve", lambda e, psc=psc, kk=kk, mk=mk: e.tensor_tensor(
                        out=scT[kk][:], in0=psc[:].rearrange("p (a b) -> p a b", a=8),
                        in1=mk.unsqueeze(1).broadcast_to([128, 8, 64]), op=ALU.mult),
                        reads=[r_psc, r_cmat], writes=[r_scT[kk]])
                    pds = [nb(), nb()]
                    for hd in range(8):
                        pd, r_pd = pds[hd // 4]
                        p0 = 64 * par
                        S.op("pe", lambda e, pd=pd, hd=hd, blk=blk, p0=p0: e.matmul(
                            pd[:, (hd % 4) * 128:(hd % 4 + 1) * 128], lhsT=ktm[p0:p0 + 64, blk, hd, :],
                            rhs=vtm[p0:p0 + 64, blk, hd * 128:(hd + 1) * 128], start=True, stop=True),
                            reads=r_ktm + [r_vtm[blk]], writes=[r_pd])
                    po, r_po = nb()
                    for hd in range(8):
                        S.op("pe", lambda e, po=po, hd=hd, c=c, sp_cur=sp_cur: e.matmul(
                            po[:, hd * 64:(hd + 1) * 64], lhsT=Sp[sp_cur][:, hd, :], rhs=qh[:, hd, c * 64:(c + 1) * 64],
                            start=True, stop=False), reads=[r_Sp[sp_cur][hd // 4], r_qh[hd]], writes=[r_po])
                        S.op("pe", lambda e, po=po, hd=hd, blk=blk, kk=kk: e.matmul(
                            po[:, hd * 64:(hd + 1) * 64], lhsT=vtm[:, blk, hd * 128:(hd + 1) * 128], rhs=scT[kk][:, hd, :],
                            start=False, stop=True), reads=[r_vtm[blk], r_scT[kk]], writes=[r_po])
                    for hh in range(2):
                        pd, r_pd = pds[hh]
                        for j in range(4):
                            hd = 4 * hh + j
                            S.op("dve", lambda e, pd=pd, hd=hd, j=j, c=c: e.scalar_tensor_tensor(
                                out=Sst[:, hd, :], in0=Sst[:, hd, :], scalar=EB[:, hd, c:c + 1], in1=pd[:, j * 128:(j + 1) * 128],
                                op0=ALU.mult, op1=ALU.add), reads=[r_pd, r_Sg[hd], r_E[hd]], writes=[r_Sg[hd]])
                        S.op("act", lambda e, hh=hh, sp_nxt=sp_nxt: e.activation(
                            out=Sp[sp_nxt][:, 4 * hh:4 * hh + 4, :], in_=Sst[:, 4 * hh:4 * hh + 4, :], func=AF.Copy),
                            reads=r_Sg[4 * hh:4 * hh + 4], writes=[r_Sp[sp_nxt][hh]])
                    po3 = po[:].rearrange("p (a b) -> p a b", a=8)
                    if fwd:
                        S.op("act", lambda e, po3=po3, c=c: e.activation(out=o_sb[:, :, c * 64:(c + 1) * 64], in_=po3, func=AF.Copy),
                             reads=[r_po], writes=r_osb)
                    else:
                        S.op("dve", lambda e, po3=po3, c=c, kk=kk: e.tensor_tensor(
                            out=o_sb[:, :, c * 64:(c + 1) * 64], in0=po3, in1=tmpn[kk][:, 0:512].rearrange("p (a b) -> p a b", a=8), op=ALU.add),
                            reads=[r_po, r_tmpn[kk]], writes=r_osb)
                        if cn + 2 < 8:
                            load_of_chunk(cn + 2)
                if fwd:
                    S.op("pool", lambda e, i=i: e.dma_start(out=ofw[:, :, T * i:T * (i + 1)], in_=o_sb),
                         reads=r_osb, writes=[r_ofw[i]], dma=True, nodep=True)
                    continue
                sg = kT
                S.op("sp", lambda e, i=i: e.dma_start(out=kT, in_=gsc[:, :, T * i:T * (i + 1)]), reads=[r_gsc[i]], writes=r_kT, dma=True)
                for hd in range(8):
                    k = hd % 2
                    S.op("act", lambda e, hd=hd, k=k: e.activation(out=sqb[k][:, 0:T], in_=o_sb[:, hd, :], func=AF.Square),
                         reads=r_osb, writes=[r_sqb[k]])
                    pb, r_pb = nb()
                    S.op("pe", lambda e, pb=pb, k=k: e.matmul(pb[:], lhsT=ones[:], rhs=sqb[k][:, 0:T], start=True, stop=True),
                         reads=[r_sqb[k], r_const], writes=[r_pb])
                    S.op("act", lambda e, pb=pb, k=k: e.activation(out=tmpn[k][:, 0:T], in_=pb[:], func=AF.Ln, scale=1.0 / 128, bias=EPS),
                         reads=[r_pb], writes=[r_tmpn[k]])
                    S.op("act", lambda e, k=k: e.activation(out=tmpn[k][:, 0:T], in_=tmpn[k][:, 0:T], func=AF.Exp, scale=-0.5),
                         reads=[r_tmpn[k]], writes=[r_tmpn[k]])
                    S.op("dve", lambda e, hd=hd, k=k: e.tensor_tensor(out=tmpn[k][:, 0:T], in0=o_sb[:, hd, :], in1=tmpn[k][:, 0:T], op=ALU.mult),
                         reads=r_osb + [r_tmpn[k]], writes=[r_tmpn[k]])
                    S.op("dve", lambda e, hd=hd, k=k: e.scalar_tensor_tensor(
                        out=h[:, hd, 0:T], in0=tmpn[k][:, 0:T], scalar=vcol(V_GNG, hd), in1=sg[:, hd, :], op0=ALU.mult, op1=ALU.mult),
                        reads=[r_tmpn[k], r_vecs, r_kT[hd]] + r_h, writes=[r_h[hd]])
                wout_residual(1, b, r_x, G_HOUT)
                ghook = mhook = None
                if n + 1 < NT:
                    inext = order[n + 1]
                    load_bwd_operands(inext)
                    stn = gate_pipeline(False, None, None, ztile=inext)

                    def ghook(g, stn=stn):
                        if g == 0:
                            stn(-2); stn(-1); stn(0)
                        else:
                            stn(g)

                    def mhook(stn=stn):
                        stn(8); stn(9)
                mlp(1, b, r_x, G_W1_1, G_W2_1, mid_hook=mhook, group_hook=ghook)
                outb = arena[:, 0:8192].bitcast(F32).rearrange("p (c t) -> p c t", c=8)
                rms_norm(xc, r_x, T, lambda c: vcol(V_FG, c), None,
                         lambda c: outb[:, c, :], lambda c: r_ar[2 * c], out_final=True)
                S.op("pool", lambda e, i=i: e.dma_start(out=outT[:, :, T * i:T * (i + 1)], in_=outb),
                     reads=r_ar[0:16], writes=[r_out[i]], dma=True, nodep=True)

        scan_pass(0)
        S.barrier(lambda e: e.memset(ED[:], 0.0))
        S.op("pool", lambda e: e.dma_start(out=cc_in, in_=Sst[:].rearrange("p a b -> p (a b)")), reads=r_Sg, writes=[r_ccin], dma=True)
        S.op("pool", lambda e: e.collective_compute("AllGather", ALU.bypass, replica_groups=[[0, 1], [2, 3], [4, 5], [6, 7]],
                                                     ins=[cc_in], outs=[cc_out]), reads=[r_ccin], writes=[r_ccout])
        Sx = arena[:, 0:4096].bitcast(F32).rearrange("p (r f) -> p r f", r=2)
        r_Sx = r_ar[0:8]
        S.op("pool", lambda e: e.dma_start(out=Sx, in_=cc_out.rearrange("(r p) f -> p r f", p=128)),
             reads=[r_ccout], writes=r_Sx, dma=True)
        Sflat = Sst[:].rearrange("p a b -> p (a b)")
        S.op("dve", lambda e: e.tensor_scalar(out=Sflat, in0=Sx[:, 0, :], scalar1=vcol(V_FLAG, 0), scalar2=None, op0=ALU.mult),
             reads=r_Sx + [r_vecs], writes=r_Sg)
        S.op("dve", lambda e: e.scalar_tensor_tensor(out=Sflat, in0=Sx[:, 1, :], scalar=vcol(V_FLAG, 1), in1=Sflat,
                                                      op0=ALU.mult, op1=ALU.add), reads=r_Sx + [r_vecs] + r_Sg, writes=r_Sg)
        nxt = state["cn"] % 2
        S.op("act", lambda e: e.activation(out=Sp[nxt][:], in_=Sst[:], func=AF.Copy), reads=r_Sg, writes=r_Sp[nxt])
        scan_pass(1)
        S.emit(final_waits=[r_out[0]])
    return nc


_NC = None


def _fm(v):
    return np.ascontiguousarray(np.asarray(v, np.float32).reshape(8, 128).T)


def kernel(x, c, norm1_g, norm2_g, ada_w, ada_b, mlp_w1, mlp_w2, conv_w_in, conv_dw_w, conv_dw_b,
           conv_ln_g, conv_ln_b, conv_w_out, hgrn_w_in, hgrn_lb_logits, hgrn_gn_g, hgrn_w_out, final_g):
    global _NC
    f = lambda a: np.ascontiguousarray(np.asarray(a, np.float32))
    x = f(x); c = f(c)
    ada_w = f(ada_w); mlp_w1 = f(mlp_w1); mlp_w2 = f(mlp_w2)
    cin = f(conv_w_in)[0]; cout = f(conv_w_out)[0]; hout = f(hgrn_w_out)[0]
    hin = f(hgrn_w_in)[0]
    hin_sw = np.ascontiguousarray(np.concatenate(
        [hin[:, 0:1024], hin[:, 2048:3072], hin[:, 1024:2048], hin[:, 3072:]], axis=1))
    dw = f(conv_dw_w)[0]
    cm = np.zeros((128, NCM), np.float32)
    cm[:, CM_ID:CM_ID + 128] = np.eye(128, dtype=np.float32)
    s_ = np.arange(64)[:, None]; t_ = np.arange(64)[None, :]
    fe = np.zeros((128, 64), np.float32); fe[:64] = (s_ <= t_)
    fo = np.zeros((128, 64), np.float32); fo[64:] = (s_ <= t_)
    be = np.zeros((128, 64), np.float32); be[:64] = (s_ >= t_)
    bo = np.zeros((128, 64), np.float32); bo[64:] = (s_ >= t_)
    cm[:, CM_MASK:CM_MASK + 256] = np.concatenate([fe, fo, be, bo], axis=1)
    sm = np.ones(512, np.float32); sm[::64] = 0.0
    cm[:, CM_SCAN:CM_SCAN + 512] = sm[None, :]
    in_maps = []
    for r in range(8):
        b, half = r // 2, r % 2
        if half == 0:
            xs = x[b, 0:LTOK + 16]
        else:
            xs = x[b, ::-1][0:LTOK + 16]
        xTr = np.zeros((128, 8, XW), np.float32)
        xTr[:, :, 16:16 + LTOK + 16] = xs.T.reshape(8, 128, LTOK + 16).transpose(1, 0, 2)
        vv = np.zeros((128, NV), np.float32)
        for l in range(2):
            vv[:, V_N1G + 8 * l:V_N1G + 8 * l + 8] = _fm(norm1_g[l])
            vv[:, V_N2G + 8 * l:V_N2G + 8 * l + 8] = _fm(norm2_g[l])
            vv[:, V_LBL + 8 * l:V_LBL + 8 * l + 8] = _fm(hgrn_lb_logits[l])
            vv[:, V_ADAB + 48 * l:V_ADAB + 48 * l + 48] = np.asarray(ada_b[l], np.float32).reshape(48, 128).T
        vv[:, V_FG:V_FG + 8] = _fm(final_g)
        vv[:, V_DWB:V_DWB + 8] = _fm(conv_dw_b[0])
        vv[:, V_LNG:V_LNG + 8] = _fm(conv_ln_g[0])
        vv[:, V_LNB:V_LNB + 8] = _fm(conv_ln_b[0])
        vv[:, V_GNG:V_GNG + 8] = _fm(hgrn_gn_g[0])
        vv[:, V_FLAG:V_FLAG + 2] = np.array([0.0, 1.0] if half == 0 else [1.0, 0.0], np.float32)[None, :]
        vv[:, V_C:V_C + 8] = _fm(c[b])
        dwr = dw if half == 0 else dw[::-1]
        vv[:, V_DW:V_DW + 248] = dwr.T.reshape(8, 128, 31).transpose(1, 0, 2).reshape(128, 248)
        in_maps.append({
            "xT": xTr, "vecs": vv, "cmat": cm, "ada_w": ada_w, "mlp_w1": mlp_w1, "mlp_w2": mlp_w2,
            "conv_w_in": cin, "conv_w_out": cout, "hgrn_w_in": hin if half == 0 else hin_sw, "hgrn_w_out": hout,
        })
    if _NC is None:
        _NC = build_nc()
    res = run_bass_kernel_spmd(_NC, in_maps, core_ids=list(range(8)))
    out = np.empty((4, 8192, 1024), np.float32)
    for r in range(8):
        b, half = r // 2, r % 2
        o = res.results[r]["outT"]
        tok = o.transpose(2, 1, 0).reshape(LTOK, 1024)
        if half == 0:
            out[b, 0:LTOK] = tok
        else:
            out[b, LTOK:] = tok[::-1]
    return out
```
